# Optimizing a Trainium2 kernel written in Bass

```python
import jax, jax.numpy as jnp
from jax import lax
import numpy as np

D_MODEL = 1024
BATCH = 32
SEQ = 2048
DEPTH = 4
DEC_BATCH = 32
DEC_SEQ = 32
PAST_LEN = 2048

CHUNK = 64
D_MIX = D_MODEL
N_HEADS_ATTN = 8
HEAD_DIM = 64
D_ATTN = N_HEADS_ATTN * HEAD_DIM
D_CONV = D_MIX - D_ATTN
CONV_WIDTH = 3
D_FF = 2816
FFN_CONV_WIDTH = 3
Q_BLOCK = 128
ALPHA = (2 * DEPTH) ** 0.25
BETA = (8 * DEPTH) ** -0.25
LN_EPS = 1e-5
D_IN = 3 * D_ATTN + N_HEADS_ATTN + 3 * D_CONV
SPLITS = [D_ATTN, 2 * D_ATTN, 3 * D_ATTN, 3 * D_ATTN + N_HEADS_ATTN,
          3 * D_ATTN + N_HEADS_ATTN + D_CONV, 3 * D_ATTN + N_HEADS_ATTN + 2 * D_CONV]

kernel_name = "fox_shortconv_convffn_deepnorm_stream_step"


def _layer_norm(x, g, b):
    xf = x.astype(jnp.float32)
    mu = jnp.mean(xf, axis=-1, keepdims=True)
    var = jnp.mean(jnp.square(xf - mu), axis=-1, keepdims=True)
    y = (xf - mu) * lax.rsqrt(var + LN_EPS) * g.astype(jnp.float32) + b.astype(jnp.float32)
    return y.astype(x.dtype)


def _causal_dwconv(u_ext, w, b, out_len):
    y = b
    for i in range(w.shape[0]):
        y = y + w[i] * u_ext[:, i:i + out_len]
    return y


def _fox_attend(q, k, v, c_q, c_k, q_pos, k_pos):
    s = jnp.einsum('nqhd,nkhd->nhqk', q.astype(jnp.float32), k.astype(jnp.float32)) * (HEAD_DIM ** -0.5)
    s = s + jnp.transpose(c_q, (0, 2, 1))[:, :, :, None] - jnp.transpose(c_k, (0, 2, 1))[:, :, None, :]
    mask = k_pos[None, :] <= q_pos[:, None]
    s = jnp.where(mask[None, None], s, -jnp.inf)
    p = jax.nn.softmax(s, axis=-1)
    return jnp.einsum('nhqk,nkhd->nqhd', p.astype(v.dtype), v)


def _fox_prompt(q, k, v, logf):
    T = q.shape[1]
    c = jnp.cumsum(logf, axis=1)
    pos = jnp.arange(T)
    outs = []
    for blk in range(T // Q_BLOCK):
        lo, hi = blk * Q_BLOCK, (blk + 1) * Q_BLOCK
        outs.append(_fox_attend(q[:, lo:hi], k[:, :hi], v[:, :hi], c[:, lo:hi], c[:, :hi], pos[lo:hi], pos[:hi]))
    return jnp.concatenate(outs, axis=1)


def _fox_sample(q, k, v, logf, cache_k, cache_v, cache_logf):
    P, T = cache_k.shape[1], q.shape[1]
    k_all = jnp.concatenate([cache_k, k], axis=1)
    v_all = jnp.concatenate([cache_v, v], axis=1)
    c = jnp.cumsum(jnp.concatenate([cache_logf.astype(jnp.float32), logf], axis=1), axis=1)
    pos = jnp.arange(P + T)
    return _fox_attend(q, k_all, v_all, c[:, P:], c, pos[P:], pos)


def _layer(x, w_in, b_f, conv_w, conv_b, w_out, ln1_g, ln1_b,
           w_up, ffn_conv_w, ffn_conv_b, w_down, ln2_g, ln2_b, past=None):
    N, T, _ = x.shape
    proj = x @ w_in
    q, k, v, f_logit, gate_b, gate_c, h = jnp.split(proj, SPLITS, axis=-1)
    q = q.reshape(N, T, N_HEADS_ATTN, HEAD_DIM)
    k = k.reshape(N, T, N_HEADS_ATTN, HEAD_DIM)
    v = v.reshape(N, T, N_HEADS_ATTN, HEAD_DIM)
    logf = jax.nn.log_sigmoid((f_logit + b_f).astype(jnp.float32))
    u = gate_c * h
    if past is None:
        attn = _fox_prompt(q, k, v, logf)
        u_ext = jnp.pad(u, ((0, 0), (CONV_WIDTH - 1, 0), (0, 0)))
    else:
        cache_k, cache_v, cache_logf, mix_state, ffn_state = past
        attn = _fox_sample(q, k, v, logf, cache_k, cache_v, cache_logf)
        u_ext = jnp.concatenate([mix_state.astype(u.dtype), u], axis=1)
    conv_out = gate_b * _causal_dwconv(u_ext, conv_w, conv_b, T)
    mixed = jnp.concatenate([attn.reshape(N, T, D_ATTN), conv_out], axis=-1) @ w_out
    x = _layer_norm(ALPHA * x + mixed, ln1_g, ln1_b)

    up = x @ w_up
    g, val = jnp.split(up, [D_FF], axis=-1)
    if past is None:
        g_ext = jnp.pad(g, ((0, 0), (FFN_CONV_WIDTH - 1, 0), (0, 0)))
    else:
        g_ext = jnp.concatenate([ffn_state.astype(g.dtype), g], axis=1)
    a = jax.nn.silu(_causal_dwconv(g_ext, ffn_conv_w, ffn_conv_b, T)) * val
    x = _layer_norm(ALPHA * x + a @ w_down, ln2_g, ln2_b)
    new_state = (k, v, logf, u_ext[:, -(CONV_WIDTH - 1):], g_ext[:, -(FFN_CONV_WIDTH - 1):])
    return x, new_state


def setup_inputs(seed: int = 0) -> dict:
    key = jax.random.key(seed)
    ks = jax.random.split(key, 24)
    f32 = jnp.float32
    nrm = lambda k, shape, s: jax.random.normal(k, shape, f32) * s
    return {
        "x_prompt": nrm(ks[0], (BATCH, SEQ, D_MODEL), 1.0),
        "x_sample": nrm(ks[1], (DEC_BATCH, DEC_SEQ, D_MODEL), 1.0),
        "cache_k": nrm(ks[2], (DEPTH, DEC_BATCH, PAST_LEN, N_HEADS_ATTN, HEAD_DIM), 1.0),
        "cache_v": nrm(ks[3], (DEPTH, DEC_BATCH, PAST_LEN, N_HEADS_ATTN, HEAD_DIM), 1.0),
        "cache_logf": jax.nn.log_sigmoid(jax.random.uniform(ks[4], (DEPTH, DEC_BATCH, PAST_LEN, N_HEADS_ATTN), f32, 1.0, 4.0)
                                          + nrm(ks[5], (DEPTH, DEC_BATCH, PAST_LEN, N_HEADS_ATTN), 1.0)),
        "state_mix_conv": nrm(ks[6], (DEPTH, DEC_BATCH, CONV_WIDTH - 1, D_CONV), 1.0),
        "state_ffn_conv": nrm(ks[7], (DEPTH, DEC_BATCH, FFN_CONV_WIDTH - 1, D_FF), 1.0),
        "w_in": nrm(ks[8], (DEPTH, D_MODEL, D_IN), D_MODEL ** -0.5),
        "b_f": jax.random.uniform(ks[9], (DEPTH, N_HEADS_ATTN), f32, 1.0, 4.0),
        "conv_w": nrm(ks[10], (DEPTH, CONV_WIDTH, D_CONV), CONV_WIDTH ** -0.5),
        "conv_b": nrm(ks[11], (DEPTH, D_CONV), 0.01),
        "w_out": nrm(ks[12], (DEPTH, D_MIX, D_MODEL), BETA * D_MIX ** -0.5),
        "ln1_g": 1.0 + nrm(ks[13], (DEPTH, D_MODEL), 0.01),
        "ln1_b": nrm(ks[14], (DEPTH, D_MODEL), 0.01),
        "w_up": nrm(ks[15], (DEPTH, D_MODEL, 2 * D_FF), D_MODEL ** -0.5),
        "ffn_conv_w": nrm(ks[16], (DEPTH, FFN_CONV_WIDTH, D_FF), FFN_CONV_WIDTH ** -0.5),
        "ffn_conv_b": nrm(ks[17], (DEPTH, D_FF), 0.01),
        "w_down": nrm(ks[18], (DEPTH, D_FF, D_MODEL), BETA * D_FF ** -0.5),
        "ln2_g": 1.0 + nrm(ks[19], (DEPTH, D_MODEL), 0.01),
        "ln2_b": nrm(ks[20], (DEPTH, D_MODEL), 0.01),
    }


def reference(x_prompt, x_sample, cache_k, cache_v, cache_logf, state_mix_conv, state_ffn_conv,
              w_in, b_f, conv_w, conv_b, w_out, ln1_g, ln1_b,
              w_up, ffn_conv_w, ffn_conv_b, w_down, ln2_g, ln2_b):
    yp, ys = x_prompt, x_sample
    sp_list, ss_list = [], []
    for l in range(DEPTH):
        lw = (w_in[l], b_f[l], conv_w[l], conv_b[l], w_out[l], ln1_g[l], ln1_b[l],
              w_up[l], ffn_conv_w[l], ffn_conv_b[l], w_down[l], ln2_g[l], ln2_b[l])
        yp, sp = _layer(yp, *lw)
        ys, ss = _layer(ys, *lw, past=(cache_k[l], cache_v[l], cache_logf[l], state_mix_conv[l], state_ffn_conv[l]))
        sp_list.append(sp)
        ss_list.append(ss)
    k_prompt = jnp.stack([s[0] for s in sp_list])
    v_prompt = jnp.stack([s[1] for s in sp_list])
    logf_prompt = jnp.stack([s[2] for s in sp_list])
    mix_conv_prompt = jnp.stack([s[3] for s in sp_list])
    ffn_conv_prompt = jnp.stack([s[4] for s in sp_list])
    k_sample = jnp.stack([s[0] for s in ss_list])
    v_sample = jnp.stack([s[1] for s in ss_list])
    logf_sample = jnp.stack([s[2] for s in ss_list])
    mix_conv_sample = jnp.stack([s[3] for s in ss_list])
    ffn_conv_sample = jnp.stack([s[4] for s in ss_list])
    return (yp, ys, k_prompt, v_prompt, logf_prompt, mix_conv_prompt, ffn_conv_prompt,
            k_sample, v_sample, logf_sample, mix_conv_sample, ffn_conv_sample)
```

```python
import os
import types
import numpy as np
import ml_dtypes
from contextlib import ExitStack
import concourse.bass as bass
import concourse.mybir as mybir
from concourse.bass_utils import run_bass_kernel_spmd

F32 = mybir.dt.float32
BF16 = mybir.dt.bfloat16
U8 = mybir.dt.uint8
AF = mybir.ActivationFunctionType
ALU = mybir.AluOpType

L = 4
D = 1024
H = 8
T = 2048
DFF = 2816
NCH = 22
NPS = 4
ALPHA = float(8 ** 0.25)
EPS = 1e-5
NEG = -30000.0
NS = 8
FFN_GROUPS = [(0, 6), (6, 12), (12, 17), (17, 22)]


def _freeze(fn):
    if fn is None or fn.__closure__ is None:
        return fn
    cells = []
    for c in fn.__closure__:
        try:
            cells.append(types.CellType(c.cell_contents))
        except ValueError:
            cells.append(c)
    return types.FunctionType(fn.__code__, fn.__globals__, fn.__name__, fn.__defaults__, tuple(cells))


class Sched:
    def __init__(self):
        self.ops = []
        self.last_w = {}
        self.readers = {}
        self.phase = 0
        self.dma_rr = {'sp': 0, 'pool': 0}
        self.dma_last = {}
        self.eng_last = {}
        self.barrier_deps = None
        self.barrier_seen = set()

    def stage(self, name):
        if os.environ.get("MK_STOP", "") == name:
            self.stopped = True

    def add(self, eng, fn, reads=(), writes=(), dma=False, force_barrier=False):
        if getattr(self, 'stopped', False):
            return -1
        i = len(self.ops)
        ps_reads = [r for r in reads if isinstance(r, tuple) and r[0] == 'ps']
        if ps_reads:
            reads = [r for r in reads if not (isinstance(r, tuple) and r[0] == 'ps')]
            writes = list(writes) + ps_reads
        deps = set()
        for r in reads:
            w = self.last_w.get(r)
            if w is not None:
                deps.add(w)
        for w_ in writes:
            w = self.last_w.get(w_)
            if w is not None:
                deps.add(w)
            deps.update(self.readers.get(w_, {}).values())
        op = dict(eng=eng, fn=_freeze(fn), deps=deps, dma=dma, phase=self.phase, signal=dma, sig=None)
        if dma:
            k = self.dma_rr[eng]
            self.dma_rr[eng] += 1
            s = ('d', eng, k % NS)
            prev = self.dma_last.get(s)
            if prev is not None:
                deps.add(prev)
            self.dma_last[s] = i
            op['dsem'] = s
            rkey = s
        else:
            rkey = eng
        if self.barrier_deps is not None and force_barrier:
            deps.update(self.barrier_deps)
        elif self.barrier_deps is not None and eng != 'pool' and eng not in self.barrier_seen:
            deps.update(self.barrier_deps)
            self.barrier_seen.add(eng)
        deps.discard(i)
        for r in reads:
            self.readers.setdefault(r, {})[rkey] = i
        for w_ in writes:
            self.last_w[w_] = i
            self.readers[w_] = {}
        self.eng_last[eng] = i
        self.ops.append(op)
        return i

    def barrier(self):
        deps = set()
        for eng in ('pe', 'act', 'dve'):
            if eng in self.eng_last:
                deps.add(self.eng_last[eng])
        for s, i in self.dma_last.items():
            if s[1] == 'sp':
                deps.add(i)
        self.barrier_deps = deps
        self.barrier_seen = set()

    def finalize(self):
        self.ops.append(dict(eng='sp', fn=None, deps=set(self.dma_last.values()), dma=False,
                             phase=self.phase, signal=False, sig=None))
        for op in self.ops:
            for d in op['deps']:
                p = self.ops[d]
                if op['eng'] == 'pe' and p['eng'] == 'pe' and not p['dma']:
                    continue
                p['signal'] = True
        cnt = {}
        for op in self.ops:
            if op['dma']:
                s = op['dsem']
                cnt[s] = cnt.get(s, 0) + 1
                op['sig'] = (s, 16 * cnt[s])
            elif op['signal']:
                s = (op['eng'], op['phase'])
                cnt[s] = cnt.get(s, 0) + 1
                op['sig'] = (s, cnt[s])
        return sorted(set(op['sig'][0] for op in self.ops if op['sig'] is not None), key=str)

    def emit(self, e, eng, sems):
        waited = {}
        ops = self.ops
        for op in ops:
            if op['eng'] != eng:
                continue
            need = {}
            for d in op['deps']:
                p = ops[d]
                if eng == 'pe' and p['eng'] == 'pe' and not p['dma']:
                    continue
                if p['sig'] is None:
                    raise RuntimeError("dep on non-signaling op")
                s, v = p['sig']
                if need.get(s, 0) < v:
                    need[s] = v
            for s, v in need.items():
                if waited.get(s, 0) < v:
                    e.wait_ge(sems[s], v)
                    waited[s] = v
            if op['fn'] is not None:
                ins = op['fn'](e)
                if op['sig'] is not None:
                    ins.then_inc(sems[op['sig'][0]], 16 if op['dma'] else 1)


def build_program(nps=NPS, nlayers=L, do_sample=True):
    nc = bass.Bass("TRN2", target_bir_lowering=False)
    S = Sched()

    def din(name, shape, dt=F32):
        return nc.dram_tensor(name, list(shape), dt, kind="ExternalInput").ap()

    def dout(name, shape, dt=F32):
        return nc.dram_tensor(name, list(shape), dt, kind="ExternalOutput").ap()

    xp = din("xp", [NPS, T, D])
    xs = din("xs", [128, D])
    ck = din("ck", [L, 4, T, 512])
    cv = din("cv", [L, 4, T, 512])
    clf = din("clf", [L, 4, T, 8])
    smix = din("smix", [128, L * 4 * 4 * 2])
    sffn = din("sffn", [128, L * NCH * 4 * 2])
    wa = din("wa", [L, 32, 128, 8 * 256])
    wv = din("wv", [L, 4, 128, 8 * 128])
    wf = din("wf", [L, 128, 8 * 8])
    wo = din("wo", [L, 8, 128, D])
    wdn = din("wdn", [L, NCH, 128, D])
    lnrep = din("lnrep", [L, 2, 128, 2 * D])
    cwd = din("cwd", [128, L * 4 * 3])
    cbd = din("cbd", [128, L * 4])
    fwd = din("fwd", [128, L * NCH * 3])
    fbd = din("fbd", [128, L * NCH])
    bfd = din("bfd", [8, L])
    idfd = din("idfd", [128, 128])
    idbd = din("idbd", [128, 128], BF16)
    maskd = din("maskd", [128, 128], BF16)

    yp = dout("yp", [NPS, T, D])
    ys = dout("ys", [128, D])
    kpd = dout("kpd", [L, NPS, 512, T])
    vpd = dout("vpd", [L, NPS, T, 512])
    lfpd = dout("lfpd", [L, NPS, 8, T])
    mcd = dout("mcd", [L, 8, 4, 128, 2])
    fcd = dout("fcd", [L, 8, NCH, 128, 2])
    ksd = dout("ksd", [L, 512, 128])
    vsd = dout("vsd", [L, 4, 32, 512])
    lfsd = dout("lfsd", [L, 8, 128])
    pkd = nc.dram_tensor("pkd", [8, 3, T], BF16).ap()
    pqd = nc.dram_tensor("pqd", [8, 3, T], BF16).ap()
    pks = nc.dram_tensor("pks", [4, 8, 3, T + 32], BF16).ap()
    pqs = nc.dram_tensor("pqs", [4, 8, 3, 32], BF16).ap()

    def sb(name, shape, dt):
        return nc.alloc_sbuf_tensor(name, list(shape), dt)

    X = sb("X", [128, 16, D], F32)
    XT = sb("XT", [128, 8, T], BF16)
    IDF = sb("IDF", [128, 128], F32)
    IDB = sb("IDB", [128, 128], BF16)
    MASK = sb("MASK", [128, 128], BF16)
    ONES = sb("ONES", [8, 512], F32)
    CW = sb("CW", [128, L * 4 * 3], F32)
    CB = sb("CB", [128, L * 4], F32)
    FW = sb("FW", [128, L * NCH * 3], F32)
    FB = sb("FB", [128, L * NCH], F32)
    BFs = sb("BFs", [8, L], F32)
    NBF = sb("NBF", [8, L], F32)
    SMIX = sb("SMIX", [128, L * 4 * 4 * 2], F32)
    SFFN = sb("SFFN", [128, L * NCH * 4 * 2], F32)
    EPSC = sb("EPSC", [128, 1], F32)
    NWA = 3
    WA = [sb(f"WA{i}", [128, 8, 256], BF16) for i in range(NWA)]
    WV = [sb(f"WV{i}", [128, 8, 128], BF16) for i in range(2)]
    WF = sb("WF", [128, 8, 8], BF16)
    NWR = 6
    WR = [sb(f"WR{i}", [128, D], BF16) for i in range(NWR)]
    GBt = sb("GBt", [128, 2 * D], F32)
    KST = [sb(f"KST{i}", [128, 512], F32) for i in range(2)]
    VST = [sb(f"VST{i}", [128, 128], F32) for i in range(2)]
    SST = [sb(f"SST{i}", [128, 2], F32) for i in range(4)]
    LNS = sb("LNS", [128, 16, 16], F32)
    LNR = sb("LNR", [128, 16, 4], F32)
    ARENA_BYTES = 55 * 1024
    AR = sb("AR", [128, ARENA_BYTES], U8)

    class Arena:
        def __init__(self):
            self.off = 0

        def reset(self):
            self.off = 0

        def get(self, nfree, dt):
            nbytes = nfree * (2 if dt == BF16 else 4)
            nbytes = (nbytes + 31) // 32 * 32
            v = AR[:, self.off:self.off + nbytes].bitcast(dt)
            self.off += nbytes
            assert self.off <= ARENA_BYTES, self.off
            return v

    arena = Arena()
    PS = [nc.alloc_psum_tensor(f"ps{i}", [128, 512], F32) for i in range(8)]

    def P(i):
        return ("ps", i)

    wa_rr = [0]
    wr_rr = [0]
    wv_rr = [0]

    def load_wa(l, unit):
        slot = wa_rr[0] % NWA
        wa_rr[0] += 1
        S.add('pool', lambda e, s=slot, l=l, u=unit: e.dma_start(
            out=WA[s][:].rearrange("p k c -> p (k c)"), in_=wa[l, u]), writes=[("WA", slot)], dma=True)
        return slot

    def load_wv(l, pair):
        slot = wv_rr[0] % 2
        wv_rr[0] += 1
        S.add('pool', lambda e, s=slot, l=l, u=pair: e.dma_start(
            out=WV[s][:].rearrange("p k c -> p (k c)"), in_=wv[l, u]), writes=[("WV", slot)], dma=True)
        return slot

    def load_wr(src_ap):
        slot = wr_rr[0] % NWR
        wr_rr[0] += 1
        S.add('pool', lambda e, s=slot, a=src_ap: e.dma_start(out=WR[s][:], in_=a), writes=[("WR", slot)], dma=True)
        return slot

    def load_ln(l, which):
        S.add('sp', lambda e, l=l, w=which: e.dma_start(out=GBt[:], in_=lnrep[l, w]), writes=["GBt"], dma=True)

    for dst, src, nm in [(IDF, idfd, "IDF"), (IDB, idbd, "IDB"), (MASK, maskd, "MASK"), (CW, cwd, "CW"), (CB, cbd, "CB"),
                         (FW, fwd, "FW"), (FB, fbd, "FB"), (BFs, bfd, "BFs"), (SMIX, smix, "SMIX"), (SFFN, sffn, "SFFN")]:
        S.add('sp', lambda e, d=dst, s=src: e.dma_start(out=d[:], in_=s), writes=[nm], dma=True)
    S.add('dve', lambda e: e.memset(ONES[:], 1.0), writes=["ONES"])
    S.add('dve', lambda e: e.memset(EPSC[:], EPS), writes=["EPSC"])
    S.add('dve', lambda e: e.tensor_scalar(out=NBF[:], in0=BFs[:], scalar1=-1.0, scalar2=None, op0=ALU.mult),
          reads=["BFs"], writes=["NBF"])

    def run_sequence(kind, n):
        isP = kind == 'p'
        TT = T if isP else 128
        TW = 512 if isP else 128
        NT = TT // TW
        NB = TT // 128
        sidx = n if isP else None

        for b in range(NB):
            src = xp[n, b * 128:(b + 1) * 128, :] if isP else xs
            S.add('sp', lambda e, b=b, s=src: e.dma_start(out=X[:, b, :], in_=s), writes=[("X", b)], dma=True)

        def build_xt():
            bank = [0]
            for b in range(NB):
                for half in range(2):
                    pb = bank[0] % 8
                    bank[0] += 1
                    for i in range(4):
                        k = half * 4 + i
                        S.add('pe', lambda e, pb=pb, i=i, b=b, k=k: e.transpose(
                            PS[pb][:, i * 128:(i + 1) * 128], X[:, b, k * 128:(k + 1) * 128], IDF[:]),
                            reads=[("X", b), "IDF"], writes=[P(pb)])
                    S.add('act', lambda e, pb=pb, b=b, half=half: e.activation(
                        out=XT[:, half * 4:(half + 1) * 4, b * 128:(b + 1) * 128],
                        in_=PS[pb][:].rearrange("p (a c) -> p a c", a=4), func=AF.Copy),
                        reads=[P(pb)], writes=[("XT", b)])

        def xt_res(t):
            return [("XT", b) for b in range(t * TW // 128, (t + 1) * TW // 128)]

        def proj(ps_i, wslot, col0, ncols, t, wres, wt=None):
            W = WA[wslot] if wt is None else wt
            for k in range(8):
                S.add('pe', lambda e, k=k, W=W: e.matmul(
                    PS[ps_i][0:ncols, 0:TW], W[:, k, col0:col0 + ncols], XT[:, k, t * TW:(t + 1) * TW],
                    start=(k == 0), stop=(k == 7)),
                    reads=[wres] + xt_res(t), writes=[P(ps_i)])

        def x_update(b, first, banks):
            for hf in range(2):
                xs_ = X[:, b, hf * 512:(hf + 1) * 512]
                pb = banks[hf]
                if first:
                    S.add('dve', lambda e, xs_=xs_, pb=pb: e.scalar_tensor_tensor(
                        out=xs_, in0=xs_, scalar=ALPHA, in1=PS[pb][:, 0:512], op0=ALU.mult, op1=ALU.add),
                        reads=[P(pb)], writes=[("X", b)])
                else:
                    S.add('dve', lambda e, xs_=xs_, pb=pb: e.tensor_tensor(
                        out=xs_, in0=xs_, in1=PS[pb][:, 0:512], op=ALU.add),
                        reads=[P(pb)], writes=[("X", b)])

        def layer_norm_all():
            for b in range(NB):
                for hf in range(2):
                    S.add('dve', lambda e, hf=hf, b=b: e.bn_stats(out=LNS[:, b, hf * 6:(hf + 1) * 6], in_=X[:, b, hf * 512:(hf + 1) * 512]),
                          reads=[("X", b)], writes=[("LNS", b)])
                S.add('dve', lambda e, b=b: e.bn_aggr(out=LNS[:, b, 12:14], in_=LNS[:, b, 0:12]), reads=[("LNS", b)], writes=[("LNS", b)])
            for b in range(NB):
                S.add('act', lambda e, b=b: e.activation(out=LNR[:, b, 0:1], in_=LNS[:, b, 13:14], func=AF.Sqrt, bias=EPSC[:, 0:1], scale=1.0),
                      reads=[("LNS", b), "EPSC"], writes=[("LNR", b)])
            for b in range(NB):
                S.add('dve', lambda e, b=b: e.reciprocal(out=LNR[:, b, 1:2], in_=LNR[:, b, 0:1]), reads=[("LNR", b)], writes=[("LNR", b)])
                S.add('dve', lambda e, b=b: e.scalar_tensor_tensor(out=LNR[:, b, 2:3], in0=LNS[:, b, 12:13], scalar=-1.0, in1=LNR[:, b, 1:2],
                                                                  op0=ALU.mult, op1=ALU.mult), reads=[("LNR", b), ("LNS", b)], writes=[("LNR", b)])
            for b in range(NB):
                S.add('act', lambda e, b=b: e.activation(out=X[:, b, :], in_=X[:, b, :], func=AF.Identity, bias=LNR[:, b, 2:3], scale=LNR[:, b, 1:2]),
                      reads=[("LNR", b)], writes=[("X", b)])
            for b in range(NB):
                S.add('dve', lambda e, b=b: e.tensor_tensor(out=X[:, b, :], in0=X[:, b, :], in1=GBt[:, 0:D], op=ALU.mult),
                      reads=["GBt"], writes=[("X", b)])
                S.add('dve', lambda e, b=b: e.tensor_tensor(out=X[:, b, :], in0=X[:, b, :], in1=GBt[:, D:2 * D], op=ALU.add),
                      reads=["GBt"], writes=[("X", b)])

        def seg(ap, w):
            return ap

        for l in range(nlayers):
            build_xt()
            S.stage("xt")
            S.barrier()
            arena.reset()
            KT = arena.get(2 * (T if isP else T + 32), BF16).rearrange("p (h t) -> p h t", h=2)
            NVB = NB if isP else 17
            VA = arena.get(NVB * 2 * 128, BF16).rearrange("p (b h c) -> p b h c", b=NVB, h=2)
            QT = [arena.get(2 * TW, BF16).rearrange("p (h t) -> p h t", h=2) for _ in range(2)]
            PT = [arena.get(1024, BF16) for _ in range(2)]
            AT = [arena.get(TW, BF16) for _ in range(2)]
            ct_off = arena.off
            CT = arena.get(4 * TT, BF16).rearrange("p (c t) -> p c t", c=4)
            UT = [arena.get(TW + 8, F32) for _ in range(2)]
            HS = arena.get(512, F32)
            T1 = arena.get(512, F32)
            RC = arena.get(512, F32)
            OS = arena.get(512, F32)
            end_off = arena.off
            if isP:
                arena.off = ct_off
            FA = arena.get(512, F32)
            FBb = [arena.get(512, F32) for _ in range(2)]
            FC = arena.get(512, F32)
            PKt = arena.get(3 * 512, BF16).rearrange("p (j t) -> p j t", j=3)
            PQt = arena.get(3 * 512, BF16).rearrange("p (j t) -> p j t", j=3)
            if isP:
                arena.off = end_off
            KC = CL = LPS = None
            if not isP:
                KC = arena.get(16 * 128, BF16).rearrange("p (b c) -> p b c", b=16)
                CL = arena.get(16 * 8, F32).rearrange("p (b c) -> p b c", b=16)
                LPS = arena.get(128, F32)

            S.add('dve', lambda e, KT=KT: e.memset(KT[64:70, :, :], 1.0),
                  writes=["KT", ("KTaug", 0), ("KTaug", 1), ("KTs", 0), ("KTs", 1)])
            for qi, q in enumerate(QT):
                S.add('dve', lambda e, q=q: e.memset(q[64:70, :, :], 1.0), writes=[("QT", qi), "QTs"])
            S.add('dve', lambda e, VA=VA: e.memset(VA[:, :, :, 64:128], 1.0), writes=["VA"])

            S.add('pool', lambda e, l=l: e.dma_start(out=WF[:].rearrange("p k c -> p (k c)"), in_=wf[l]), writes=["WF"], dma=True)

            def decay_chain(src_fa_ready_res, width, carry, slot, kdst, qdst):
                w = width
                cur = FBb[slot]
                ini = 0.0 if carry is None else carry
                S.add('dve', lambda e, cur=cur, ini=ini, w=w: e.tensor_tensor_scan(
                    out=cur[0:8, 0:w], data0=ONES[0:8, 0:w], data1=FA[0:8, 0:w], initial=ini, op0=ALU.mult, op1=ALU.add),
                    reads=["FA", "ONES", ("FB", 1 - slot)], writes=[("FB", slot)])
                S.add('dve', lambda e, cur=cur, w=w: e.tensor_scalar(out=FC[0:8, 0:w], in0=cur[0:8, 0:w], scalar1=8.0, scalar2=None, op0=ALU.mult),
                      reads=[("FB", slot)], writes=["FC"])
                for j in range(3):
                    S.add('dve', lambda e, j=j, w=w: e.tensor_copy(out=PKt[0:8, j, 0:w], in_=FC[0:8, 0:w]), reads=["FC"], writes=["PKt"])
                    if j < 2:
                        S.add('dve', lambda e, j=j, w=w: e.tensor_tensor(out=FC[0:8, 0:w], in0=FC[0:8, 0:w], in1=PKt[0:8, j, 0:w], op=ALU.subtract),
                              reads=["PKt"], writes=["FC"])
                S.add('sp', lambda e, w=w, kdst=kdst: e.dma_start(out=kdst, in_=PKt[0:8, :, 0:w]), reads=["PKt"], writes=["pkd"], dma=True)
                if qdst is not None:
                    S.add('dve', lambda e, w=w: e.tensor_scalar(out=PQt[0:8, :, 0:w], in0=PKt[0:8, :, 0:w], scalar1=-1.0, scalar2=None, op0=ALU.mult),
                          reads=["PKt"], writes=["PQt"])
                    S.add('sp', lambda e, w=w, qdst=qdst: e.dma_start(out=qdst, in_=PQt[0:8, :, 0:w]), reads=["PQt"], writes=["pqd"], dma=True)
                return cur[0:8, w - 1:w]

            def lp_from_psum(pb, w, outdst):
                S.add('act', lambda e, pb=pb, w=w: e.activation(out=FA[0:8, 0:w], in_=PS[pb][0:8, 0:w], func=AF.Exp,
                                                              bias=NBF[:, l:l + 1], scale=-1.0), reads=[P(pb), "NBF"], writes=["FA"])
                S.add('act', lambda e, w=w: e.activation(out=FA[0:8, 0:w], in_=FA[0:8, 0:w], func=AF.Ln, bias=1.0, scale=1.0),
                      reads=["FA"], writes=["FA"])
                S.add('dve', lambda e, w=w: e.tensor_scalar(out=FC[0:8, 0:w], in0=FA[0:8, 0:w], scalar1=-1.0, scalar2=None, op0=ALU.mult),
                      reads=["FA"], writes=["FC"])
                S.add('sp', lambda e, w=w, o=outdst: e.dma_start(out=o, in_=FC[0:8, 0:w]), reads=["FC"], writes=[], dma=True)

            if isP:
                carry = None
                for t in range(NT):
                    pb = t % 4
                    proj(pb, None, 0, 8, t, "WF", wt=WF)
                    lp_from_psum(pb, TW, lfpd[l, n, :, t * TW:(t + 1) * TW])
                    carry = decay_chain(None, TW, carry, t % 2, pkd[:, :, t * TW:(t + 1) * TW], pqd[:, :, t * TW:(t + 1) * TW])
            else:
                proj(0, None, 0, 8, 0, "WF", wt=WF)
                lp_from_psum(0, 128, lfsd[l])
                S.add('dve', lambda e: e.tensor_copy(out=LPS[0:8, 0:128], in_=FA[0:8, 0:128]), reads=["FA"], writes=["LPS"])
                for sq in range(4):
                    S.add('sp', lambda e, sq=sq: e.dma_start(out=CL[:], in_=clf[l, sq].rearrange("(b p) h -> p b h", p=128)),
                          writes=["CL"], dma=True)
                    carry = None
                    for t4 in range(4):
                        pb = t4 % 4
                        for i in range(4):
                            b = t4 * 4 + i
                            S.add('pe', lambda e, pb=pb, i=i, b=b: e.transpose(PS[pb][0:8, i * 128:(i + 1) * 128], CL[:, b, :], IDF[:]),
                                  reads=["CL", "IDF"], writes=[P(pb)])
                        S.add('act', lambda e, pb=pb: e.activation(out=FA[0:8, 0:512], in_=PS[pb][0:8, 0:512], func=AF.Copy, scale=-1.0),
                              reads=[P(pb)], writes=["FA"])
                        carry = decay_chain(None, 512, carry, t4 % 2, pks[sq, :, :, t4 * 512:(t4 + 1) * 512], None)
                    S.add('dve', lambda e, sq=sq: e.tensor_copy(out=FA[0:8, 0:32], in_=LPS[0:8, sq * 32:(sq + 1) * 32]), reads=["LPS"], writes=["FA"])
                    decay_chain(None, 32, carry, 0, pks[sq, :, :, T:T + 32], pqs[sq])

            S.stage("F")
            if isP:
                S.barrier()
            cslot = {}

            def conv_views(buf):
                if isP:
                    return buf[:, 2:2 + TW], buf[:, 1:1 + TW], buf[:, 0:TW], None
                v = buf[:, 0:4 * 34].rearrange("p (s c) -> p s c", s=4)
                return v[:, :, 2:34], v[:, :, 1:33], v[:, :, 0:32], v[:, :, 0:2]

            def psv(pb):
                if isP:
                    return PS[pb][:, 0:TW]
                return PS[pb][:, 0:128].rearrange("p (s c) -> p s c", s=4)

            def sbv(ap):
                if isP:
                    return ap
                return ap.rearrange("p (s c) -> p s c", s=4)

            h_slot = None
            for c in range(4):
                s_bc = load_wa(l, 4 + c)
                if c % 2 == 0:
                    h_slot = load_wa(l, 8 + c // 2)
                for t in range(NT):
                    base = (3 * (c * NT + t)) % 6
                    pgb, pgc, ph = base, base + 1, base + 2
                    proj(pgb, s_bc, 0, 128, t, ("WA", s_bc))
                    proj(pgc, s_bc, 128, 128, t, ("WA", s_bc))
                    proj(ph, h_slot, (c % 2) * 128, 128, t, ("WA", h_slot))
                    ub = UT[t % 2]
                    cur, m1, m2, halo = conv_views(ub)
                    S.add('act', lambda e, ph=ph: e.activation(out=sbv(HS[:, 0:TW]), in_=psv(ph), func=AF.Copy),
                          reads=[P(ph)], writes=["HS"])
                    S.add('dve', lambda e, pgc=pgc, cur=cur: e.tensor_tensor(out=cur, in0=psv(pgc), in1=sbv(HS[:, 0:TW]), op=ALU.mult),
                          reads=[P(pgc), "HS"], writes=[("UT", t % 2)])
                    if isP:
                        if t == 0:
                            S.add('dve', lambda e, ub=ub: e.memset(ub[:, 0:2], 0.0), writes=[("UT", t % 2)])
                        else:
                            pu = UT[(t - 1) % 2]
                            S.add('dve', lambda e, ub=ub, pu=pu: e.tensor_copy(out=ub[:, 0:2], in_=pu[:, TW:TW + 2]),
                                  reads=[("UT", (t - 1) % 2)], writes=[("UT", t % 2)])
                    else:
                        o0 = ((l * 4 + c) * 4) * 2
                        S.add('dve', lambda e, halo=halo, o0=o0: e.tensor_copy(
                            out=halo, in_=SMIX[:, o0:o0 + 8].rearrange("p (s c) -> p s c", s=4)),
                            reads=["SMIX"], writes=[("UT", t % 2)])
                    wi = (l * 4 + c) * 3
                    S.add('act', lambda e, cur=cur, wi=wi, c=c: e.activation(
                        out=sbv(T1[:, 0:TW]), in_=cur, func=AF.Identity, bias=CB[:, l * 4 + c:l * 4 + c + 1], scale=CW[:, wi + 2:wi + 3]),
                        reads=[("UT", t % 2), "CW", "CB"], writes=["T1"])
                    S.add('dve', lambda e, m1=m1, wi=wi: e.scalar_tensor_tensor(
                        out=sbv(T1[:, 0:TW]), in0=m1, scalar=CW[:, wi + 1:wi + 2], in1=sbv(T1[:, 0:TW]), op0=ALU.mult, op1=ALU.add),
                        reads=[("UT", t % 2), "CW"], writes=["T1"])
                    S.add('dve', lambda e, m2=m2, wi=wi: e.scalar_tensor_tensor(
                        out=sbv(T1[:, 0:TW]), in0=m2, scalar=CW[:, wi:wi + 1], in1=sbv(T1[:, 0:TW]), op0=ALU.mult, op1=ALU.add),
                        reads=[("UT", t % 2), "CW"], writes=["T1"])
                    S.add('dve', lambda e, pgb=pgb, c=c, t=t: e.tensor_tensor(
                        out=sbv(CT[:, c, t * TW:(t + 1) * TW]), in0=psv(pgb), in1=sbv(T1[:, 0:TW]), op=ALU.mult),
                        reads=[P(pgb), "T1"], writes=[("CT", c, t)])
                    if isP and t == NT - 1:
                        S.add('sp', lambda e, ub=ub, c=c: e.dma_start(out=mcd[l, n, c], in_=ub[:, TW:TW + 2]),
                              reads=[("UT", t % 2)], writes=[], dma=True)
                    if not isP:
                        v = ub[:, 0:4 * 34].rearrange("p (s c) -> p s c", s=4)
                        for sq in range(4):
                            S.add('sp', lambda e, v=v, c=c, sq=sq: e.dma_start(out=mcd[l, 4 + sq, c], in_=v[:, sq, 32:34]),
                                  reads=[("UT", t % 2)], writes=[], dma=True)

            S.stage("C")
            conv_rows = None
            for pair in range(4):
                s_qk = load_wa(l, pair)
                s_v = load_wv(l, pair)
                wo_a = load_wr(wo[l, pair])
                if pair == 0:
                    conv_rows = [load_wr(wo[l, 4 + c]) for c in range(4)]
                S.stage("w1")
                for t in range(NT):
                    pb = (t % 2) * 1 + 6
                    proj(pb, s_qk, 128, 128, t, ("WA", s_qk))
                    S.stage("ka")
                    S.stage(f"ka{t}")
                    ks = KST[t % 2]
                    S.add('act', lambda e, pb=pb, ks=ks: e.activation(out=ks[:, 0:TW], in_=PS[pb][:, 0:TW], func=AF.Copy),
                          reads=[P(pb)], writes=[("KST", t % 2)])
                    S.stage("kb")
                    S.stage(f"kb{t}")
                    kdst = (kpd[l, n, pair * 128:(pair + 1) * 128, t * TW:(t + 1) * TW] if isP
                            else ksd[l, pair * 128:(pair + 1) * 128, :])
                    S.add('sp', lambda e, ks=ks, kdst=kdst: e.dma_start(out=kdst, in_=ks[:, 0:TW]),
                          reads=[("KST", t % 2)], writes=[], dma=True)
                    S.stage("kc")
                    S.stage(f"kc{t}")
                    if isP:
                        for hl in range(2):
                            if hl == 1:
                                S.stage("kd")
                                S.stage(f"kd{t}")
                            mkv = os.environ.get("MK_V", "")
                            if hl == 0:
                                tt_ = (3 - t) if mkv == "addr" else t
                                if True:
                                    S.add('dve', lambda e, ks=ks, tt_=tt_: e.tensor_copy(out=KT[0:64, 0, tt_ * TW:(tt_ + 1) * TW], in_=ks[0:64, 0:TW]),
                                          reads=[("KST", t % 2)], writes=[("KT", hl, t)])
                                else:
                                    S.add('dve', lambda e, pb=pb, tt_=tt_: e.tensor_copy(out=KT[0:64, 0, tt_ * TW:(tt_ + 1) * TW], in_=PS[pb][0:64, 0:TW]),
                                          reads=[P(pb)], writes=[("KT", hl, t)])
                            else:
                                S.add('act', lambda e, pb=pb, t=t: e.activation(out=KT[0:64, 1, t * TW:(t + 1) * TW], in_=PS[pb][64:128, 0:TW], func=AF.Copy),
                                      reads=[P(pb)], writes=[("KT", hl, t)])
                    S.stage("ke")
                    S.stage(f"ke{t}")
                S.stage("k1")
                if isP:
                    for b in range(NB):
                        pb = b % 2 + 4
                        for k in range(8):
                            S.add('pe', lambda e, pb=pb, k=k, b=b: e.matmul(
                                PS[pb][:, 0:128], XT[:, k, b * 128:(b + 1) * 128], WV[s_v][:, k, :], start=(k == 0), stop=(k == 7)),
                                reads=[("XT", b), ("WV", s_v)], writes=[P(pb)])
                        vs_ = VST[b % 2]
                        S.add('act', lambda e, pb=pb, vs_=vs_: e.activation(out=vs_[:], in_=PS[pb][:, 0:128], func=AF.Copy),
                              reads=[P(pb)], writes=[("VST", b % 2)])
                        S.add('sp', lambda e, vs_=vs_, b=b: e.dma_start(out=vpd[l, n, b * 128:(b + 1) * 128, pair * 128:(pair + 1) * 128], in_=vs_[:]),
                              reads=[("VST", b % 2)], writes=[], dma=True)
                        S.add('dve', lambda e, pb=pb, b=b: e.tensor_copy(
                            out=VA[:, b, :, 0:64], in_=PS[pb][:, 0:128].rearrange("p (h c) -> p h c", h=2)),
                            reads=[P(pb)], writes=[("VA", b)])
                    S.stage("v1")
                    for hl in range(2):
                        hh = pair * 2 + hl
                        S.add('sp', lambda e, hl=hl, hh=hh: e.dma_start(out=KT[67:70, hl, 0:TT], in_=pkd[hh, :, 0:TT]),
                              reads=["pkd"], writes=[("KTaug", hl)], dma=True)

                S.stage("kv")
                if isP:
                    for t in range(NT):
                        qt = QT[t % 2]
                        qres = ("QT", t % 2)
                        pb = 6 + (t % 2)
                        proj(pb, s_qk, 0, 128, t, ("WA", s_qk))
                        S.add('act', lambda e, pb=pb, qt=qt: e.activation(out=qt[0:64, 0, :], in_=PS[pb][0:64, 0:TW], func=AF.Copy),
                              reads=[P(pb)], writes=[qres])
                        S.add('act', lambda e, pb=pb, qt=qt: e.activation(out=qt[0:64, 1, :], in_=PS[pb][64:128, 0:TW], func=AF.Copy),
                              reads=[P(pb)], writes=[qres])
                        for hl in range(2):
                            hh = pair * 2 + hl
                            S.add('sp', lambda e, hl=hl, hh=hh, qt=qt, t=t: e.dma_start(out=qt[64:67, hl, :], in_=pqd[hh, :, t * TW:(t + 1) * TW]),
                                  reads=["pqd"], writes=[qres], dma=True)
                        at = AT[t % 2]
                        ares = ("AT", t % 2)
                        for hl in range(2):
                            ob = 4 + hl
                            nkb = 4 * t + 4
                            kres = [("KT", hl, tt) for tt in range(t + 1)] + [("KTaug", hl), "KT"]
                            for g in range(nkb // 2):
                                sb0 = (g % 2) * 2
                                ptb = PT[g % 2]
                                pres = ("PT", g % 2)
                                for i in range(2):
                                    kb = g * 2 + i
                                    j = kb - 4 * t
                                    c0 = 0 if j < 0 else j * 128
                                    S.add('pe', lambda e, sbk=sb0 + i, kb=kb, hl=hl, qt=qt, c0=c0, j=j: e.matmul(
                                        PS[sbk][:, c0:TW], KT[0:70, hl, kb * 128:(kb + 1) * 128], qt[0:70, hl, c0:TW],
                                        start=True, stop=(j < 0)),
                                        reads=kres + [qres], writes=[P(sb0 + i)])
                                    if j >= 0:
                                        S.add('pe', lambda e, sbk=sb0 + i, c0=c0: e.matmul(
                                            PS[sbk][:, c0:c0 + 128], IDB[:], MASK[:], start=False, stop=True),
                                            reads=["IDB", "MASK"], writes=[P(sb0 + i)])
                                    S.add('act', lambda e, sbk=sb0 + i, ptb=ptb, i=i, c0=c0: e.activation(
                                        out=ptb[:, i * 512 + c0:i * 512 + TW], in_=PS[sbk][:, c0:TW], func=AF.Exp, scale=0.125),
                                        reads=[P(sb0 + i)], writes=[pres])
                                for i in range(2):
                                    kb = g * 2 + i
                                    j = kb - 4 * t
                                    c0 = 0 if j < 0 else j * 128
                                    S.add('pe', lambda e, ob=ob, kb=kb, hl=hl, ptb=ptb, i=i, c0=c0, nkb=nkb: e.matmul(
                                        PS[ob][:, c0:TW], VA[:, kb, hl, :], ptb[:, i * 512 + c0:i * 512 + TW],
                                        start=(kb == 0), stop=(kb == nkb - 1)),
                                        reads=[pres, ("VA", kb), "VA"], writes=[P(ob)])
                            S.add('dve', lambda e, ob=ob: e.reciprocal(out=RC[64:128, 0:TW], in_=PS[ob][64:128, 0:TW]),
                                  reads=[P(ob)], writes=["RC"])
                            S.add('dve', lambda e, ob=ob, hl=hl, at=at: e.tensor_tensor(
                                out=at[hl * 64:(hl + 1) * 64, 0:TW], in0=PS[ob][0:64, 0:TW], in1=RC[64:128, 0:TW], op=ALU.mult),
                                reads=[P(ob), "RC"], writes=[ares])
                        S.stage("att")
                        for bb in range(TW // 128):
                            b = t * (TW // 128) + bb
                            terms = [(at[:, bb * 128:(bb + 1) * 128], WR[wo_a], [ares, ("WR", wo_a)])]
                            if pair == 0:
                                for c in range(4):
                                    terms.append((CT[:, c, b * 128:(b + 1) * 128], WR[conv_rows[c]], [("CT", c, t), ("WR", conv_rows[c])]))
                            for hf in range(2):
                                pb2 = 6 + hf
                                for ti, (lt, rw, rs) in enumerate(terms):
                                    S.add('pe', lambda e, pb2=pb2, lt=lt, rw=rw, hf=hf, ti=ti, nt_=len(terms): e.matmul(
                                        PS[pb2][:, 0:512], lt, rw[:, hf * 512:(hf + 1) * 512], start=(ti == 0), stop=(ti == nt_ - 1)),
                                        reads=rs, writes=[P(pb2)])
                            x_update(b, pair == 0, (6, 7))
                else:
                    sample_attention(l, pair, s_qk, s_v, wo_a, conv_rows, KT, VA, QT, PT, AT, CT, RC, OS, FA, FBb, FC, PKt, PQt, KC, CL,
                                     decay_chain, proj, x_update)

            S.stage("pairs")
            load_ln(l, 0)
            layer_norm_all()
            S.stage("ln1")

            build_xt()
            S.barrier()
            arena.reset()
            A2 = arena.get(6 * TT, BF16).rearrange("p (c t) -> p c t", c=6)
            GT = [arena.get(TW + 8, F32) for _ in range(2)]
            T2 = [arena.get(512, F32) for _ in range(2)]
            T3 = [arena.get(512, F32) for _ in range(2)]
            for gi, (j0, j1) in enumerate(FFN_GROUPS):
                for j in range(j0, j1):
                    jl = j - j0
                    s_up = load_wa(l, 10 + j)
                    for t in range(NT):
                        base = 2 * ((j * NT + t) % 3)
                        pg, pv = base, base + 1
                        proj(pg, s_up, 0, 128, t, ("WA", s_up))
                        proj(pv, s_up, 128, 128, t, ("WA", s_up))
                        gb_ = GT[t % 2]
                        cur, m1, m2, halo = conv_views(gb_)
                        t2 = T2[t % 2]
                        t3 = T3[t % 2]
                        S.add('act', lambda e, pg=pg, cur=cur: e.activation(out=cur, in_=psv(pg), func=AF.Copy),
                              reads=[P(pg)], writes=[("GT", t % 2)])
                        if isP:
                            if t == 0:
                                S.add('dve', lambda e, gb_=gb_: e.memset(gb_[:, 0:2], 0.0), writes=[("GT", t % 2)])
                            else:
                                pu = GT[(t - 1) % 2]
                                S.add('dve', lambda e, gb_=gb_, pu=pu: e.tensor_copy(out=gb_[:, 0:2], in_=pu[:, TW:TW + 2]),
                                      reads=[("GT", (t - 1) % 2)], writes=[("GT", t % 2)])
                        else:
                            o0 = ((l * NCH + j) * 4) * 2
                            S.add('dve', lambda e, halo=halo, o0=o0: e.tensor_copy(
                                out=halo, in_=SFFN[:, o0:o0 + 8].rearrange("p (s c) -> p s c", s=4)),
                                reads=["SFFN"], writes=[("GT", t % 2)])
                        wi = (l * NCH + j) * 3
                        bi = l * NCH + j
                        S.add('act', lambda e, pg=pg, t2=t2, wi=wi, bi=bi: e.activation(
                            out=sbv(t2[:, 0:TW]), in_=psv(pg), func=AF.Identity, bias=FB[:, bi:bi + 1], scale=FW[:, wi + 2:wi + 3]),
                            reads=[P(pg), "FW", "FB"], writes=[("T2", t % 2)])
                        S.add('dve', lambda e, m1=m1, t2=t2, wi=wi: e.scalar_tensor_tensor(
                            out=sbv(t2[:, 0:TW]), in0=m1, scalar=FW[:, wi + 1:wi + 2], in1=sbv(t2[:, 0:TW]), op0=ALU.mult, op1=ALU.add),
                            reads=[("GT", t % 2), "FW"], writes=[("T2", t % 2)])
                        S.add('dve', lambda e, m2=m2, t2=t2, wi=wi: e.scalar_tensor_tensor(
                            out=sbv(t2[:, 0:TW]), in0=m2, scalar=FW[:, wi:wi + 1], in1=sbv(t2[:, 0:TW]), op0=ALU.mult, op1=ALU.add),
                            reads=[("GT", t % 2), "FW"], writes=[("T2", t % 2)])
                        S.add('act', lambda e, t2=t2, t3=t3: e.activation(out=t3[:, 0:TW], in_=t2[:, 0:TW], func=AF.Silu),
                              reads=[("T2", t % 2)], writes=[("T3", t % 2)])
                        S.add('dve', lambda e, pv=pv, t3=t3, jl=jl, t=t: e.tensor_tensor(
                            out=A2[:, jl, t * TW:(t + 1) * TW], in0=PS[pv][:, 0:TW], in1=t3[:, 0:TW], op=ALU.mult),
                            reads=[P(pv), ("T3", t % 2)], writes=[("A2", jl, t)])
                        if isP and t == NT - 1:
                            S.add('sp', lambda e, gb_=gb_, j=j: e.dma_start(out=fcd[l, n, j], in_=gb_[:, TW:TW + 2]),
                                  reads=[("GT", t % 2)], writes=[], dma=True)
                        if not isP:
                            v = gb_[:, 0:4 * 34].rearrange("p (s c) -> p s c", s=4)
                            for sq in range(4):
                                S.add('sp', lambda e, v=v, j=j, sq=sq: e.dma_start(out=fcd[l, 4 + sq, j], in_=v[:, sq, 32:34]),
                                      reads=[("GT", t % 2)], writes=[], dma=True)
                rows = [load_wr(wdn[l, j]) for j in range(j0, j1)]
                for b in range(NB):
                    t = b * 128 // TW
                    for hf in range(2):
                        pb2 = 6 + hf
                        for ti, j in enumerate(range(j0, j1)):
                            jl = j - j0
                            S.add('pe', lambda e, pb2=pb2, jl=jl, b=b, r=rows[ti], hf=hf, ti=ti, n_=j1 - j0: e.matmul(
                                PS[pb2][:, 0:512], A2[:, jl, b * 128:(b + 1) * 128], WR[r][:, hf * 512:(hf + 1) * 512],
                                start=(ti == 0), stop=(ti == n_ - 1)),
                                reads=[("A2", jl, t), ("WR", rows[ti])], writes=[P(pb2)])
                    x_update(b, gi == 0, (6, 7))
            load_ln(l, 1)
            layer_norm_all()

        for b in range(NB):
            dst = yp[n, b * 128:(b + 1) * 128, :] if isP else ys
            S.add('sp', lambda e, b=b, d=dst: e.dma_start(out=d, in_=X[:, b, :]), reads=[("X", b)], writes=[], dma=True)

    def sample_attention(l, pair, s_qk, s_v, wo_a, conv_rows, KT, VA, QT, PT, AT, CT, RC, OS, FA, FBb, FC, PKt, PQt, KC, CL,
                         decay_chain, proj, x_update):
        TW = 128
        qt = QT[0]
        proj(6, s_qk, 0, 128, 0, ("WA", s_qk))
        S.add('act', lambda e: e.activation(out=qt[0:64, 0, :], in_=PS[6][0:64, 0:128], func=AF.Copy), reads=[P(6)], writes=["QTs"])
        S.add('act', lambda e: e.activation(out=qt[0:64, 1, :], in_=PS[6][64:128, 0:128], func=AF.Copy), reads=[P(6)], writes=["QTs"])
        at = AT[0]
        for sq in range(4):
            S.add('pool', lambda e, sq=sq: e.dma_start(
                out=KC[:], in_=ck[l, sq, :, pair * 128:(pair + 1) * 128].rearrange("(b p) c -> p b c", p=128)),
                writes=["KC"], dma=True, force_barrier=True)
            for hl in range(2):
                S.add('pool', lambda e, sq=sq, hl=hl: e.dma_start(
                    out=VA[:, 0:16, hl, 0:64],
                    in_=cv[l, sq, :, pair * 128 + hl * 64:pair * 128 + (hl + 1) * 64].rearrange("(b p) c -> p b c", p=128)),
                    writes=[("VAs", hl)], dma=True, force_barrier=True)
            for g in range(4):
                pb = g % 2
                for i in range(4):
                    b = g * 4 + i
                    S.add('pe', lambda e, pb=pb, i=i, b=b: e.matmul(
                        PS[pb][:, i * 128:(i + 1) * 128], KC[:, b, :], IDB[:], start=True, stop=True),
                        reads=["KC", "IDB"], writes=[P(pb)])
                S.add('act', lambda e, pb=pb, g=g: e.activation(out=KT[0:64, 0, g * 512:(g + 1) * 512], in_=PS[pb][0:64, 0:512], func=AF.Copy),
                      reads=[P(pb)], writes=[("KTs", 0)])
                S.add('act', lambda e, pb=pb, g=g: e.activation(out=KT[0:64, 1, g * 512:(g + 1) * 512], in_=PS[pb][64:128, 0:512], func=AF.Copy),
                      reads=[P(pb)], writes=[("KTs", 1)])
            proj(7, s_qk, 128, 128, 0, ("WA", s_qk))
            S.add('act', lambda e, sq=sq: e.activation(out=KT[0:64, 0, T:T + 32], in_=PS[7][0:64, sq * 32:(sq + 1) * 32], func=AF.Copy),
                  reads=[P(7)], writes=[("KTs", 0)])
            S.add('act', lambda e, sq=sq: e.activation(out=KT[0:64, 1, T:T + 32], in_=PS[7][64:128, sq * 32:(sq + 1) * 32], func=AF.Copy),
                  reads=[P(7)], writes=[("KTs", 1)])
            for k in range(8):
                S.add('pe', lambda e, k=k, sq=sq: e.matmul(
                    PS[5][0:32, 0:128], XT[:, k, sq * 32:(sq + 1) * 32], WV[s_v][:, k, :], start=(k == 0), stop=(k == 7)),
                    reads=[("XT", 0), ("WV", s_v)], writes=[P(5)])
            vs_ = VST[sq % 2]
            S.add('act', lambda e, vs_=vs_: e.activation(out=vs_[0:32, :], in_=PS[5][0:32, 0:128], func=AF.Copy),
                  reads=[P(5)], writes=[("VST", sq % 2)])
            S.add('sp', lambda e, vs_=vs_, sq=sq: e.dma_start(out=vsd[l, sq, :, pair * 128:(pair + 1) * 128], in_=vs_[0:32, :]),
                  reads=[("VST", sq % 2)], writes=[], dma=True)
            S.add('act', lambda e: e.activation(out=VA[0:32, 16, :, 0:64], in_=PS[5][0:32, 0:128].rearrange("p (h c) -> p h c", h=2), func=AF.Copy),
                  reads=[P(5)], writes=[("VAs", 2)])
            for hl in range(2):
                hh = pair * 2 + hl
                S.add('sp', lambda e, hl=hl, hh=hh, sq=sq: e.dma_start(out=KT[67:70, hl, 0:T + 32], in_=pks[sq, hh, :, :]),
                      reads=["pkd"], writes=[("KTs", hl)], dma=True)
                S.add('sp', lambda e, hl=hl, hh=hh, sq=sq: e.dma_start(out=qt[64:67, hl, sq * 32:(sq + 1) * 32], in_=pqs[sq, hh, :, :]),
                      reads=["pqd"], writes=["QTs"], dma=True)
            for hl in range(2):
                ob = 4
                q_ap = qt[0:70, hl, sq * 32:(sq + 1) * 32]
                for b in range(16):
                    S.add('pe', lambda e, b=b, hl=hl, q_ap=q_ap: e.matmul(
                        PS[2][:, b * 32:(b + 1) * 32], KT[0:70, hl, b * 128:(b + 1) * 128], q_ap, start=True, stop=True),
                        reads=[("KTs", hl), "QTs", "KT"], writes=[P(2)])
                S.add('pe', lambda e, hl=hl, q_ap=q_ap: e.matmul(
                    PS[3][0:32, 0:32], KT[0:70, hl, T:T + 32], q_ap, start=True, stop=False),
                    reads=[("KTs", hl), "QTs", "KT"], writes=[P(3)])
                S.add('pe', lambda e: e.matmul(PS[3][0:32, 0:32], IDB[0:32, 0:32], MASK[0:32, 0:32], start=False, stop=True),
                      reads=["IDB", "MASK"], writes=[P(3)])
                S.add('act', lambda e: e.activation(out=PT[0][:, 0:512], in_=PS[2][:, 0:512], func=AF.Exp, scale=0.125),
                      reads=[P(2)], writes=[("PT", 0)])
                S.add('act', lambda e: e.activation(out=PT[1][0:32, 0:32], in_=PS[3][0:32, 0:32], func=AF.Exp, scale=0.125),
                      reads=[P(3)], writes=[("PT", 1)])
                for b in range(16):
                    S.add('pe', lambda e, b=b, hl=hl: e.matmul(
                        PS[ob][:, 0:32], VA[:, b, hl, :], PT[0][:, b * 32:(b + 1) * 32], start=(b == 0), stop=False),
                        reads=[("PT", 0), ("VAs", hl), "VA"], writes=[P(ob)])
                S.add('pe', lambda e, hl=hl: e.matmul(
                    PS[ob][:, 0:32], VA[0:32, 16, hl, :], PT[1][0:32, 0:32], start=False, stop=True),
                    reads=[("PT", 1), ("VAs", 2), "VA"], writes=[P(ob)])
                S.add('dve', lambda e: e.reciprocal(out=RC[64:128, 0:32], in_=PS[ob][64:128, 0:32]), reads=[P(ob)], writes=["RC"])
                S.add('dve', lambda e, hl=hl, sq=sq: e.tensor_tensor(
                    out=at[hl * 64:(hl + 1) * 64, sq * 32:(sq + 1) * 32], in0=PS[ob][0:64, 0:32], in1=RC[64:128, 0:32], op=ALU.mult),
                    reads=[P(ob), "RC"], writes=[("AT", 0)])
        terms = [(at[:, 0:128], WR[wo_a], [("AT", 0), ("WR", wo_a)])]
        if pair == 0:
            for c in range(4):
                terms.append((CT[:, c, 0:128], WR[conv_rows[c]], [("CT", c, 0), ("WR", conv_rows[c])]))
        for hf in range(2):
            pb2 = 6 + hf
            for ti, (lt, rw, rs) in enumerate(terms):
                S.add('pe', lambda e, pb2=pb2, lt=lt, rw=rw, hf=hf, ti=ti, nt_=len(terms): e.matmul(
                    PS[pb2][:, 0:512], lt, rw[:, hf * 512:(hf + 1) * 512], start=(ti == 0), stop=(ti == nt_ - 1)),
                    reads=rs, writes=[P(pb2)])
        x_update(0, pair == 0, (6, 7))

    for n in range(nps):
        S.phase = n
        run_sequence('p', n)
    if do_sample:
        S.phase = 4
        run_sequence('s', 0)

    sem_keys = S.finalize()
    with ExitStack() as es:
        sems = {k: es.enter_context(nc.semaphore("s_" + "_".join(str(x) for x in k))) for k in sem_keys}
        block = es.enter_context(nc.Block())

        @block.tensor
        def _(e):
            S.emit(e, 'pe', sems)

        @block.scalar
        def _(e):
            S.emit(e, 'act', sems)

        @block.vector
        def _(e):
            S.emit(e, 'dve', sems)

        @block.gpsimd
        def _(e):
            S.emit(e, 'pool', sems)

        @block.sync
        def _(e):
            S.emit(e, 'sp', sems)
    return nc


def _prep_weights(w_in, b_f, conv_w, conv_b, w_out, ln1_g, ln1_b, w_up, ffn_conv_w, ffn_conv_b, w_down, ln2_g, ln2_b):
    f = np.float32
    w_in = np.asarray(w_in, f)
    w_up = np.asarray(w_up, f)

    def unit(mat):
        C = mat.shape[1]
        return mat.reshape(8, 128, C).transpose(1, 0, 2)

    wa = np.empty((L, 32, 128, 8, 256), f)
    wv = np.empty((L, 4, 128, 8, 128), f)
    wf = np.empty((L, 128, 8, 8), f)
    for l in range(L):
        W = w_in[l]
        q, k, v = W[:, 0:512], W[:, 512:1024], W[:, 1024:1536]
        fl = W[:, 1536:1544]
        gb, gc, hh = W[:, 1544:2056], W[:, 2056:2568], W[:, 2568:3080]
        for p in range(4):
            wa[l, p, :, :, 0:128] = unit(q[:, p * 128:(p + 1) * 128])
            wa[l, p, :, :, 128:256] = unit(k[:, p * 128:(p + 1) * 128])
            wv[l, p] = unit(v[:, p * 128:(p + 1) * 128])
        for c in range(4):
            wa[l, 4 + c, :, :, 0:128] = unit(gb[:, c * 128:(c + 1) * 128])
            wa[l, 4 + c, :, :, 128:256] = unit(gc[:, c * 128:(c + 1) * 128])
        for c2 in range(2):
            wa[l, 8 + c2, :, :, 0:128] = unit(hh[:, (2 * c2) * 128:(2 * c2 + 1) * 128])
            wa[l, 8 + c2, :, :, 128:256] = unit(hh[:, (2 * c2 + 1) * 128:(2 * c2 + 2) * 128])
        for j in range(NCH):
            wa[l, 10 + j, :, :, 0:128] = unit(w_up[l][:, j * 128:(j + 1) * 128])
            wa[l, 10 + j, :, :, 128:256] = unit(w_up[l][:, DFF + j * 128:DFF + (j + 1) * 128])
        wf[l] = unit(fl)
    wo = np.ascontiguousarray(np.asarray(w_out, f).reshape(L, 8, 128, D))
    wdn = np.ascontiguousarray(np.asarray(w_down, f).reshape(L, NCH, 128, D))
    lnrep = np.empty((L, 2, 128, 2 * D), f)
    lnrep[:, 0, :, 0:D] = np.asarray(ln1_g, f)[:, None, :]
    lnrep[:, 0, :, D:] = np.asarray(ln1_b, f)[:, None, :]
    lnrep[:, 1, :, 0:D] = np.asarray(ln2_g, f)[:, None, :]
    lnrep[:, 1, :, D:] = np.asarray(ln2_b, f)[:, None, :]
    cwd = np.ascontiguousarray(np.asarray(conv_w, f).reshape(L, 3, 4, 128).transpose(3, 0, 2, 1)).reshape(128, L * 4 * 3)
    cbd = np.ascontiguousarray(np.asarray(conv_b, f).reshape(L, 4, 128).transpose(2, 0, 1)).reshape(128, L * 4)
    fwd = np.ascontiguousarray(np.asarray(ffn_conv_w, f).reshape(L, 3, NCH, 128).transpose(3, 0, 2, 1)).reshape(128, L * NCH * 3)
    fbd = np.ascontiguousarray(np.asarray(ffn_conv_b, f).reshape(L, NCH, 128).transpose(2, 0, 1)).reshape(128, L * NCH)
    bfd = np.ascontiguousarray(np.asarray(b_f, f).T)
    idfd = np.eye(128, dtype=f)
    idbd = np.eye(128).astype(ml_dtypes.bfloat16)
    kk = np.arange(128)[:, None]
    qq = np.arange(128)[None, :]
    maskd = np.where(kk <= qq, 0.0, NEG).astype(ml_dtypes.bfloat16)
    return dict(wa=wa.reshape(L, 32, 128, 8 * 256), wv=wv.reshape(L, 4, 128, 8 * 128), wf=wf.reshape(L, 128, 64), wo=wo, wdn=wdn,
                lnrep=lnrep, cwd=cwd, cbd=cbd, fwd=fwd, fbd=fbd, bfd=bfd, idfd=idfd, idbd=idbd, maskd=maskd)


_NC_CACHE = {}


def kernel(x_prompt, x_sample, cache_k, cache_v, cache_logf, state_mix_conv, state_ffn_conv,
           w_in, b_f, conv_w, conv_b, w_out, ln1_g, ln1_b, w_up, ffn_conv_w, ffn_conv_b, w_down, ln2_g, ln2_b):
    f = np.float32
    nps = int(os.environ.get("MK_NPS", NPS))
    nl = int(os.environ.get("MK_NL", L))
    do_s = int(os.environ.get("MK_SAMPLE", 1)) == 1
    ncores = 8
    wd = _prep_weights(w_in, b_f, conv_w, conv_b, w_out, ln1_g, ln1_b, w_up, ffn_conv_w, ffn_conv_b, w_down, ln2_g, ln2_b)
    x_prompt = np.asarray(x_prompt, f)
    x_sample = np.asarray(x_sample, f)
    cache_k = np.asarray(cache_k, f)
    cache_v = np.asarray(cache_v, f)
    cache_logf = np.asarray(cache_logf, f)
    smc = np.asarray(state_mix_conv, f)
    sfc = np.asarray(state_ffn_conv, f)
    in_maps = []
    for c in range(ncores):
        sl = slice(c * 4, (c + 1) * 4)
        m = dict(wd)
        m["xp"] = np.ascontiguousarray(x_prompt[sl])
        m["xs"] = np.ascontiguousarray(x_sample[sl].reshape(128, D))
        m["ck"] = np.ascontiguousarray(cache_k[:, sl].reshape(L, 4, T, 512))
        m["cv"] = np.ascontiguousarray(cache_v[:, sl].reshape(L, 4, T, 512))
        m["clf"] = np.ascontiguousarray(cache_logf[:, sl])
        m["smix"] = np.ascontiguousarray(smc[:, sl].reshape(L, 4, 2, 4, 128).transpose(4, 0, 3, 1, 2)).reshape(128, -1)
        m["sffn"] = np.ascontiguousarray(sfc[:, sl].reshape(L, 4, 2, NCH, 128).transpose(4, 0, 3, 1, 2)).reshape(128, -1)
        in_maps.append(m)
    key = (nps, nl, do_s)
    if key not in _NC_CACHE:
        _NC_CACHE[key] = build_program(nps, nl, do_s)
    nc = _NC_CACHE[key]
    res = run_bass_kernel_spmd(nc, in_maps, core_ids=list(range(ncores)))
    R = res.results
    B = 32
    y_prompt = np.empty((B, T, D), f)
    y_sample = np.empty((B, 32, D), f)
    k_prompt = np.empty((L, B, T, H, 64), f)
    v_prompt = np.empty((L, B, T, H, 64), f)
    logf_prompt = np.empty((L, B, T, H), f)
    mix_p = np.empty((L, B, 2, 512), f)
    ffn_p = np.empty((L, B, 2, DFF), f)
    k_sample = np.empty((L, B, 32, H, 64), f)
    v_sample = np.empty((L, B, 32, H, 64), f)
    logf_sample = np.empty((L, B, 32, H), f)
    mix_s = np.empty((L, B, 2, 512), f)
    ffn_s = np.empty((L, B, 2, DFF), f)
    for c in range(ncores):
        r = R[c]
        sl = slice(c * 4, (c + 1) * 4)
        y_prompt[sl] = r["yp"]
        y_sample[sl] = r["ys"].reshape(4, 32, D)
        k_prompt[:, sl] = r["kpd"].reshape(L, 4, H, 64, T).transpose(0, 1, 4, 2, 3)
        v_prompt[:, sl] = r["vpd"].reshape(L, 4, T, H, 64)
        logf_prompt[:, sl] = r["lfpd"].transpose(0, 1, 3, 2)
        mc = r["mcd"]
        fc = r["fcd"]
        mix_p[:, sl] = mc[:, 0:4].transpose(0, 1, 4, 2, 3).reshape(L, 4, 2, 512)
        mix_s[:, sl] = mc[:, 4:8].transpose(0, 1, 4, 2, 3).reshape(L, 4, 2, 512)
        ffn_p[:, sl] = fc[:, 0:4].transpose(0, 1, 4, 2, 3).reshape(L, 4, 2, DFF)
        ffn_s[:, sl] = fc[:, 4:8].transpose(0, 1, 4, 2, 3).reshape(L, 4, 2, DFF)
        k_sample[:, sl] = r["ksd"].reshape(L, H, 64, 4, 32).transpose(0, 3, 4, 1, 2)
        v_sample[:, sl] = r["vsd"].reshape(L, 4, 32, H, 64)
        logf_sample[:, sl] = r["lfsd"].reshape(L, H, 4, 32).transpose(0, 2, 3, 1)
    return (y_prompt, y_sample, k_prompt, v_prompt, logf_prompt, mix_p, ffn_p,
            k_sample, v_sample, logf_sample, mix_s, ffn_s)
```

```python
import os
import types
import numpy as np
import ml_dtypes
from contextlib import ExitStack
import concourse.bass as bass
import concourse.mybir as mybir
from concourse.bass_utils import run_bass_kernel_spmd

F32 = mybir.dt.float32
BF16 = mybir.dt.bfloat16
U8 = mybir.dt.uint8
AF = mybir.ActivationFunctionType
ALU = mybir.AluOpType

L = 4
D = 1024
H = 8
T = 2048
DFF = 2816
NCH = 22
NPS = 4
ALPHA = float(8 ** 0.25)
EPS = 1e-5
NEG = -30000.0
NS = 8
FFN_GROUPS = [(0, 6), (6, 12), (12, 17), (17, 22)]


def _freeze(fn):
    if fn is None or fn.__closure__ is None:
        return fn
    cells = []
    for c in fn.__closure__:
        try:
            cells.append(types.CellType(c.cell_contents))
        except ValueError:
            cells.append(c)
    return types.FunctionType(fn.__code__, fn.__globals__, fn.__name__, fn.__defaults__, tuple(cells))


class Sched:
    def __init__(self):
        self.ops = []
        self.last_w = {}
        self.readers = {}
        self.phase = 0
        self.dma_rr = {'sp': 0, 'pool': 0}
        self.dma_last = {}
        self.eng_last = {}
        self.barrier_deps = None
        self.barrier_seen = set()

    def stage(self, name):
        if os.environ.get("MK_STOP", "") == name:
            self.stopped = True

    def add(self, eng, fn, reads=(), writes=(), dma=False, force_barrier=False):
        if getattr(self, 'stopped', False):
            return -1
        i = len(self.ops)
        ps_reads = [r for r in reads if isinstance(r, tuple) and r[0] == 'ps']
        if ps_reads:
            reads = [r for r in reads if not (isinstance(r, tuple) and r[0] == 'ps')]
            writes = list(writes) + ps_reads
        deps = set()
        for r in reads:
            w = self.last_w.get(r)
            if w is not None:
                deps.add(w)
        for w_ in writes:
            w = self.last_w.get(w_)
            if w is not None:
                deps.add(w)
            deps.update(self.readers.get(w_, {}).values())
        op = dict(eng=eng, fn=_freeze(fn), deps=deps, dma=dma, phase=self.phase, signal=dma, sig=None)
        if dma:
            k = self.dma_rr[eng]
            self.dma_rr[eng] += 1
            s = ('d', eng, k % NS)
            prev = self.dma_last.get(s)
            if prev is not None:
                deps.add(prev)
            self.dma_last[s] = i
            op['dsem'] = s
            rkey = s
        else:
            rkey = eng
        if self.barrier_deps is not None and force_barrier:
            deps.update(self.barrier_deps)
        elif self.barrier_deps is not None and eng != 'pool' and eng not in self.barrier_seen:
            deps.update(self.barrier_deps)
            self.barrier_seen.add(eng)
        deps.discard(i)
        for r in reads:
            self.readers.setdefault(r, {})[rkey] = i
        for w_ in writes:
            self.last_w[w_] = i
            self.readers[w_] = {}
        self.eng_last[eng] = i
        self.ops.append(op)
        return i

    def barrier(self):
        deps = set()
        for eng in ('pe', 'act', 'dve'):
            if eng in self.eng_last:
                deps.add(self.eng_last[eng])
        for s, i in self.dma_last.items():
            if s[1] == 'sp':
                deps.add(i)
        self.barrier_deps = deps
        self.barrier_seen = set()

    def finalize(self):
        self.ops.append(dict(eng='sp', fn=None, deps=set(self.dma_last.values()), dma=False,
                             phase=self.phase, signal=False, sig=None))
        for op in self.ops:
            for d in op['deps']:
                p = self.ops[d]
                if op['eng'] == 'pe' and p['eng'] == 'pe' and not p['dma']:
                    continue
                p['signal'] = True
        cnt = {}
        for op in self.ops:
            if op['dma']:
                s = op['dsem']
                cnt[s] = cnt.get(s, 0) + 1
                op['sig'] = (s, 16 * cnt[s])
            elif op['signal']:
                s = (op['eng'], op['phase'])
                cnt[s] = cnt.get(s, 0) + 1
                op['sig'] = (s, cnt[s])
        return sorted(set(op['sig'][0] for op in self.ops if op['sig'] is not None), key=str)

    def emit(self, e, eng, sems):
        waited = {}
        ops = self.ops
        for op in ops:
            if op['eng'] != eng:
                continue
            need = {}
            for d in op['deps']:
                p = ops[d]
                if eng == 'pe' and p['eng'] == 'pe' and not p['dma']:
                    continue
                if p['sig'] is None:
                    raise RuntimeError("dep on non-signaling op")
                s, v = p['sig']
                if need.get(s, 0) < v:
                    need[s] = v
            for s, v in need.items():
                if waited.get(s, 0) < v:
                    e.wait_ge(sems[s], v)
                    waited[s] = v
            if op['fn'] is not None:
                ins = op['fn'](e)
                if op['sig'] is not None:
                    ins.then_inc(sems[op['sig'][0]], 16 if op['dma'] else 1)


def build_program(nps=NPS, nlayers=L, do_sample=True):
    nc = bass.Bass("TRN2", target_bir_lowering=False)
    S = Sched()

    def din(name, shape, dt=F32):
        return nc.dram_tensor(name, list(shape), dt, kind="ExternalInput").ap()

    def dout(name, shape, dt=F32):
        return nc.dram_tensor(name, list(shape), dt, kind="ExternalOutput").ap()

    xp = din("xp", [NPS, T, D])
    xs = din("xs", [128, D])
    ck = din("ck", [L, 4, T, 512])
    cv = din("cv", [L, 4, T, 512])
    clf = din("clf", [L, 4, T, 8])
    smix = din("smix", [128, L * 4 * 4 * 2])
    sffn = din("sffn", [128, L * NCH * 4 * 2])
    wa = din("wa", [L, 32, 128, 8 * 256])
    wv = din("wv", [L, 4, 128, 8 * 128])
    wf = din("wf", [L, 128, 8 * 8])
    wo = din("wo", [L, 8, 128, D])
    wdn = din("wdn", [L, NCH, 128, D])
    lnrep = din("lnrep", [L, 2, 128, 2 * D])
    cwd = din("cwd", [128, L * 4 * 3])
    cbd = din("cbd", [128, L * 4])
    fwd = din("fwd", [128, L * NCH * 3])
    fbd = din("fbd", [128, L * NCH])
    bfd = din("bfd", [8, L])
    idfd = din("idfd", [128, 128])
    idbd = din("idbd", [128, 128], BF16)
    maskd = din("maskd", [128, 128], BF16)

    yp = dout("yp", [NPS, T, D])
    ys = dout("ys", [128, D])
    kpd = dout("kpd", [L, NPS, 512, T])
    vpd = dout("vpd", [L, NPS, T, 512])
    lfpd = dout("lfpd", [L, NPS, 8, T])
    mcd = dout("mcd", [L, 8, 4, 128, 2])
    fcd = dout("fcd", [L, 8, NCH, 128, 2])
    ksd = dout("ksd", [L, 512, 128])
    vsd = dout("vsd", [L, 4, 32, 512])
    lfsd = dout("lfsd", [L, 8, 128])
    pkd = nc.dram_tensor("pkd", [8, 3, T], BF16).ap()
    pqd = nc.dram_tensor("pqd", [8, 3, T], BF16).ap()
    pks = nc.dram_tensor("pks", [4, 8, 3, T + 32], BF16).ap()
    pqs = nc.dram_tensor("pqs", [4, 8, 3, 32], BF16).ap()

    def sb(name, shape, dt):
        return nc.alloc_sbuf_tensor(name, list(shape), dt)

    X = sb("X", [128, 16, D], F32)
    XT = sb("XT", [128, 8, T], BF16)
    IDF = sb("IDF", [128, 128], F32)
    IDB = sb("IDB", [128, 128], BF16)
    MASK = sb("MASK", [128, 128], BF16)
    ONES = sb("ONES", [8, 512], F32)
    CW = sb("CW", [128, L * 4 * 3], F32)
    CB = sb("CB", [128, L * 4], F32)
    FW = sb("FW", [128, L * NCH * 3], F32)
    FB = sb("FB", [128, L * NCH], F32)
    BFs = sb("BFs", [8, L], F32)
    NBF = sb("NBF", [8, L], F32)
    SMIX = sb("SMIX", [128, L * 4 * 4 * 2], F32)
    SFFN = sb("SFFN", [128, L * NCH * 4 * 2], F32)
    EPSC = sb("EPSC", [128, 1], F32)
    NWA = 4
    WA = [sb(f"WA{i}", [128, 8, 256], BF16) for i in range(NWA)]
    WV = [sb(f"WV{i}", [128, 8, 128], BF16) for i in range(2)]
    WF = sb("WF", [128, 8, 8], BF16)
    NWR = 6
    WR = [sb(f"WR{i}", [128, D], BF16) for i in range(NWR)]
    GBt = sb("GBt", [128, 2 * D], F32)
    KST = [sb(f"KST{i}", [128, 512], F32) for i in range(2)]
    VST = [sb(f"VST{i}", [128, 128], F32) for i in range(2)]
    SST = [sb(f"SST{i}", [128, 2], F32) for i in range(4)]
    LNS = sb("LNS", [128, 16, 16], F32)
    LNR = sb("LNR", [128, 16, 4], F32)
    ARENA_BYTES = 55 * 1024
    AR = sb("AR", [128, ARENA_BYTES], U8)

    class Arena:
        def __init__(self):
            self.off = 0

        def reset(self):
            self.off = 0

        def get(self, nfree, dt):
            nbytes = nfree * (2 if dt == BF16 else 4)
            nbytes = (nbytes + 31) // 32 * 32
            v = AR[:, self.off:self.off + nbytes].bitcast(dt)
            self.off += nbytes
            assert self.off <= ARENA_BYTES, self.off
            return v

    arena = Arena()
    PS = [nc.alloc_psum_tensor(f"ps{i}", [128, 512], F32) for i in range(8)]

    def P(i):
        return ("ps", i)

    wa_rr = [0]
    wr_rr = [0]
    wv_rr = [0]

    def load_wa(l, unit):
        slot = wa_rr[0] % NWA
        wa_rr[0] += 1
        S.add('pool', lambda e, s=slot, l=l, u=unit: e.dma_start(
            out=WA[s][:].rearrange("p k c -> p (k c)"), in_=wa[l, u]), writes=[("WA", slot)], dma=True)
        return slot

    def load_wv(l, pair):
        slot = wv_rr[0] % 2
        wv_rr[0] += 1
        S.add('pool', lambda e, s=slot, l=l, u=pair: e.dma_start(
            out=WV[s][:].rearrange("p k c -> p (k c)"), in_=wv[l, u]), writes=[("WV", slot)], dma=True)
        return slot

    def load_wr(src_ap):
        slot = wr_rr[0] % NWR
        wr_rr[0] += 1
        S.add('pool', lambda e, s=slot, a=src_ap: e.dma_start(out=WR[s][:], in_=a), writes=[("WR", slot)], dma=True)
        return slot

    def load_ln(l, which):
        S.add('sp', lambda e, l=l, w=which: e.dma_start(out=GBt[:], in_=lnrep[l, w]), writes=["GBt"], dma=True)

    for dst, src, nm in [(IDF, idfd, "IDF"), (IDB, idbd, "IDB"), (MASK, maskd, "MASK"), (CW, cwd, "CW"), (CB, cbd, "CB"),
                         (FW, fwd, "FW"), (FB, fbd, "FB"), (BFs, bfd, "BFs"), (SMIX, smix, "SMIX"), (SFFN, sffn, "SFFN")]:
        S.add('sp', lambda e, d=dst, s=src: e.dma_start(out=d[:], in_=s), writes=[nm], dma=True)
    S.add('dve', lambda e: e.memset(ONES[:], 1.0), writes=["ONES"])
    S.add('dve', lambda e: e.memset(EPSC[:], EPS), writes=["EPSC"])
    S.add('dve', lambda e: e.tensor_scalar(out=NBF[:], in0=BFs[:], scalar1=-1.0, scalar2=None, op0=ALU.mult),
          reads=["BFs"], writes=["NBF"])

    def run_sequence(kind, n):
        isP = kind == 'p'
        TT = T if isP else 128
        TW = 512 if isP else 128
        NT = TT // TW
        NB = TT // 128
        sidx = n if isP else None

        for b in range(NB):
            src = xp[n, b * 128:(b + 1) * 128, :] if isP else xs
            S.add('sp', lambda e, b=b, s=src: e.dma_start(out=X[:, b, :], in_=s), writes=[("X", b)], dma=True)

        def build_xt():
            bank = [0]
            for b in range(NB):
                for half in range(2):
                    pb = bank[0] % 8
                    bank[0] += 1
                    for i in range(4):
                        k = half * 4 + i
                        S.add('pe', lambda e, pb=pb, i=i, b=b, k=k: e.transpose(
                            PS[pb][:, i * 128:(i + 1) * 128], X[:, b, k * 128:(k + 1) * 128], IDF[:]),
                            reads=[("X", b), "IDF"], writes=[P(pb)])
                    S.add('act', lambda e, pb=pb, b=b, half=half: e.activation(
                        out=XT[:, half * 4:(half + 1) * 4, b * 128:(b + 1) * 128],
                        in_=PS[pb][:].rearrange("p (a c) -> p a c", a=4), func=AF.Copy),
                        reads=[P(pb)], writes=[("XT", b)])

        def xt_res(t):
            return [("XT", b) for b in range(t * TW // 128, (t + 1) * TW // 128)]

        def proj(ps_i, wslot, col0, ncols, t, wres, wt=None):
            W = WA[wslot] if wt is None else wt
            for k in range(8):
                S.add('pe', lambda e, k=k, W=W: e.matmul(
                    PS[ps_i][0:ncols, 0:TW], W[:, k, col0:col0 + ncols], XT[:, k, t * TW:(t + 1) * TW],
                    start=(k == 0), stop=(k == 7)),
                    reads=[wres] + xt_res(t), writes=[P(ps_i)])

        def x_update(b, first, banks):
            for hf in range(2):
                xs_ = X[:, b, hf * 512:(hf + 1) * 512]
                pb = banks[hf]
                if first:
                    S.add('dve', lambda e, xs_=xs_, pb=pb: e.scalar_tensor_tensor(
                        out=xs_, in0=xs_, scalar=ALPHA, in1=PS[pb][:, 0:512], op0=ALU.mult, op1=ALU.add),
                        reads=[P(pb)], writes=[("X", b)])
                else:
                    S.add('dve', lambda e, xs_=xs_, pb=pb: e.tensor_tensor(
                        out=xs_, in0=xs_, in1=PS[pb][:, 0:512], op=ALU.add),
                        reads=[P(pb)], writes=[("X", b)])

        def layer_norm_all():
            GRP = 4
            for g0 in range(0, NB, GRP):
                blks = list(range(g0, min(NB, g0 + GRP)))
                for b in blks:
                    for hf in range(2):
                        S.add('dve', lambda e, hf=hf, b=b: e.bn_stats(out=LNS[:, b, hf * 6:(hf + 1) * 6], in_=X[:, b, hf * 512:(hf + 1) * 512]),
                              reads=[("X", b)], writes=[("LNS", b)])
                    S.add('dve', lambda e, b=b: e.bn_aggr(out=LNS[:, b, 12:14], in_=LNS[:, b, 0:12]), reads=[("LNS", b)], writes=[("LNS", b)])
                for b in blks:
                    S.add('act', lambda e, b=b: e.activation(out=LNR[:, b, 0:1], in_=LNS[:, b, 13:14], func=AF.Sqrt, bias=EPSC[:, 0:1], scale=1.0),
                          reads=[("LNS", b), "EPSC"], writes=[("LNR", b)])
                for b in blks:
                    S.add('dve', lambda e, b=b: e.reciprocal(out=LNR[:, b, 1:2], in_=LNR[:, b, 0:1]), reads=[("LNR", b)], writes=[("LNR", b)])
                for b in blks:
                    S.add('dve', lambda e, b=b: e.scalar_tensor_tensor(out=X[:, b, :], in0=X[:, b, :], scalar=LNS[:, b, 12:13], in1=GBt[:, 0:D],
                                                                      op0=ALU.subtract, op1=ALU.mult),
                          reads=["GBt", ("LNS", b)], writes=[("X", b)])
                    S.add('dve', lambda e, b=b: e.scalar_tensor_tensor(out=X[:, b, :], in0=X[:, b, :], scalar=LNR[:, b, 1:2], in1=GBt[:, D:2 * D],
                                                                      op0=ALU.mult, op1=ALU.add),
                          reads=["GBt", ("LNR", b)], writes=[("X", b)])

        def seg(ap, w):
            return ap

        for l in range(nlayers):
            build_xt()
            S.stage("xt")
            S.barrier()
            arena.reset()
            KT = arena.get(2 * (T if isP else T + 32), BF16).rearrange("p (h t) -> p h t", h=2)
            NVB = NB if isP else 17
            VA = arena.get(NVB * 2 * 128, BF16).rearrange("p (b h c) -> p b h c", b=NVB, h=2)
            QT = [arena.get(2 * TW, BF16).rearrange("p (h t) -> p h t", h=2) for _ in range(2)]
            PT = [arena.get(1024, BF16) for _ in range(2)]
            AT = [arena.get(TW, BF16) for _ in range(2)]
            ct_off = arena.off
            CT = arena.get(4 * TT, BF16).rearrange("p (c t) -> p c t", c=4)
            UT = [arena.get(TW + 8, F32) for _ in range(2)]
            HS = arena.get(512, F32)
            T1 = arena.get(512, F32)
            RC = arena.get(512, F32)
            OS = arena.get(512, F32)
            end_off = arena.off
            if isP:
                arena.off = ct_off
            FA = arena.get(512, F32)
            FBb = [arena.get(512, F32) for _ in range(2)]
            FC = arena.get(512, F32)
            PKt = arena.get(3 * 512, BF16).rearrange("p (j t) -> p j t", j=3)
            PQt = arena.get(3 * 512, BF16).rearrange("p (j t) -> p j t", j=3)
            if isP:
                arena.off = end_off
            KC = CL = LPS = None
            if not isP:
                KC = arena.get(16 * 128, BF16).rearrange("p (b c) -> p b c", b=16)
                CL = arena.get(16 * 8, F32).rearrange("p (b c) -> p b c", b=16)
                LPS = arena.get(128, F32)

            S.add('dve', lambda e, KT=KT: e.memset(KT[64:70, :, :], 1.0),
                  writes=["KT", ("KTaug", 0), ("KTaug", 1), ("KTs", 0), ("KTs", 1)])
            for qi, q in enumerate(QT):
                S.add('dve', lambda e, q=q: e.memset(q[64:70, :, :], 1.0), writes=[("QT", qi), "QTs"])
            S.add('dve', lambda e, VA=VA: e.memset(VA[:, :, :, 64:128], 1.0), writes=["VA"])

            S.add('pool', lambda e, l=l: e.dma_start(out=WF[:].rearrange("p k c -> p (k c)"), in_=wf[l]), writes=["WF"], dma=True)

            def decay_chain(src_fa_ready_res, width, carry, slot, kdst, qdst):
                w = width
                cur = FBb[slot]
                ini = 0.0 if carry is None else carry
                S.add('dve', lambda e, cur=cur, ini=ini, w=w: e.tensor_tensor_scan(
                    out=cur[0:8, 0:w], data0=ONES[0:8, 0:w], data1=FA[0:8, 0:w], initial=ini, op0=ALU.mult, op1=ALU.add),
                    reads=["FA", "ONES", ("FB", 1 - slot)], writes=[("FB", slot)])
                S.add('dve', lambda e, cur=cur, w=w: e.tensor_scalar(out=FC[0:8, 0:w], in0=cur[0:8, 0:w], scalar1=8.0, scalar2=None, op0=ALU.mult),
                      reads=[("FB", slot)], writes=["FC"])
                for j in range(3):
                    S.add('dve', lambda e, j=j, w=w: e.tensor_copy(out=PKt[0:8, j, 0:w], in_=FC[0:8, 0:w]), reads=["FC"], writes=["PKt"])
                    if j < 2:
                        S.add('dve', lambda e, j=j, w=w: e.tensor_tensor(out=FC[0:8, 0:w], in0=FC[0:8, 0:w], in1=PKt[0:8, j, 0:w], op=ALU.subtract),
                              reads=["PKt"], writes=["FC"])
                S.add('sp', lambda e, w=w, kdst=kdst: e.dma_start(out=kdst, in_=PKt[0:8, :, 0:w]), reads=["PKt"], writes=["pkd"], dma=True)
                if qdst is not None:
                    S.add('dve', lambda e, w=w: e.tensor_scalar(out=PQt[0:8, :, 0:w], in0=PKt[0:8, :, 0:w], scalar1=-1.0, scalar2=None, op0=ALU.mult),
                          reads=["PKt"], writes=["PQt"])
                    S.add('sp', lambda e, w=w, qdst=qdst: e.dma_start(out=qdst, in_=PQt[0:8, :, 0:w]), reads=["PQt"], writes=["pqd"], dma=True)
                return cur[0:8, w - 1:w]

            def lp_from_psum(pb, w, outdst):
                S.add('act', lambda e, pb=pb, w=w: e.activation(out=FA[0:8, 0:w], in_=PS[pb][0:8, 0:w], func=AF.Exp,
                                                              bias=NBF[:, l:l + 1], scale=-1.0), reads=[P(pb), "NBF"], writes=["FA"])
                S.add('act', lambda e, w=w: e.activation(out=FA[0:8, 0:w], in_=FA[0:8, 0:w], func=AF.Ln, bias=1.0, scale=1.0),
                      reads=["FA"], writes=["FA"])
                S.add('dve', lambda e, w=w: e.tensor_scalar(out=FC[0:8, 0:w], in0=FA[0:8, 0:w], scalar1=-1.0, scalar2=None, op0=ALU.mult),
                      reads=["FA"], writes=["FC"])
                S.add('sp', lambda e, w=w, o=outdst: e.dma_start(out=o, in_=FC[0:8, 0:w]), reads=["FC"], writes=[], dma=True)

            if isP:
                carry = None
                for t in range(NT):
                    pb = t % 4
                    proj(pb, None, 0, 8, t, "WF", wt=WF)
                    lp_from_psum(pb, TW, lfpd[l, n, :, t * TW:(t + 1) * TW])
                    carry = decay_chain(None, TW, carry, t % 2, pkd[:, :, t * TW:(t + 1) * TW], pqd[:, :, t * TW:(t + 1) * TW])
            else:
                proj(0, None, 0, 8, 0, "WF", wt=WF)
                lp_from_psum(0, 128, lfsd[l])
                S.add('dve', lambda e: e.tensor_copy(out=LPS[0:8, 0:128], in_=FA[0:8, 0:128]), reads=["FA"], writes=["LPS"])
                for sq in range(4):
                    S.add('sp', lambda e, sq=sq: e.dma_start(out=CL[:], in_=clf[l, sq].rearrange("(b p) h -> p b h", p=128)),
                          writes=["CL"], dma=True)
                    carry = None
                    for t4 in range(4):
                        pb = t4 % 4
                        for i in range(4):
                            b = t4 * 4 + i
                            S.add('pe', lambda e, pb=pb, i=i, b=b: e.transpose(PS[pb][0:8, i * 128:(i + 1) * 128], CL[:, b, :], IDF[:]),
                                  reads=["CL", "IDF"], writes=[P(pb)])
                        S.add('act', lambda e, pb=pb: e.activation(out=FA[0:8, 0:512], in_=PS[pb][0:8, 0:512], func=AF.Copy, scale=-1.0),
                              reads=[P(pb)], writes=["FA"])
                        carry = decay_chain(None, 512, carry, t4 % 2, pks[sq, :, :, t4 * 512:(t4 + 1) * 512], None)
                    S.add('dve', lambda e, sq=sq: e.tensor_copy(out=FA[0:8, 0:32], in_=LPS[0:8, sq * 32:(sq + 1) * 32]), reads=["LPS"], writes=["FA"])
                    decay_chain(None, 32, carry, 0, pks[sq, :, :, T:T + 32], pqs[sq])

            S.stage("F")
            if isP:
                S.barrier()
            cslot = {}

            def conv_views(buf):
                if isP:
                    return buf[:, 2:2 + TW], buf[:, 1:1 + TW], buf[:, 0:TW], None
                v = buf[:, 0:4 * 34].rearrange("p (s c) -> p s c", s=4)
                return v[:, :, 2:34], v[:, :, 1:33], v[:, :, 0:32], v[:, :, 0:2]

            def psv(pb):
                if isP:
                    return PS[pb][:, 0:TW]
                return PS[pb][:, 0:128].rearrange("p (s c) -> p s c", s=4)

            def sbv(ap):
                if isP:
                    return ap
                return ap.rearrange("p (s c) -> p s c", s=4)

            h_slot = None
            for c in range(4):
                s_bc = load_wa(l, 4 + c)
                if c % 2 == 0:
                    h_slot = load_wa(l, 8 + c // 2)
                for t in range(NT):
                    base = (3 * (c * NT + t)) % 6
                    pgb, pgc, ph = base, base + 1, base + 2
                    proj(pgb, s_bc, 0, 128, t, ("WA", s_bc))
                    proj(pgc, s_bc, 128, 128, t, ("WA", s_bc))
                    proj(ph, h_slot, (c % 2) * 128, 128, t, ("WA", h_slot))
                    ub = UT[t % 2]
                    cur, m1, m2, halo = conv_views(ub)
                    S.add('act', lambda e, ph=ph: e.activation(out=sbv(HS[:, 0:TW]), in_=psv(ph), func=AF.Copy),
                          reads=[P(ph)], writes=["HS"])
                    S.add('dve', lambda e, pgc=pgc, cur=cur: e.tensor_tensor(out=cur, in0=psv(pgc), in1=sbv(HS[:, 0:TW]), op=ALU.mult),
                          reads=[P(pgc), "HS"], writes=[("UT", t % 2)])
                    if isP:
                        if t == 0:
                            S.add('dve', lambda e, ub=ub: e.memset(ub[:, 0:2], 0.0), writes=[("UT", t % 2)])
                        else:
                            pu = UT[(t - 1) % 2]
                            S.add('dve', lambda e, ub=ub, pu=pu: e.tensor_copy(out=ub[:, 0:2], in_=pu[:, TW:TW + 2]),
                                  reads=[("UT", (t - 1) % 2)], writes=[("UT", t % 2)])
                    else:
                        o0 = ((l * 4 + c) * 4) * 2
                        S.add('dve', lambda e, halo=halo, o0=o0: e.tensor_copy(
                            out=halo, in_=SMIX[:, o0:o0 + 8].rearrange("p (s c) -> p s c", s=4)),
                            reads=["SMIX"], writes=[("UT", t % 2)])
                    wi = (l * 4 + c) * 3
                    S.add('act', lambda e, cur=cur, wi=wi, c=c: e.activation(
                        out=sbv(T1[:, 0:TW]), in_=cur, func=AF.Identity, bias=CB[:, l * 4 + c:l * 4 + c + 1], scale=CW[:, wi + 2:wi + 3]),
                        reads=[("UT", t % 2), "CW", "CB"], writes=["T1"])
                    S.add('dve', lambda e, m1=m1, wi=wi: e.scalar_tensor_tensor(
                        out=sbv(T1[:, 0:TW]), in0=m1, scalar=CW[:, wi + 1:wi + 2], in1=sbv(T1[:, 0:TW]), op0=ALU.mult, op1=ALU.add),
                        reads=[("UT", t % 2), "CW"], writes=["T1"])
                    S.add('dve', lambda e, m2=m2, wi=wi: e.scalar_tensor_tensor(
                        out=sbv(T1[:, 0:TW]), in0=m2, scalar=CW[:, wi:wi + 1], in1=sbv(T1[:, 0:TW]), op0=ALU.mult, op1=ALU.add),
                        reads=[("UT", t % 2), "CW"], writes=["T1"])
                    S.add('dve', lambda e, pgb=pgb, c=c, t=t: e.tensor_tensor(
                        out=sbv(CT[:, c, t * TW:(t + 1) * TW]), in0=psv(pgb), in1=sbv(T1[:, 0:TW]), op=ALU.mult),
                        reads=[P(pgb), "T1"], writes=[("CT", c, t)])
                    if isP and t == NT - 1:
                        S.add('sp', lambda e, ub=ub, c=c: e.dma_start(out=mcd[l, n, c], in_=ub[:, TW:TW + 2]),
                              reads=[("UT", t % 2)], writes=[], dma=True)
                    if not isP:
                        v = ub[:, 0:4 * 34].rearrange("p (s c) -> p s c", s=4)
                        for sq in range(4):
                            S.add('sp', lambda e, v=v, c=c, sq=sq: e.dma_start(out=mcd[l, 4 + sq, c], in_=v[:, sq, 32:34]),
                                  reads=[("UT", t % 2)], writes=[], dma=True)

            S.stage("C")
            conv_rows = None
            for pair in range(4):
                s_qk = load_wa(l, pair)
                s_v = load_wv(l, pair)
                wo_a = load_wr(wo[l, pair])
                if pair == 0:
                    conv_rows = [load_wr(wo[l, 4 + c]) for c in range(4)]
                S.stage("w1")
                for t in range(NT):
                    pb = (t % 2) * 1 + 6
                    proj(pb, s_qk, 128, 128, t, ("WA", s_qk))
                    S.stage("ka")
                    S.stage(f"ka{t}")
                    ks = KST[t % 2]
                    S.add('act', lambda e, pb=pb, ks=ks: e.activation(out=ks[:, 0:TW], in_=PS[pb][:, 0:TW], func=AF.Copy),
                          reads=[P(pb)], writes=[("KST", t % 2)])
                    S.stage("kb")
                    S.stage(f"kb{t}")
                    kdst = (kpd[l, n, pair * 128:(pair + 1) * 128, t * TW:(t + 1) * TW] if isP
                            else ksd[l, pair * 128:(pair + 1) * 128, :])
                    S.add('sp', lambda e, ks=ks, kdst=kdst: e.dma_start(out=kdst, in_=ks[:, 0:TW]),
                          reads=[("KST", t % 2)], writes=[], dma=True)
                    S.stage("kc")
                    S.stage(f"kc{t}")
                    if isP:
                        for hl in range(2):
                            if hl == 1:
                                S.stage("kd")
                                S.stage(f"kd{t}")
                            mkv = os.environ.get("MK_V", "")
                            if hl == 0:
                                tt_ = (3 - t) if mkv == "addr" else t
                                if True:
                                    S.add('dve', lambda e, ks=ks, tt_=tt_: e.tensor_copy(out=KT[0:64, 0, tt_ * TW:(tt_ + 1) * TW], in_=ks[0:64, 0:TW]),
                                          reads=[("KST", t % 2)], writes=[("KT", hl, t)])
                                else:
                                    S.add('dve', lambda e, pb=pb, tt_=tt_: e.tensor_copy(out=KT[0:64, 0, tt_ * TW:(tt_ + 1) * TW], in_=PS[pb][0:64, 0:TW]),
                                          reads=[P(pb)], writes=[("KT", hl, t)])
                            else:
                                S.add('act', lambda e, pb=pb, t=t: e.activation(out=KT[0:64, 1, t * TW:(t + 1) * TW], in_=PS[pb][64:128, 0:TW], func=AF.Copy),
                                      reads=[P(pb)], writes=[("KT", hl, t)])
                    S.stage("ke")
                    S.stage(f"ke{t}")
                S.stage("k1")
                if isP:
                    for b in range(NB):
                        pb = b % 2 + 4
                        for k in range(8):
                            S.add('pe', lambda e, pb=pb, k=k, b=b: e.matmul(
                                PS[pb][:, 0:128], XT[:, k, b * 128:(b + 1) * 128], WV[s_v][:, k, :], start=(k == 0), stop=(k == 7)),
                                reads=[("XT", b), ("WV", s_v)], writes=[P(pb)])
                        vs_ = VST[b % 2]
                        S.add('act', lambda e, pb=pb, vs_=vs_: e.activation(out=vs_[:], in_=PS[pb][:, 0:128], func=AF.Copy),
                              reads=[P(pb)], writes=[("VST", b % 2)])
                        S.add('sp', lambda e, vs_=vs_, b=b: e.dma_start(out=vpd[l, n, b * 128:(b + 1) * 128, pair * 128:(pair + 1) * 128], in_=vs_[:]),
                              reads=[("VST", b % 2)], writes=[], dma=True)
                        S.add('dve', lambda e, pb=pb, b=b: e.tensor_copy(
                            out=VA[:, b, :, 0:64], in_=PS[pb][:, 0:128].rearrange("p (h c) -> p h c", h=2)),
                            reads=[P(pb)], writes=[("VA", b)])
                    S.stage("v1")
                    for hl in range(2):
                        hh = pair * 2 + hl
                        S.add('sp', lambda e, hl=hl, hh=hh: e.dma_start(out=KT[67:70, hl, 0:TT], in_=pkd[hh, :, 0:TT]),
                              reads=["pkd"], writes=[("KTaug", hl)], dma=True)

                S.stage("kv")
                if isP:
                    for t in range(NT):
                        qt = QT[t % 2]
                        qres = ("QT", t % 2)
                        pb = 6 + (t % 2)
                        proj(pb, s_qk, 0, 128, t, ("WA", s_qk))
                        S.add('act', lambda e, pb=pb, qt=qt: e.activation(out=qt[0:64, 0, :], in_=PS[pb][0:64, 0:TW], func=AF.Copy),
                              reads=[P(pb)], writes=[qres])
                        S.add('act', lambda e, pb=pb, qt=qt: e.activation(out=qt[0:64, 1, :], in_=PS[pb][64:128, 0:TW], func=AF.Copy),
                              reads=[P(pb)], writes=[qres])
                        for hl in range(2):
                            hh = pair * 2 + hl
                            S.add('sp', lambda e, hl=hl, hh=hh, qt=qt, t=t: e.dma_start(out=qt[64:67, hl, :], in_=pqd[hh, :, t * TW:(t + 1) * TW]),
                                  reads=["pqd"], writes=[qres], dma=True)
                        at = AT[t % 2]
                        ares = ("AT", t % 2)
                        for hl in range(2):
                            ob = 4 + hl
                            nkb = 4 * t + 4
                            kres = [("KT", hl, tt) for tt in range(t + 1)] + [("KTaug", hl), "KT"]
                            for g in range(nkb // 2):
                                sb0 = (g % 2) * 2
                                ptb = PT[g % 2]
                                pres = ("PT", g % 2)
                                for i in range(2):
                                    kb = g * 2 + i
                                    j = kb - 4 * t
                                    c0 = 0 if j < 0 else j * 128
                                    S.add('pe', lambda e, sbk=sb0 + i, kb=kb, hl=hl, qt=qt, c0=c0, j=j: e.matmul(
                                        PS[sbk][:, c0:TW], KT[0:70, hl, kb * 128:(kb + 1) * 128], qt[0:70, hl, c0:TW],
                                        start=True, stop=(j < 0)),
                                        reads=kres + [qres], writes=[P(sb0 + i)])
                                    if j >= 0:
                                        S.add('pe', lambda e, sbk=sb0 + i, c0=c0: e.matmul(
                                            PS[sbk][:, c0:c0 + 128], IDB[:], MASK[:], start=False, stop=True),
                                            reads=["IDB", "MASK"], writes=[P(sb0 + i)])
                                    S.add('act', lambda e, sbk=sb0 + i, ptb=ptb, i=i, c0=c0: e.activation(
                                        out=ptb[:, i * 512 + c0:i * 512 + TW], in_=PS[sbk][:, c0:TW], func=AF.Exp, scale=0.125),
                                        reads=[P(sb0 + i)], writes=[pres])
                                for i in range(2):
                                    kb = g * 2 + i
                                    j = kb - 4 * t
                                    c0 = 0 if j < 0 else j * 128
                                    S.add('pe', lambda e, ob=ob, kb=kb, hl=hl, ptb=ptb, i=i, c0=c0, nkb=nkb: e.matmul(
                                        PS[ob][:, c0:TW], VA[:, kb, hl, :], ptb[:, i * 512 + c0:i * 512 + TW],
                                        start=(kb == 0), stop=(kb == nkb - 1)),
                                        reads=[pres, ("VA", kb), "VA"], writes=[P(ob)])
                            S.add('dve', lambda e, ob=ob: e.reciprocal(out=RC[64:128, 0:TW], in_=PS[ob][64:128, 0:TW]),
                                  reads=[P(ob)], writes=["RC"])
                            S.add('dve', lambda e, ob=ob, hl=hl, at=at: e.tensor_tensor(
                                out=at[hl * 64:(hl + 1) * 64, 0:TW], in0=PS[ob][0:64, 0:TW], in1=RC[64:128, 0:TW], op=ALU.mult),
                                reads=[P(ob), "RC"], writes=[ares])
                        S.stage("att")
                        for bb in range(TW // 128):
                            b = t * (TW // 128) + bb
                            terms = [(at[:, bb * 128:(bb + 1) * 128], WR[wo_a], [ares, ("WR", wo_a)])]
                            if pair == 0:
                                for c in range(4):
                                    terms.append((CT[:, c, b * 128:(b + 1) * 128], WR[conv_rows[c]], [("CT", c, t), ("WR", conv_rows[c])]))
                            for hf in range(2):
                                pb2 = 6 + hf
                                for ti, (lt, rw, rs) in enumerate(terms):
                                    S.add('pe', lambda e, pb2=pb2, lt=lt, rw=rw, hf=hf, ti=ti, nt_=len(terms): e.matmul(
                                        PS[pb2][:, 0:512], lt, rw[:, hf * 512:(hf + 1) * 512], start=(ti == 0), stop=(ti == nt_ - 1)),
                                        reads=rs, writes=[P(pb2)])
                            x_update(b, pair == 0, (6, 7))
                else:
                    sample_attention(l, pair, s_qk, s_v, wo_a, conv_rows, KT, VA, QT, PT, AT, CT, RC, OS, FA, FBb, FC, PKt, PQt, KC, CL,
                                     decay_chain, proj, x_update)

            S.stage("pairs")
            load_ln(l, 0)
            layer_norm_all()
            S.stage("ln1")

            build_xt()
            S.barrier()
            arena.reset()
            A2 = arena.get(6 * TT, BF16).rearrange("p (c t) -> p c t", c=6)
            GT = [arena.get(TW + 8, F32) for _ in range(2)]
            T2 = [arena.get(512, F32) for _ in range(2)]
            T3 = [arena.get(512, F32) for _ in range(2)]
            for gi, (j0, j1) in enumerate(FFN_GROUPS):
                for j in range(j0, j1):
                    jl = j - j0
                    s_up = load_wa(l, 10 + j)
                    for t in range(NT):
                        base = 2 * ((j * NT + t) % 3)
                        pg, pv = base, base + 1
                        proj(pg, s_up, 0, 128, t, ("WA", s_up))
                        proj(pv, s_up, 128, 128, t, ("WA", s_up))
                        gb_ = GT[t % 2]
                        cur, m1, m2, halo = conv_views(gb_)
                        t2 = T2[t % 2]
                        t3 = T3[t % 2]
                        S.add('act', lambda e, pg=pg, cur=cur: e.activation(out=cur, in_=psv(pg), func=AF.Copy),
                              reads=[P(pg)], writes=[("GT", t % 2)])
                        if isP:
                            if t == 0:
                                S.add('dve', lambda e, gb_=gb_: e.memset(gb_[:, 0:2], 0.0), writes=[("GT", t % 2)])
                            else:
                                pu = GT[(t - 1) % 2]
                                S.add('dve', lambda e, gb_=gb_, pu=pu: e.tensor_copy(out=gb_[:, 0:2], in_=pu[:, TW:TW + 2]),
                                      reads=[("GT", (t - 1) % 2)], writes=[("GT", t % 2)])
                        else:
                            o0 = ((l * NCH + j) * 4) * 2
                            S.add('dve', lambda e, halo=halo, o0=o0: e.tensor_copy(
                                out=halo, in_=SFFN[:, o0:o0 + 8].rearrange("p (s c) -> p s c", s=4)),
                                reads=["SFFN"], writes=[("GT", t % 2)])
                        wi = (l * NCH + j) * 3
                        bi = l * NCH + j
                        S.add('act', lambda e, pg=pg, t2=t2, wi=wi, bi=bi: e.activation(
                            out=sbv(t2[:, 0:TW]), in_=psv(pg), func=AF.Identity, bias=FB[:, bi:bi + 1], scale=FW[:, wi + 2:wi + 3]),
                            reads=[P(pg), "FW", "FB"], writes=[("T2", t % 2)])
                        S.add('dve', lambda e, m1=m1, t2=t2, wi=wi: e.scalar_tensor_tensor(
                            out=sbv(t2[:, 0:TW]), in0=m1, scalar=FW[:, wi + 1:wi + 2], in1=sbv(t2[:, 0:TW]), op0=ALU.mult, op1=ALU.add),
                            reads=[("GT", t % 2), "FW"], writes=[("T2", t % 2)])
                        S.add('dve', lambda e, m2=m2, t2=t2, wi=wi: e.scalar_tensor_tensor(
                            out=sbv(t2[:, 0:TW]), in0=m2, scalar=FW[:, wi:wi + 1], in1=sbv(t2[:, 0:TW]), op0=ALU.mult, op1=ALU.add),
                            reads=[("GT", t % 2), "FW"], writes=[("T2", t % 2)])
                        S.add('act', lambda e, t2=t2, t3=t3: e.activation(out=t3[:, 0:TW], in_=t2[:, 0:TW], func=AF.Silu),
                              reads=[("T2", t % 2)], writes=[("T3", t % 2)])
                        S.add('dve', lambda e, pv=pv, t3=t3, jl=jl, t=t: e.tensor_tensor(
                            out=A2[:, jl, t * TW:(t + 1) * TW], in0=PS[pv][:, 0:TW], in1=t3[:, 0:TW], op=ALU.mult),
                            reads=[P(pv), ("T3", t % 2)], writes=[("A2", jl, t)])
                        if isP and t == NT - 1:
                            S.add('sp', lambda e, gb_=gb_, j=j: e.dma_start(out=fcd[l, n, j], in_=gb_[:, TW:TW + 2]),
                                  reads=[("GT", t % 2)], writes=[], dma=True)
                        if not isP:
                            v = gb_[:, 0:4 * 34].rearrange("p (s c) -> p s c", s=4)
                            for sq in range(4):
                                S.add('sp', lambda e, v=v, j=j, sq=sq: e.dma_start(out=fcd[l, 4 + sq, j], in_=v[:, sq, 32:34]),
                                      reads=[("GT", t % 2)], writes=[], dma=True)
                rows = [load_wr(wdn[l, j]) for j in range(j0, j1)]
                for b in range(NB):
                    t = b * 128 // TW
                    bpair = [(6, 7), (4, 5), (2, 3), (0, 1)][b % 4]
                    for hf in range(2):
                        pb2 = bpair[hf]
                        for ti, j in enumerate(range(j0, j1)):
                            jl = j - j0
                            S.add('pe', lambda e, pb2=pb2, jl=jl, b=b, r=rows[ti], hf=hf, ti=ti, n_=j1 - j0: e.matmul(
                                PS[pb2][:, 0:512], A2[:, jl, b * 128:(b + 1) * 128], WR[r][:, hf * 512:(hf + 1) * 512],
                                start=(ti == 0), stop=(ti == n_ - 1)),
                                reads=[("A2", jl, t), ("WR", rows[ti])], writes=[P(pb2)])
                    x_update(b, gi == 0, bpair)
            load_ln(l, 1)
            layer_norm_all()

        for b in range(NB):
            dst = yp[n, b * 128:(b + 1) * 128, :] if isP else ys
            S.add('sp', lambda e, b=b, d=dst: e.dma_start(out=d, in_=X[:, b, :]), reads=[("X", b)], writes=[], dma=True)

    def sample_attention(l, pair, s_qk, s_v, wo_a, conv_rows, KT, VA, QT, PT, AT, CT, RC, OS, FA, FBb, FC, PKt, PQt, KC, CL,
                         decay_chain, proj, x_update):
        TW = 128
        qt = QT[0]
        proj(6, s_qk, 0, 128, 0, ("WA", s_qk))
        S.add('act', lambda e: e.activation(out=qt[0:64, 0, :], in_=PS[6][0:64, 0:128], func=AF.Copy), reads=[P(6)], writes=["QTs"])
        S.add('act', lambda e: e.activation(out=qt[0:64, 1, :], in_=PS[6][64:128, 0:128], func=AF.Copy), reads=[P(6)], writes=["QTs"])
        at = AT[0]
        for sq in range(4):
            S.add('pool', lambda e, sq=sq: e.dma_start(
                out=KC[:], in_=ck[l, sq, :, pair * 128:(pair + 1) * 128].rearrange("(b p) c -> p b c", p=128)),
                writes=["KC"], dma=True, force_barrier=True)
            for hl in range(2):
                S.add('pool', lambda e, sq=sq, hl=hl: e.dma_start(
                    out=VA[:, 0:16, hl, 0:64],
                    in_=cv[l, sq, :, pair * 128 + hl * 64:pair * 128 + (hl + 1) * 64].rearrange("(b p) c -> p b c", p=128)),
                    writes=[("VAs", hl)], dma=True, force_barrier=True)
            for g in range(4):
                pb = g % 2
                for i in range(4):
                    b = g * 4 + i
                    S.add('pe', lambda e, pb=pb, i=i, b=b: e.matmul(
                        PS[pb][:, i * 128:(i + 1) * 128], KC[:, b, :], IDB[:], start=True, stop=True),
                        reads=["KC", "IDB"], writes=[P(pb)])
                S.add('act', lambda e, pb=pb, g=g: e.activation(out=KT[0:64, 0, g * 512:(g + 1) * 512], in_=PS[pb][0:64, 0:512], func=AF.Copy),
                      reads=[P(pb)], writes=[("KTs", 0)])
                S.add('act', lambda e, pb=pb, g=g: e.activation(out=KT[0:64, 1, g * 512:(g + 1) * 512], in_=PS[pb][64:128, 0:512], func=AF.Copy),
                      reads=[P(pb)], writes=[("KTs", 1)])
            proj(7, s_qk, 128, 128, 0, ("WA", s_qk))
            S.add('act', lambda e, sq=sq: e.activation(out=KT[0:64, 0, T:T + 32], in_=PS[7][0:64, sq * 32:(sq + 1) * 32], func=AF.Copy),
                  reads=[P(7)], writes=[("KTs", 0)])
            S.add('act', lambda e, sq=sq: e.activation(out=KT[0:64, 1, T:T + 32], in_=PS[7][64:128, sq * 32:(sq + 1) * 32], func=AF.Copy),
                  reads=[P(7)], writes=[("KTs", 1)])
            for k in range(8):
                S.add('pe', lambda e, k=k, sq=sq: e.matmul(
                    PS[5][0:32, 0:128], XT[:, k, sq * 32:(sq + 1) * 32], WV[s_v][:, k, :], start=(k == 0), stop=(k == 7)),
                    reads=[("XT", 0), ("WV", s_v)], writes=[P(5)])
            vs_ = VST[sq % 2]
            S.add('act', lambda e, vs_=vs_: e.activation(out=vs_[0:32, :], in_=PS[5][0:32, 0:128], func=AF.Copy),
                  reads=[P(5)], writes=[("VST", sq % 2)])
            S.add('sp', lambda e, vs_=vs_, sq=sq: e.dma_start(out=vsd[l, sq, :, pair * 128:(pair + 1) * 128], in_=vs_[0:32, :]),
                  reads=[("VST", sq % 2)], writes=[], dma=True)
            S.add('act', lambda e: e.activation(out=VA[0:32, 16, :, 0:64], in_=PS[5][0:32, 0:128].rearrange("p (h c) -> p h c", h=2), func=AF.Copy),
                  reads=[P(5)], writes=[("VAs", 2)])
            for hl in range(2):
                hh = pair * 2 + hl
                S.add('sp', lambda e, hl=hl, hh=hh, sq=sq: e.dma_start(out=KT[67:70, hl, 0:T + 32], in_=pks[sq, hh, :, :]),
                      reads=["pkd"], writes=[("KTs", hl)], dma=True)
                S.add('sp', lambda e, hl=hl, hh=hh, sq=sq: e.dma_start(out=qt[64:67, hl, sq * 32:(sq + 1) * 32], in_=pqs[sq, hh, :, :]),
                      reads=["pqd"], writes=["QTs"], dma=True)
            for hl in range(2):
                ob = 4
                q_ap = qt[0:70, hl, sq * 32:(sq + 1) * 32]
                for b in range(16):
                    S.add('pe', lambda e, b=b, hl=hl, q_ap=q_ap: e.matmul(
                        PS[2][:, b * 32:(b + 1) * 32], KT[0:70, hl, b * 128:(b + 1) * 128], q_ap, start=True, stop=True),
                        reads=[("KTs", hl), "QTs", "KT"], writes=[P(2)])
                S.add('pe', lambda e, hl=hl, q_ap=q_ap: e.matmul(
                    PS[3][0:32, 0:32], KT[0:70, hl, T:T + 32], q_ap, start=True, stop=False),
                    reads=[("KTs", hl), "QTs", "KT"], writes=[P(3)])
                S.add('pe', lambda e: e.matmul(PS[3][0:32, 0:32], IDB[0:32, 0:32], MASK[0:32, 0:32], start=False, stop=True),
                      reads=["IDB", "MASK"], writes=[P(3)])
                S.add('act', lambda e: e.activation(out=PT[0][:, 0:512], in_=PS[2][:, 0:512], func=AF.Exp, scale=0.125),
                      reads=[P(2)], writes=[("PT", 0)])
                S.add('act', lambda e: e.activation(out=PT[1][0:32, 0:32], in_=PS[3][0:32, 0:32], func=AF.Exp, scale=0.125),
                      reads=[P(3)], writes=[("PT", 1)])
                for b in range(16):
                    S.add('pe', lambda e, b=b, hl=hl: e.matmul(
                        PS[ob][:, 0:32], VA[:, b, hl, :], PT[0][:, b * 32:(b + 1) * 32], start=(b == 0), stop=False),
                        reads=[("PT", 0), ("VAs", hl), "VA"], writes=[P(ob)])
                S.add('pe', lambda e, hl=hl: e.matmul(
                    PS[ob][:, 0:32], VA[0:32, 16, hl, :], PT[1][0:32, 0:32], start=False, stop=True),
                    reads=[("PT", 1), ("VAs", 2), "VA"], writes=[P(ob)])
                S.add('dve', lambda e: e.reciprocal(out=RC[64:128, 0:32], in_=PS[ob][64:128, 0:32]), reads=[P(ob)], writes=["RC"])
                S.add('dve', lambda e, hl=hl, sq=sq: e.tensor_tensor(
                    out=at[hl * 64:(hl + 1) * 64, sq * 32:(sq + 1) * 32], in0=PS[ob][0:64, 0:32], in1=RC[64:128, 0:32], op=ALU.mult),
                    reads=[P(ob), "RC"], writes=[("AT", 0)])
        terms = [(at[:, 0:128], WR[wo_a], [("AT", 0), ("WR", wo_a)])]
        if pair == 0:
            for c in range(4):
                terms.append((CT[:, c, 0:128], WR[conv_rows[c]], [("CT", c, 0), ("WR", conv_rows[c])]))
        for hf in range(2):
            pb2 = 6 + hf
            for ti, (lt, rw, rs) in enumerate(terms):
                S.add('pe', lambda e, pb2=pb2, lt=lt, rw=rw, hf=hf, ti=ti, nt_=len(terms): e.matmul(
                    PS[pb2][:, 0:512], lt, rw[:, hf * 512:(hf + 1) * 512], start=(ti == 0), stop=(ti == nt_ - 1)),
                    reads=rs, writes=[P(pb2)])
        x_update(0, pair == 0, (6, 7))

    for n in range(nps):
        S.phase = n
        run_sequence('p', n)
    if do_sample:
        S.phase = 4
        run_sequence('s', 0)

    sem_keys = S.finalize()
    with ExitStack() as es:
        sems = {k: es.enter_context(nc.semaphore("s_" + "_".join(str(x) for x in k))) for k in sem_keys}
        block = es.enter_context(nc.Block())

        @block.tensor
        def _(e):
            S.emit(e, 'pe', sems)

        @block.scalar
        def _(e):
            S.emit(e, 'act', sems)

        @block.vector
        def _(e):
            S.emit(e, 'dve', sems)

        @block.gpsimd
        def _(e):
            S.emit(e, 'pool', sems)

        @block.sync
        def _(e):
            S.emit(e, 'sp', sems)
    return nc


def _prep_weights(w_in, b_f, conv_w, conv_b, w_out, ln1_g, ln1_b, w_up, ffn_conv_w, ffn_conv_b, w_down, ln2_g, ln2_b):
    f = np.float32
    w_in = np.asarray(w_in, f)
    w_up = np.asarray(w_up, f)

    def unit(mat):
        C = mat.shape[1]
        return mat.reshape(8, 128, C).transpose(1, 0, 2)

    wa = np.empty((L, 32, 128, 8, 256), f)
    wv = np.empty((L, 4, 128, 8, 128), f)
    wf = np.empty((L, 128, 8, 8), f)
    for l in range(L):
        W = w_in[l]
        q, k, v = W[:, 0:512], W[:, 512:1024], W[:, 1024:1536]
        fl = W[:, 1536:1544]
        gb, gc, hh = W[:, 1544:2056], W[:, 2056:2568], W[:, 2568:3080]
        for p in range(4):
            wa[l, p, :, :, 0:128] = unit(q[:, p * 128:(p + 1) * 128])
            wa[l, p, :, :, 128:256] = unit(k[:, p * 128:(p + 1) * 128])
            wv[l, p] = unit(v[:, p * 128:(p + 1) * 128])
        for c in range(4):
            wa[l, 4 + c, :, :, 0:128] = unit(gb[:, c * 128:(c + 1) * 128])
            wa[l, 4 + c, :, :, 128:256] = unit(gc[:, c * 128:(c + 1) * 128])
        for c2 in range(2):
            wa[l, 8 + c2, :, :, 0:128] = unit(hh[:, (2 * c2) * 128:(2 * c2 + 1) * 128])
            wa[l, 8 + c2, :, :, 128:256] = unit(hh[:, (2 * c2 + 1) * 128:(2 * c2 + 2) * 128])
        for j in range(NCH):
            wa[l, 10 + j, :, :, 0:128] = unit(w_up[l][:, j * 128:(j + 1) * 128])
            wa[l, 10 + j, :, :, 128:256] = unit(w_up[l][:, DFF + j * 128:DFF + (j + 1) * 128])
        wf[l] = unit(fl)
    wo = np.ascontiguousarray(np.asarray(w_out, f).reshape(L, 8, 128, D))
    wdn = np.ascontiguousarray(np.asarray(w_down, f).reshape(L, NCH, 128, D))
    lnrep = np.empty((L, 2, 128, 2 * D), f)
    lnrep[:, 0, :, 0:D] = np.asarray(ln1_g, f)[:, None, :]
    lnrep[:, 0, :, D:] = np.asarray(ln1_b, f)[:, None, :]
    lnrep[:, 1, :, 0:D] = np.asarray(ln2_g, f)[:, None, :]
    lnrep[:, 1, :, D:] = np.asarray(ln2_b, f)[:, None, :]
    cwd = np.ascontiguousarray(np.asarray(conv_w, f).reshape(L, 3, 4, 128).transpose(3, 0, 2, 1)).reshape(128, L * 4 * 3)
    cbd = np.ascontiguousarray(np.asarray(conv_b, f).reshape(L, 4, 128).transpose(2, 0, 1)).reshape(128, L * 4)
    fwd = np.ascontiguousarray(np.asarray(ffn_conv_w, f).reshape(L, 3, NCH, 128).transpose(3, 0, 2, 1)).reshape(128, L * NCH * 3)
    fbd = np.ascontiguousarray(np.asarray(ffn_conv_b, f).reshape(L, NCH, 128).transpose(2, 0, 1)).reshape(128, L * NCH)
    bfd = np.ascontiguousarray(np.asarray(b_f, f).T)
    idfd = np.eye(128, dtype=f)
    idbd = np.eye(128).astype(ml_dtypes.bfloat16)
    kk = np.arange(128)[:, None]
    qq = np.arange(128)[None, :]
    maskd = np.where(kk <= qq, 0.0, NEG).astype(ml_dtypes.bfloat16)
    return dict(wa=wa.reshape(L, 32, 128, 8 * 256), wv=wv.reshape(L, 4, 128, 8 * 128), wf=wf.reshape(L, 128, 64), wo=wo, wdn=wdn,
                lnrep=lnrep, cwd=cwd, cbd=cbd, fwd=fwd, fbd=fbd, bfd=bfd, idfd=idfd, idbd=idbd, maskd=maskd)


_NC_CACHE = {}


def kernel(x_prompt, x_sample, cache_k, cache_v, cache_logf, state_mix_conv, state_ffn_conv,
           w_in, b_f, conv_w, conv_b, w_out, ln1_g, ln1_b, w_up, ffn_conv_w, ffn_conv_b, w_down, ln2_g, ln2_b):
    f = np.float32
    nps = int(os.environ.get("MK_NPS", NPS))
    nl = int(os.environ.get("MK_NL", L))
    do_s = int(os.environ.get("MK_SAMPLE", 1)) == 1
    ncores = 8
    wd = _prep_weights(w_in, b_f, conv_w, conv_b, w_out, ln1_g, ln1_b, w_up, ffn_conv_w, ffn_conv_b, w_down, ln2_g, ln2_b)
    x_prompt = np.asarray(x_prompt, f)
    x_sample = np.asarray(x_sample, f)
    cache_k = np.asarray(cache_k, f)
    cache_v = np.asarray(cache_v, f)
    cache_logf = np.asarray(cache_logf, f)
    smc = np.asarray(state_mix_conv, f)
    sfc = np.asarray(state_ffn_conv, f)
    in_maps = []
    for c in range(ncores):
        sl = slice(c * 4, (c + 1) * 4)
        m = dict(wd)
        m["xp"] = np.ascontiguousarray(x_prompt[sl])
        m["xs"] = np.ascontiguousarray(x_sample[sl].reshape(128, D))
        m["ck"] = np.ascontiguousarray(cache_k[:, sl].reshape(L, 4, T, 512))
        m["cv"] = np.ascontiguousarray(cache_v[:, sl].reshape(L, 4, T, 512))
        m["clf"] = np.ascontiguousarray(cache_logf[:, sl])
        m["smix"] = np.ascontiguousarray(smc[:, sl].reshape(L, 4, 2, 4, 128).transpose(4, 0, 3, 1, 2)).reshape(128, -1)
        m["sffn"] = np.ascontiguousarray(sfc[:, sl].reshape(L, 4, 2, NCH, 128).transpose(4, 0, 3, 1, 2)).reshape(128, -1)
        in_maps.append(m)
    key = (nps, nl, do_s)
    if key not in _NC_CACHE:
        _NC_CACHE[key] = build_program(nps, nl, do_s)
    nc = _NC_CACHE[key]
    res = run_bass_kernel_spmd(nc, in_maps, core_ids=list(range(ncores)))
    R = res.results
    B = 32
    y_prompt = np.empty((B, T, D), f)
    y_sample = np.empty((B, 32, D), f)
    k_prompt = np.empty((L, B, T, H, 64), f)
    v_prompt = np.empty((L, B, T, H, 64), f)
    logf_prompt = np.empty((L, B, T, H), f)
    mix_p = np.empty((L, B, 2, 512), f)
    ffn_p = np.empty((L, B, 2, DFF), f)
    k_sample = np.empty((L, B, 32, H, 64), f)
    v_sample = np.empty((L, B, 32, H, 64), f)
    logf_sample = np.empty((L, B, 32, H), f)
    mix_s = np.empty((L, B, 2, 512), f)
    ffn_s = np.empty((L, B, 2, DFF), f)
    for c in range(ncores):
        r = R[c]
        sl = slice(c * 4, (c + 1) * 4)
        y_prompt[sl] = r["yp"]
        y_sample[sl] = r["ys"].reshape(4, 32, D)
        k_prompt[:, sl] = r["kpd"].reshape(L, 4, H, 64, T).transpose(0, 1, 4, 2, 3)
        v_prompt[:, sl] = r["vpd"].reshape(L, 4, T, H, 64)
        logf_prompt[:, sl] = r["lfpd"].transpose(0, 1, 3, 2)
        mc = r["mcd"]
        fc = r["fcd"]
        mix_p[:, sl] = mc[:, 0:4].transpose(0, 1, 4, 2, 3).reshape(L, 4, 2, 512)
        mix_s[:, sl] = mc[:, 4:8].transpose(0, 1, 4, 2, 3).reshape(L, 4, 2, 512)
        ffn_p[:, sl] = fc[:, 0:4].transpose(0, 1, 4, 2, 3).reshape(L, 4, 2, DFF)
        ffn_s[:, sl] = fc[:, 4:8].transpose(0, 1, 4, 2, 3).reshape(L, 4, 2, DFF)
        k_sample[:, sl] = r["ksd"].reshape(L, H, 64, 4, 32).transpose(0, 3, 4, 1, 2)
        v_sample[:, sl] = r["vsd"].reshape(L, 4, 32, H, 64)
        logf_sample[:, sl] = r["lfsd"].reshape(L, H, 4, 32).transpose(0, 2, 3, 1)
    return (y_prompt, y_sample, k_prompt, v_prompt, logf_prompt, mix_p, ffn_p,
            k_sample, v_sample, logf_sample, mix_s, ffn_s)
```

```python
import os
import types
import numpy as np
import ml_dtypes
from contextlib import ExitStack
import concourse.bass as bass
import concourse.mybir as mybir
from concourse.bass_utils import run_bass_kernel_spmd

F32 = mybir.dt.float32
BF16 = mybir.dt.bfloat16
U8 = mybir.dt.uint8
AF = mybir.ActivationFunctionType
ALU = mybir.AluOpType

L = 4
D = 1024
H = 8
T = 2048
DFF = 2816
NCH = 22
NPS = 4
ALPHA = float(8 ** 0.25)
EPS = 1e-5
NEG = -30000.0
NS = 8
FFN_GROUPS = [(0, 6), (6, 12), (12, 17), (17, 22)]


def _freeze(fn):
    if fn is None or fn.__closure__ is None:
        return fn
    cells = []
    for c in fn.__closure__:
        try:
            cells.append(types.CellType(c.cell_contents))
        except ValueError:
            cells.append(c)
    return types.FunctionType(fn.__code__, fn.__globals__, fn.__name__, fn.__defaults__, tuple(cells))


class Sched:
    def __init__(self):
        self.ops = []
        self.last_w = {}
        self.readers = {}
        self.phase = 0
        self.dma_rr = {'sp': 0, 'pool': 0}
        self.dma_last = {}
        self.eng_last = {}
        self.barrier_deps = None
        self.barrier_seen = set()

    def stage(self, name):
        if os.environ.get("MK_STOP", "") == name:
            self.stopped = True

    def add(self, eng, fn, reads=(), writes=(), dma=False, force_barrier=False):
        if getattr(self, 'stopped', False):
            return -1
        i = len(self.ops)
        ps_reads = [r for r in reads if isinstance(r, tuple) and r[0] == 'ps']
        if ps_reads:
            reads = [r for r in reads if not (isinstance(r, tuple) and r[0] == 'ps')]
            writes = list(writes) + ps_reads
        deps = set()
        for r in reads:
            w = self.last_w.get(r)
            if w is not None:
                deps.add(w)
        for w_ in writes:
            w = self.last_w.get(w_)
            if w is not None:
                deps.add(w)
            deps.update(self.readers.get(w_, {}).values())
        op = dict(eng=eng, fn=_freeze(fn), deps=deps, dma=dma, phase=self.phase, signal=dma, sig=None)
        if dma:
            k = self.dma_rr[eng]
            self.dma_rr[eng] += 1
            s = ('d', eng, k % NS)
            prev = self.dma_last.get(s)
            if prev is not None:
                deps.add(prev)
            self.dma_last[s] = i
            op['dsem'] = s
            rkey = s
        else:
            rkey = eng
        if self.barrier_deps is not None and force_barrier:
            deps.update(self.barrier_deps)
        elif self.barrier_deps is not None and eng != 'pool' and eng not in self.barrier_seen:
            deps.update(self.barrier_deps)
            self.barrier_seen.add(eng)
        deps.discard(i)
        for r in reads:
            self.readers.setdefault(r, {})[rkey] = i
        for w_ in writes:
            self.last_w[w_] = i
            self.readers[w_] = {}
        self.eng_last[eng] = i
        self.ops.append(op)
        return i

    def barrier(self):
        deps = set()
        for eng in ('pe', 'act', 'dve'):
            if eng in self.eng_last:
                deps.add(self.eng_last[eng])
        for s, i in self.dma_last.items():
            if s[1] == 'sp':
                deps.add(i)
        self.barrier_deps = deps
        self.barrier_seen = set()

    def finalize(self):
        self.ops.append(dict(eng='sp', fn=None, deps=set(self.dma_last.values()), dma=False,
                             phase=self.phase, signal=False, sig=None))
        for op in self.ops:
            for d in op['deps']:
                p = self.ops[d]
                if op['eng'] == 'pe' and p['eng'] == 'pe' and not p['dma']:
                    continue
                p['signal'] = True
        cnt = {}
        for op in self.ops:
            if op['dma']:
                s = op['dsem']
                cnt[s] = cnt.get(s, 0) + 1
                op['sig'] = (s, 16 * cnt[s])
            elif op['signal']:
                s = (op['eng'], op['phase'])
                cnt[s] = cnt.get(s, 0) + 1
                op['sig'] = (s, cnt[s])
        return sorted(set(op['sig'][0] for op in self.ops if op['sig'] is not None), key=str)

    def emit(self, e, eng, sems):
        waited = {}
        ops = self.ops
        for op in ops:
            if op['eng'] != eng:
                continue
            need = {}
            for d in op['deps']:
                p = ops[d]
                if eng == 'pe' and p['eng'] == 'pe' and not p['dma']:
                    continue
                if p['sig'] is None:
                    raise RuntimeError("dep on non-signaling op")
                s, v = p['sig']
                if need.get(s, 0) < v:
                    need[s] = v
            for s, v in need.items():
                if waited.get(s, 0) < v:
                    e.wait_ge(sems[s], v)
                    waited[s] = v
            if op['fn'] is not None:
                ins = op['fn'](e)
                if op['sig'] is not None:
                    ins.then_inc(sems[op['sig'][0]], 16 if op['dma'] else 1)


def build_program(nps=NPS, nlayers=L, do_sample=True):
    nc = bass.Bass("TRN2", target_bir_lowering=False)
    S = Sched()

    def din(name, shape, dt=F32):
        return nc.dram_tensor(name, list(shape), dt, kind="ExternalInput").ap()

    def dout(name, shape, dt=F32):
        return nc.dram_tensor(name, list(shape), dt, kind="ExternalOutput").ap()

    xp = din("xp", [NPS, T, D])
    xs = din("xs", [128, D])
    ck = din("ck", [L, 4, T, 512])
    cv = din("cv", [L, 4, T, 512])
    clf = din("clf", [L, 4, T, 8])
    smix = din("smix", [128, L * 4 * 4 * 2])
    sffn = din("sffn", [128, L * NCH * 4 * 2])
    wa = din("wa", [L, 32, 128, 8 * 256])
    wv = din("wv", [L, 4, 128, 8 * 128])
    wf = din("wf", [L, 128, 8 * 8])
    wo = din("wo", [L, 8, 128, D])
    wdn = din("wdn", [L, NCH, 128, D])
    lnrep = din("lnrep", [L, 2, 128, 2 * D])
    cwd = din("cwd", [128, L * 4 * 3])
    cbd = din("cbd", [128, L * 4])
    fwd = din("fwd", [128, L * NCH * 3])
    fbd = din("fbd", [128, L * NCH])
    bfd = din("bfd", [8, L])
    idfd = din("idfd", [128, 128])
    idbd = din("idbd", [128, 128], BF16)
    maskd = din("maskd", [128, 128], BF16)

    yp = dout("yp", [NPS, T, D])
    ys = dout("ys", [128, D])
    kpd = dout("kpd", [L, NPS, 512, T])
    vpd = dout("vpd", [L, NPS, T, 512])
    lfpd = dout("lfpd", [L, NPS, 8, T])
    mcd = dout("mcd", [L, 8, 4, 128, 2])
    fcd = dout("fcd", [L, 8, NCH, 128, 2])
    ksd = dout("ksd", [L, 512, 128])
    vsd = dout("vsd", [L, 4, 32, 512])
    lfsd = dout("lfsd", [L, 8, 128])
    pkd = nc.dram_tensor("pkd", [8, 3, T], BF16).ap()
    pqd = nc.dram_tensor("pqd", [8, 3, T], BF16).ap()
    pks = nc.dram_tensor("pks", [4, 8, 3, T + 32], BF16).ap()
    pqs = nc.dram_tensor("pqs", [4, 8, 3, 32], BF16).ap()

    def sb(name, shape, dt):
        return nc.alloc_sbuf_tensor(name, list(shape), dt)

    X = sb("X", [128, 16, D], F32)
    XT = sb("XT", [128, 8, T], BF16)
    IDF = sb("IDF", [128, 128], F32)
    IDB = sb("IDB", [128, 128], BF16)
    MASK = sb("MASK", [128, 128], BF16)
    ONES = sb("ONES", [8, 512], F32)
    CW = sb("CW", [128, L * 4 * 3], F32)
    CB = sb("CB", [128, L * 4], F32)
    FW = sb("FW", [128, L * NCH * 3], F32)
    FB = sb("FB", [128, L * NCH], F32)
    BFs = sb("BFs", [8, L], F32)
    NBF = sb("NBF", [8, L], F32)
    SMIX = sb("SMIX", [128, L * 4 * 4 * 2], F32)
    SFFN = sb("SFFN", [128, L * NCH * 4 * 2], F32)
    EPSC = sb("EPSC", [128, 1], F32)
    NWA = 4
    WA = [sb(f"WA{i}", [128, 8, 256], BF16) for i in range(NWA)]
    WV = [sb(f"WV{i}", [128, 8, 128], BF16) for i in range(2)]
    WF = sb("WF", [128, 8, 8], BF16)
    NWR = 6
    WR = [sb(f"WR{i}", [128, D], BF16) for i in range(NWR)]
    GBt = sb("GBt", [128, 2 * D], F32)
    KST = [sb(f"KST{i}", [128, 512], F32) for i in range(2)]
    VST = [sb(f"VST{i}", [128, 128], F32) for i in range(2)]
    SST = [sb(f"SST{i}", [128, 2], F32) for i in range(4)]
    LNS = sb("LNS", [128, 16, 16], F32)
    LNR = sb("LNR", [128, 16, 4], F32)
    ARENA_BYTES = 55 * 1024
    AR = sb("AR", [128, ARENA_BYTES], U8)

    class Arena:
        def __init__(self):
            self.off = 0

        def reset(self):
            self.off = 0

        def get(self, nfree, dt):
            nbytes = nfree * (2 if dt == BF16 else 4)
            nbytes = (nbytes + 31) // 32 * 32
            v = AR[:, self.off:self.off + nbytes].bitcast(dt)
            self.off += nbytes
            assert self.off <= ARENA_BYTES, self.off
            return v

    arena = Arena()
    PS = [nc.alloc_psum_tensor(f"ps{i}", [128, 512], F32) for i in range(8)]

    def P(i):
        return ("ps", i)

    wa_rr = [0]
    wr_rr = [0]
    wv_rr = [0]

    def load_wa(l, unit):
        slot = wa_rr[0] % NWA
        wa_rr[0] += 1
        S.add('pool', lambda e, s=slot, l=l, u=unit: e.dma_start(
            out=WA[s][:].rearrange("p k c -> p (k c)"), in_=wa[l, u]), writes=[("WA", slot)], dma=True)
        return slot

    def load_wv(l, pair):
        slot = wv_rr[0] % 2
        wv_rr[0] += 1
        S.add('pool', lambda e, s=slot, l=l, u=pair: e.dma_start(
            out=WV[s][:].rearrange("p k c -> p (k c)"), in_=wv[l, u]), writes=[("WV", slot)], dma=True)
        return slot

    def load_wr(src_ap):
        slot = wr_rr[0] % NWR
        wr_rr[0] += 1
        S.add('pool', lambda e, s=slot, a=src_ap: e.dma_start(out=WR[s][:], in_=a), writes=[("WR", slot)], dma=True)
        return slot

    def load_ln(l, which):
        S.add('sp', lambda e, l=l, w=which: e.dma_start(out=GBt[:], in_=lnrep[l, w]), writes=["GBt"], dma=True)

    for dst, src, nm in [(IDF, idfd, "IDF"), (IDB, idbd, "IDB"), (MASK, maskd, "MASK"), (CW, cwd, "CW"), (CB, cbd, "CB"),
                         (FW, fwd, "FW"), (FB, fbd, "FB"), (BFs, bfd, "BFs"), (SMIX, smix, "SMIX"), (SFFN, sffn, "SFFN")]:
        S.add('sp', lambda e, d=dst, s=src: e.dma_start(out=d[:], in_=s), writes=[nm], dma=True)
    S.add('dve', lambda e: e.memset(ONES[:], 1.0), writes=["ONES"])
    S.add('dve', lambda e: e.memset(EPSC[:], EPS), writes=["EPSC"])
    S.add('dve', lambda e: e.tensor_scalar(out=NBF[:], in0=BFs[:], scalar1=-1.0, scalar2=None, op0=ALU.mult),
          reads=["BFs"], writes=["NBF"])

    def run_sequence(kind, n):
        isP = kind == 'p'
        TT = T if isP else 128
        TW = 512 if isP else 128
        NT = TT // TW
        NB = TT // 128
        sidx = n if isP else None

        for b in range(NB):
            src = xp[n, b * 128:(b + 1) * 128, :] if isP else xs
            S.add('sp', lambda e, b=b, s=src: e.dma_start(out=X[:, b, :], in_=s), writes=[("X", b)], dma=True)

        def build_xt():
            bank = [0]
            for b in range(NB):
                for half in range(2):
                    pb = bank[0] % 8
                    bank[0] += 1
                    for i in range(4):
                        k = half * 4 + i
                        S.add('pe', lambda e, pb=pb, i=i, b=b, k=k: e.transpose(
                            PS[pb][:, i * 128:(i + 1) * 128], X[:, b, k * 128:(k + 1) * 128], IDF[:]),
                            reads=[("X", b), "IDF"], writes=[P(pb)])
                    S.add('act', lambda e, pb=pb, b=b, half=half: e.activation(
                        out=XT[:, half * 4:(half + 1) * 4, b * 128:(b + 1) * 128],
                        in_=PS[pb][:].rearrange("p (a c) -> p a c", a=4), func=AF.Copy),
                        reads=[P(pb)], writes=[("XT", b)])

        def xt_res(t):
            return [("XT", b) for b in range(t * TW // 128, (t + 1) * TW // 128)]

        def proj(ps_i, wslot, col0, ncols, t, wres, wt=None):
            W = WA[wslot] if wt is None else wt
            for k in range(8):
                S.add('pe', lambda e, k=k, W=W: e.matmul(
                    PS[ps_i][0:ncols, 0:TW], W[:, k, col0:col0 + ncols], XT[:, k, t * TW:(t + 1) * TW],
                    start=(k == 0), stop=(k == 7)),
                    reads=[wres] + xt_res(t), writes=[P(ps_i)])

        def x_update(b, first, banks):
            for hf in range(2):
                xs_ = X[:, b, hf * 512:(hf + 1) * 512]
                pb = banks[hf]
                if first:
                    S.add('dve', lambda e, xs_=xs_, pb=pb: e.scalar_tensor_tensor(
                        out=xs_, in0=xs_, scalar=ALPHA, in1=PS[pb][:, 0:512], op0=ALU.mult, op1=ALU.add),
                        reads=[P(pb)], writes=[("X", b)])
                else:
                    S.add('dve', lambda e, xs_=xs_, pb=pb: e.tensor_tensor(
                        out=xs_, in0=xs_, in1=PS[pb][:, 0:512], op=ALU.add),
                        reads=[P(pb)], writes=[("X", b)])

        def layer_norm_all():
            GRP = 4
            for g0 in range(0, NB, GRP):
                blks = list(range(g0, min(NB, g0 + GRP)))
                for b in blks:
                    for hf in range(2):
                        S.add('dve', lambda e, hf=hf, b=b: e.bn_stats(out=LNS[:, b, hf * 6:(hf + 1) * 6], in_=X[:, b, hf * 512:(hf + 1) * 512]),
                              reads=[("X", b)], writes=[("LNS", b)])
                    S.add('dve', lambda e, b=b: e.bn_aggr(out=LNS[:, b, 12:14], in_=LNS[:, b, 0:12]), reads=[("LNS", b)], writes=[("LNS", b)])
                for b in blks:
                    S.add('act', lambda e, b=b: e.activation(out=LNR[:, b, 0:1], in_=LNS[:, b, 13:14], func=AF.Sqrt, bias=EPSC[:, 0:1], scale=1.0),
                          reads=[("LNS", b), "EPSC"], writes=[("LNR", b)])
                for b in blks:
                    S.add('dve', lambda e, b=b: e.reciprocal(out=LNR[:, b, 1:2], in_=LNR[:, b, 0:1]), reads=[("LNR", b)], writes=[("LNR", b)])
                for b in blks:
                    S.add('dve', lambda e, b=b: e.scalar_tensor_tensor(out=X[:, b, :], in0=X[:, b, :], scalar=LNS[:, b, 12:13], in1=GBt[:, 0:D],
                                                                      op0=ALU.subtract, op1=ALU.mult),
                          reads=["GBt", ("LNS", b)], writes=[("X", b)])
                    S.add('dve', lambda e, b=b: e.scalar_tensor_tensor(out=X[:, b, :], in0=X[:, b, :], scalar=LNR[:, b, 1:2], in1=GBt[:, D:2 * D],
                                                                      op0=ALU.mult, op1=ALU.add),
                          reads=["GBt", ("LNR", b)], writes=[("X", b)])

        def seg(ap, w):
            return ap

        for l in range(nlayers):
            build_xt()
            S.stage("xt")
            S.barrier()
            arena.reset()
            KT = arena.get(2 * (T if isP else T + 32), BF16).rearrange("p (h t) -> p h t", h=2)
            NVB = NB if isP else 17
            VA = arena.get(NVB * 2 * 128, BF16).rearrange("p (b h c) -> p b h c", b=NVB, h=2)
            QT = [arena.get(2 * TW, BF16).rearrange("p (h t) -> p h t", h=2) for _ in range(2)]
            PT = [arena.get(1024, BF16) for _ in range(2)]
            AT = [arena.get(TW, BF16) for _ in range(2)]
            ct_off = arena.off
            CT = arena.get(4 * TT, BF16).rearrange("p (c t) -> p c t", c=4)
            UT = [arena.get(TW + 8, F32) for _ in range(2)]
            HS = arena.get(512, F32)
            T1 = arena.get(512, F32)
            RC = arena.get(512, F32)
            OS = arena.get(512, F32)
            end_off = arena.off
            if isP:
                arena.off = ct_off
            FA = arena.get(512, F32)
            FBb = [arena.get(512, F32) for _ in range(2)]
            FC = arena.get(512, F32)
            PKt = arena.get(3 * 512, BF16).rearrange("p (j t) -> p j t", j=3)
            PQt = arena.get(3 * 512, BF16).rearrange("p (j t) -> p j t", j=3)
            if isP:
                arena.off = end_off
            KC = CL = LPS = None
            if not isP:
                KC = arena.get(16 * 128, BF16).rearrange("p (b c) -> p b c", b=16)
                CL = arena.get(16 * 8, F32).rearrange("p (b c) -> p b c", b=16)
                LPS = arena.get(128, F32)

            S.add('dve', lambda e, KT=KT: e.memset(KT[64:70, :, :], 1.0),
                  writes=["KT", ("KTaug", 0), ("KTaug", 1), ("KTs", 0), ("KTs", 1)])
            for qi, q in enumerate(QT):
                S.add('dve', lambda e, q=q: e.memset(q[64:70, :, :], 1.0), writes=[("QT", qi), "QTs"])
            S.add('dve', lambda e, VA=VA: e.memset(VA[:, :, :, 64:128], 1.0), writes=["VA"])

            S.add('pool', lambda e, l=l: e.dma_start(out=WF[:].rearrange("p k c -> p (k c)"), in_=wf[l]), writes=["WF"], dma=True)

            def decay_chain(src_fa_ready_res, width, carry, slot, kdst, qdst):
                w = width
                cur = FBb[slot]
                ini = 0.0 if carry is None else carry
                S.add('dve', lambda e, cur=cur, ini=ini, w=w: e.tensor_tensor_scan(
                    out=cur[0:8, 0:w], data0=ONES[0:8, 0:w], data1=FA[0:8, 0:w], initial=ini, op0=ALU.mult, op1=ALU.add),
                    reads=["FA", "ONES", ("FB", 1 - slot)], writes=[("FB", slot)])
                S.add('dve', lambda e, cur=cur, w=w: e.tensor_scalar(out=FC[0:8, 0:w], in0=cur[0:8, 0:w], scalar1=8.0, scalar2=None, op0=ALU.mult),
                      reads=[("FB", slot)], writes=["FC"])
                for j in range(3):
                    S.add('dve', lambda e, j=j, w=w: e.tensor_copy(out=PKt[0:8, j, 0:w], in_=FC[0:8, 0:w]), reads=["FC"], writes=["PKt"])
                    if j < 2:
                        S.add('dve', lambda e, j=j, w=w: e.tensor_tensor(out=FC[0:8, 0:w], in0=FC[0:8, 0:w], in1=PKt[0:8, j, 0:w], op=ALU.subtract),
                              reads=["PKt"], writes=["FC"])
                S.add('sp', lambda e, w=w, kdst=kdst: e.dma_start(out=kdst, in_=PKt[0:8, :, 0:w]), reads=["PKt"], writes=["pkd"], dma=True)
                if qdst is not None:
                    S.add('dve', lambda e, w=w: e.tensor_scalar(out=PQt[0:8, :, 0:w], in0=PKt[0:8, :, 0:w], scalar1=-1.0, scalar2=None, op0=ALU.mult),
                          reads=["PKt"], writes=["PQt"])
                    S.add('sp', lambda e, w=w, qdst=qdst: e.dma_start(out=qdst, in_=PQt[0:8, :, 0:w]), reads=["PQt"], writes=["pqd"], dma=True)
                return cur[0:8, w - 1:w]

            def lp_from_psum(pb, w, outdst):
                S.add('act', lambda e, pb=pb, w=w: e.activation(out=FA[0:8, 0:w], in_=PS[pb][0:8, 0:w], func=AF.Exp,
                                                              bias=NBF[:, l:l + 1], scale=-1.0), reads=[P(pb), "NBF"], writes=["FA"])
                S.add('act', lambda e, w=w: e.activation(out=FA[0:8, 0:w], in_=FA[0:8, 0:w], func=AF.Ln, bias=1.0, scale=1.0),
                      reads=["FA"], writes=["FA"])
                S.add('dve', lambda e, w=w: e.tensor_scalar(out=FC[0:8, 0:w], in0=FA[0:8, 0:w], scalar1=-1.0, scalar2=None, op0=ALU.mult),
                      reads=["FA"], writes=["FC"])
                S.add('sp', lambda e, w=w, o=outdst: e.dma_start(out=o, in_=FC[0:8, 0:w]), reads=["FC"], writes=[], dma=True)

            if isP:
                carry = None
                for t in range(NT):
                    pb = t % 4
                    proj(pb, None, 0, 8, t, "WF", wt=WF)
                    lp_from_psum(pb, TW, lfpd[l, n, :, t * TW:(t + 1) * TW])
                    carry = decay_chain(None, TW, carry, t % 2, pkd[:, :, t * TW:(t + 1) * TW], pqd[:, :, t * TW:(t + 1) * TW])
            else:
                proj(0, None, 0, 8, 0, "WF", wt=WF)
                lp_from_psum(0, 128, lfsd[l])
                S.add('dve', lambda e: e.tensor_copy(out=LPS[0:8, 0:128], in_=FA[0:8, 0:128]), reads=["FA"], writes=["LPS"])
                for sq in range(4):
                    S.add('sp', lambda e, sq=sq: e.dma_start(out=CL[:], in_=clf[l, sq].rearrange("(b p) h -> p b h", p=128)),
                          writes=["CL"], dma=True)
                    carry = None
                    for t4 in range(4):
                        pb = t4 % 4
                        for i in range(4):
                            b = t4 * 4 + i
                            S.add('pe', lambda e, pb=pb, i=i, b=b: e.transpose(PS[pb][0:8, i * 128:(i + 1) * 128], CL[:, b, :], IDF[:]),
                                  reads=["CL", "IDF"], writes=[P(pb)])
                        S.add('act', lambda e, pb=pb: e.activation(out=FA[0:8, 0:512], in_=PS[pb][0:8, 0:512], func=AF.Copy, scale=-1.0),
                              reads=[P(pb)], writes=["FA"])
                        carry = decay_chain(None, 512, carry, t4 % 2, pks[sq, :, :, t4 * 512:(t4 + 1) * 512], None)
                    S.add('dve', lambda e, sq=sq: e.tensor_copy(out=FA[0:8, 0:32], in_=LPS[0:8, sq * 32:(sq + 1) * 32]), reads=["LPS"], writes=["FA"])
                    decay_chain(None, 32, carry, 0, pks[sq, :, :, T:T + 32], pqs[sq])

            S.stage("F")
            if isP:
                S.barrier()
            cslot = {}

            def conv_views(buf):
                if isP:
                    return buf[:, 2:2 + TW], buf[:, 1:1 + TW], buf[:, 0:TW], None
                v = buf[:, 0:4 * 34].rearrange("p (s c) -> p s c", s=4)
                return v[:, :, 2:34], v[:, :, 1:33], v[:, :, 0:32], v[:, :, 0:2]

            def psv(pb):
                if isP:
                    return PS[pb][:, 0:TW]
                return PS[pb][:, 0:128].rearrange("p (s c) -> p s c", s=4)

            def sbv(ap):
                if isP:
                    return ap
                return ap.rearrange("p (s c) -> p s c", s=4)

            h_slot = None
            for c in range(4):
                s_bc = load_wa(l, 4 + c)
                if c % 2 == 0:
                    h_slot = load_wa(l, 8 + c // 2)
                for t in range(NT):
                    base = (3 * (c * NT + t)) % 6
                    pgb, pgc, ph = base, base + 1, base + 2
                    proj(pgb, s_bc, 0, 128, t, ("WA", s_bc))
                    proj(pgc, s_bc, 128, 128, t, ("WA", s_bc))
                    proj(ph, h_slot, (c % 2) * 128, 128, t, ("WA", h_slot))
                    ub = UT[t % 2]
                    cur, m1, m2, halo = conv_views(ub)
                    S.add('act', lambda e, ph=ph: e.activation(out=sbv(HS[:, 0:TW]), in_=psv(ph), func=AF.Copy),
                          reads=[P(ph)], writes=["HS"])
                    S.add('dve', lambda e, pgc=pgc, cur=cur: e.tensor_tensor(out=cur, in0=psv(pgc), in1=sbv(HS[:, 0:TW]), op=ALU.mult),
                          reads=[P(pgc), "HS"], writes=[("UT", t % 2)])
                    if isP:
                        if t == 0:
                            S.add('dve', lambda e, ub=ub: e.memset(ub[:, 0:2], 0.0), writes=[("UT", t % 2)])
                        else:
                            pu = UT[(t - 1) % 2]
                            S.add('dve', lambda e, ub=ub, pu=pu: e.tensor_copy(out=ub[:, 0:2], in_=pu[:, TW:TW + 2]),
                                  reads=[("UT", (t - 1) % 2)], writes=[("UT", t % 2)])
                    else:
                        o0 = ((l * 4 + c) * 4) * 2
                        S.add('dve', lambda e, halo=halo, o0=o0: e.tensor_copy(
                            out=halo, in_=SMIX[:, o0:o0 + 8].rearrange("p (s c) -> p s c", s=4)),
                            reads=["SMIX"], writes=[("UT", t % 2)])
                    wi = (l * 4 + c) * 3
                    S.add('act', lambda e, cur=cur, wi=wi, c=c: e.activation(
                        out=sbv(T1[:, 0:TW]), in_=cur, func=AF.Identity, bias=CB[:, l * 4 + c:l * 4 + c + 1], scale=CW[:, wi + 2:wi + 3]),
                        reads=[("UT", t % 2), "CW", "CB"], writes=["T1"])
                    S.add('dve', lambda e, m1=m1, wi=wi: e.scalar_tensor_tensor(
                        out=sbv(T1[:, 0:TW]), in0=m1, scalar=CW[:, wi + 1:wi + 2], in1=sbv(T1[:, 0:TW]), op0=ALU.mult, op1=ALU.add),
                        reads=[("UT", t % 2), "CW"], writes=["T1"])
                    S.add('dve', lambda e, m2=m2, wi=wi: e.scalar_tensor_tensor(
                        out=sbv(T1[:, 0:TW]), in0=m2, scalar=CW[:, wi:wi + 1], in1=sbv(T1[:, 0:TW]), op0=ALU.mult, op1=ALU.add),
                        reads=[("UT", t % 2), "CW"], writes=["T1"])
                    S.add('dve', lambda e, pgb=pgb, c=c, t=t: e.tensor_tensor(
                        out=sbv(CT[:, c, t * TW:(t + 1) * TW]), in0=psv(pgb), in1=sbv(T1[:, 0:TW]), op=ALU.mult),
                        reads=[P(pgb), "T1"], writes=[("CT", c, t)])
                    if isP and t == NT - 1:
                        S.add('sp', lambda e, ub=ub, c=c: e.dma_start(out=mcd[l, n, c], in_=ub[:, TW:TW + 2]),
                              reads=[("UT", t % 2)], writes=[], dma=True)
                    if not isP:
                        v = ub[:, 0:4 * 34].rearrange("p (s c) -> p s c", s=4)
                        for sq in range(4):
                            S.add('sp', lambda e, v=v, c=c, sq=sq: e.dma_start(out=mcd[l, 4 + sq, c], in_=v[:, sq, 32:34]),
                                  reads=[("UT", t % 2)], writes=[], dma=True)

            S.stage("C")
            conv_rows = None
            for pair in range(4):
                s_qk = load_wa(l, pair)
                s_v = load_wv(l, pair)
                wo_a = load_wr(wo[l, pair])
                if pair == 0:
                    conv_rows = [load_wr(wo[l, 4 + c]) for c in range(4)]
                S.stage("w1")
                for t in range(NT):
                    pb = (t % 2) * 1 + 6
                    proj(pb, s_qk, 128, 128, t, ("WA", s_qk))
                    S.stage("ka")
                    S.stage(f"ka{t}")
                    ks = KST[t % 2]
                    S.add('act', lambda e, pb=pb, ks=ks: e.activation(out=ks[:, 0:TW], in_=PS[pb][:, 0:TW], func=AF.Copy),
                          reads=[P(pb)], writes=[("KST", t % 2)])
                    S.stage("kb")
                    S.stage(f"kb{t}")
                    kdst = (kpd[l, n, pair * 128:(pair + 1) * 128, t * TW:(t + 1) * TW] if isP
                            else ksd[l, pair * 128:(pair + 1) * 128, :])
                    S.add('sp', lambda e, ks=ks, kdst=kdst: e.dma_start(out=kdst, in_=ks[:, 0:TW]),
                          reads=[("KST", t % 2)], writes=[], dma=True)
                    S.stage("kc")
                    S.stage(f"kc{t}")
                    if isP:
                        for hl in range(2):
                            if hl == 1:
                                S.stage("kd")
                                S.stage(f"kd{t}")
                            mkv = os.environ.get("MK_V", "")
                            if hl == 0:
                                tt_ = (3 - t) if mkv == "addr" else t
                                if True:
                                    S.add('dve', lambda e, ks=ks, tt_=tt_: e.tensor_copy(out=KT[0:64, 0, tt_ * TW:(tt_ + 1) * TW], in_=ks[0:64, 0:TW]),
                                          reads=[("KST", t % 2)], writes=[("KT", hl, t)])
                                else:
                                    S.add('dve', lambda e, pb=pb, tt_=tt_: e.tensor_copy(out=KT[0:64, 0, tt_ * TW:(tt_ + 1) * TW], in_=PS[pb][0:64, 0:TW]),
                                          reads=[P(pb)], writes=[("KT", hl, t)])
                            else:
                                S.add('act', lambda e, pb=pb, t=t: e.activation(out=KT[0:64, 1, t * TW:(t + 1) * TW], in_=PS[pb][64:128, 0:TW], func=AF.Copy),
                                      reads=[P(pb)], writes=[("KT", hl, t)])
                    S.stage("ke")
                    S.stage(f"ke{t}")
                S.stage("k1")
                if isP:
                    for b in range(NB):
                        pb = b % 2 + 4
                        for k in range(8):
                            S.add('pe', lambda e, pb=pb, k=k, b=b: e.matmul(
                                PS[pb][:, 0:128], XT[:, k, b * 128:(b + 1) * 128], WV[s_v][:, k, :], start=(k == 0), stop=(k == 7)),
                                reads=[("XT", b), ("WV", s_v)], writes=[P(pb)])
                        vs_ = VST[b % 2]
                        S.add('act', lambda e, pb=pb, vs_=vs_: e.activation(out=vs_[:], in_=PS[pb][:, 0:128], func=AF.Copy),
                              reads=[P(pb)], writes=[("VST", b % 2)])
                        S.add('sp', lambda e, vs_=vs_, b=b: e.dma_start(out=vpd[l, n, b * 128:(b + 1) * 128, pair * 128:(pair + 1) * 128], in_=vs_[:]),
                              reads=[("VST", b % 2)], writes=[], dma=True)
                        S.add('dve', lambda e, pb=pb, b=b: e.tensor_copy(
                            out=VA[:, b, :, 0:64], in_=PS[pb][:, 0:128].rearrange("p (h c) -> p h c", h=2)),
                            reads=[P(pb)], writes=[("VA", b)])
                    S.stage("v1")
                    for hl in range(2):
                        hh = pair * 2 + hl
                        S.add('sp', lambda e, hl=hl, hh=hh: e.dma_start(out=KT[67:70, hl, 0:TT], in_=pkd[hh, :, 0:TT]),
                              reads=["pkd"], writes=[("KTaug", hl)], dma=True)

                S.stage("kv")
                if isP:
                    for t in range(NT):
                        qt = QT[t % 2]
                        qres = ("QT", t % 2)
                        pb = 6 + (t % 2)
                        proj(pb, s_qk, 0, 128, t, ("WA", s_qk))
                        S.add('act', lambda e, pb=pb, qt=qt: e.activation(out=qt[0:64, 0, :], in_=PS[pb][0:64, 0:TW], func=AF.Copy),
                              reads=[P(pb)], writes=[qres])
                        S.add('act', lambda e, pb=pb, qt=qt: e.activation(out=qt[0:64, 1, :], in_=PS[pb][64:128, 0:TW], func=AF.Copy),
                              reads=[P(pb)], writes=[qres])
                        for hl in range(2):
                            hh = pair * 2 + hl
                            S.add('sp', lambda e, hl=hl, hh=hh, qt=qt, t=t: e.dma_start(out=qt[64:67, hl, :], in_=pqd[hh, :, t * TW:(t + 1) * TW]),
                                  reads=["pqd"], writes=[qres], dma=True)
                        at = AT[t % 2]
                        ares = ("AT", t % 2)
                        for hl in range(2):
                            ob = 4 + hl
                            nkb = 4 * t + 4
                            kres = [("KT", hl, tt) for tt in range(t + 1)] + [("KTaug", hl), "KT"]
                            for g in range(nkb // 2):
                                sb0 = (g % 2) * 2
                                ptb = PT[g % 2]
                                pres = ("PT", g % 2)
                                for i in range(2):
                                    kb = g * 2 + i
                                    j = kb - 4 * t
                                    c0 = 0 if j < 0 else j * 128
                                    S.add('pe', lambda e, sbk=sb0 + i, kb=kb, hl=hl, qt=qt, c0=c0, j=j: e.matmul(
                                        PS[sbk][:, c0:TW], KT[0:70, hl, kb * 128:(kb + 1) * 128], qt[0:70, hl, c0:TW],
                                        start=True, stop=(j < 0)),
                                        reads=kres + [qres], writes=[P(sb0 + i)])
                                    if j >= 0:
                                        S.add('pe', lambda e, sbk=sb0 + i, c0=c0: e.matmul(
                                            PS[sbk][:, c0:c0 + 128], IDB[:], MASK[:], start=False, stop=True),
                                            reads=["IDB", "MASK"], writes=[P(sb0 + i)])
                                    S.add('act', lambda e, sbk=sb0 + i, ptb=ptb, i=i, c0=c0: e.activation(
                                        out=ptb[:, i * 512 + c0:i * 512 + TW], in_=PS[sbk][:, c0:TW], func=AF.Exp, scale=0.125),
                                        reads=[P(sb0 + i)], writes=[pres])
                                for i in range(2):
                                    kb = g * 2 + i
                                    j = kb - 4 * t
                                    c0 = 0 if j < 0 else j * 128
                                    S.add('pe', lambda e, ob=ob, kb=kb, hl=hl, ptb=ptb, i=i, c0=c0, nkb=nkb: e.matmul(
                                        PS[ob][:, c0:TW], VA[:, kb, hl, :], ptb[:, i * 512 + c0:i * 512 + TW],
                                        start=(kb == 0), stop=(kb == nkb - 1)),
                                        reads=[pres, ("VA", kb), "VA"], writes=[P(ob)])
                            S.add('dve', lambda e, ob=ob: e.reciprocal(out=RC[64:128, 0:TW], in_=PS[ob][64:128, 0:TW]),
                                  reads=[P(ob)], writes=["RC"])
                            S.add('dve', lambda e, ob=ob, hl=hl, at=at: e.tensor_tensor(
                                out=at[hl * 64:(hl + 1) * 64, 0:TW], in0=PS[ob][0:64, 0:TW], in1=RC[64:128, 0:TW], op=ALU.mult),
                                reads=[P(ob), "RC"], writes=[ares])
                        S.stage("att")
                        for bb in range(TW // 128):
                            b = t * (TW // 128) + bb
                            terms = [(at[:, bb * 128:(bb + 1) * 128], WR[wo_a], [ares, ("WR", wo_a)])]
                            if pair == 0:
                                for c in range(4):
                                    terms.append((CT[:, c, b * 128:(b + 1) * 128], WR[conv_rows[c]], [("CT", c, t), ("WR", conv_rows[c])]))
                            bpair = [(6, 7), (4, 5)][bb % 2]
                            for hf in range(2):
                                pb2 = bpair[hf]
                                for ti, (lt, rw, rs) in enumerate(terms):
                                    S.add('pe', lambda e, pb2=pb2, lt=lt, rw=rw, hf=hf, ti=ti, nt_=len(terms): e.matmul(
                                        PS[pb2][:, 0:512], lt, rw[:, hf * 512:(hf + 1) * 512], start=(ti == 0), stop=(ti == nt_ - 1)),
                                        reads=rs, writes=[P(pb2)])
                            x_update(b, pair == 0, bpair)
                else:
                    sample_attention(l, pair, s_qk, s_v, wo_a, conv_rows, KT, VA, QT, PT, AT, CT, RC, OS, FA, FBb, FC, PKt, PQt, KC, CL,
                                     decay_chain, proj, x_update)

            S.stage("pairs")
            load_ln(l, 0)
            layer_norm_all()
            S.stage("ln1")

            build_xt()
            S.barrier()
            arena.reset()
            A2 = arena.get(6 * TT, BF16).rearrange("p (c t) -> p c t", c=6)
            GT = [arena.get(TW + 8, F32) for _ in range(2)]
            T2 = [arena.get(512, F32) for _ in range(2)]
            T3 = [arena.get(512, F32) for _ in range(2)]
            for gi, (j0, j1) in enumerate(FFN_GROUPS):
                for j in range(j0, j1):
                    jl = j - j0
                    s_up = load_wa(l, 10 + j)
                    for t in range(NT):
                        base = 2 * ((j * NT + t) % 3)
                        pg, pv = base, base + 1
                        proj(pg, s_up, 0, 128, t, ("WA", s_up))
                        proj(pv, s_up, 128, 128, t, ("WA", s_up))
                        gb_ = GT[t % 2]
                        cur, m1, m2, halo = conv_views(gb_)
                        t2 = T2[t % 2]
                        t3 = T3[t % 2]
                        S.add('act', lambda e, pg=pg, cur=cur: e.activation(out=cur, in_=psv(pg), func=AF.Copy),
                              reads=[P(pg)], writes=[("GT", t % 2)])
                        if isP:
                            if t == 0:
                                S.add('dve', lambda e, gb_=gb_: e.memset(gb_[:, 0:2], 0.0), writes=[("GT", t % 2)])
                            else:
                                pu = GT[(t - 1) % 2]
                                S.add('dve', lambda e, gb_=gb_, pu=pu: e.tensor_copy(out=gb_[:, 0:2], in_=pu[:, TW:TW + 2]),
                                      reads=[("GT", (t - 1) % 2)], writes=[("GT", t % 2)])
                        else:
                            o0 = ((l * NCH + j) * 4) * 2
                            S.add('dve', lambda e, halo=halo, o0=o0: e.tensor_copy(
                                out=halo, in_=SFFN[:, o0:o0 + 8].rearrange("p (s c) -> p s c", s=4)),
                                reads=["SFFN"], writes=[("GT", t % 2)])
                        wi = (l * NCH + j) * 3
                        bi = l * NCH + j
                        S.add('act', lambda e, pg=pg, t2=t2, wi=wi, bi=bi: e.activation(
                            out=sbv(t2[:, 0:TW]), in_=psv(pg), func=AF.Identity, bias=FB[:, bi:bi + 1], scale=FW[:, wi + 2:wi + 3]),
                            reads=[P(pg), "FW", "FB"], writes=[("T2", t % 2)])
                        S.add('dve', lambda e, m1=m1, t2=t2, wi=wi: e.scalar_tensor_tensor(
                            out=sbv(t2[:, 0:TW]), in0=m1, scalar=FW[:, wi + 1:wi + 2], in1=sbv(t2[:, 0:TW]), op0=ALU.mult, op1=ALU.add),
                            reads=[("GT", t % 2), "FW"], writes=[("T2", t % 2)])
                        S.add('dve', lambda e, m2=m2, t2=t2, wi=wi: e.scalar_tensor_tensor(
                            out=sbv(t2[:, 0:TW]), in0=m2, scalar=FW[:, wi:wi + 1], in1=sbv(t2[:, 0:TW]), op0=ALU.mult, op1=ALU.add),
                            reads=[("GT", t % 2), "FW"], writes=[("T2", t % 2)])
                        S.add('act', lambda e, t2=t2, t3=t3: e.activation(out=t3[:, 0:TW], in_=t2[:, 0:TW], func=AF.Silu),
                              reads=[("T2", t % 2)], writes=[("T3", t % 2)])
                        S.add('dve', lambda e, pv=pv, t3=t3, jl=jl, t=t: e.tensor_tensor(
                            out=A2[:, jl, t * TW:(t + 1) * TW], in0=PS[pv][:, 0:TW], in1=t3[:, 0:TW], op=ALU.mult),
                            reads=[P(pv), ("T3", t % 2)], writes=[("A2", jl, t)])
                        if isP and t == NT - 1:
                            S.add('sp', lambda e, gb_=gb_, j=j: e.dma_start(out=fcd[l, n, j], in_=gb_[:, TW:TW + 2]),
                                  reads=[("GT", t % 2)], writes=[], dma=True)
                        if not isP:
                            v = gb_[:, 0:4 * 34].rearrange("p (s c) -> p s c", s=4)
                            for sq in range(4):
                                S.add('sp', lambda e, v=v, j=j, sq=sq: e.dma_start(out=fcd[l, 4 + sq, j], in_=v[:, sq, 32:34]),
                                      reads=[("GT", t % 2)], writes=[], dma=True)
                rows = [load_wr(wdn[l, j]) for j in range(j0, j1)]
                for b in range(NB):
                    t = b * 128 // TW
                    bpair = [(6, 7), (4, 5), (2, 3), (0, 1)][b % 4]
                    for hf in range(2):
                        pb2 = bpair[hf]
                        for ti, j in enumerate(range(j0, j1)):
                            jl = j - j0
                            S.add('pe', lambda e, pb2=pb2, jl=jl, b=b, r=rows[ti], hf=hf, ti=ti, n_=j1 - j0: e.matmul(
                                PS[pb2][:, 0:512], A2[:, jl, b * 128:(b + 1) * 128], WR[r][:, hf * 512:(hf + 1) * 512],
                                start=(ti == 0), stop=(ti == n_ - 1)),
                                reads=[("A2", jl, t), ("WR", rows[ti])], writes=[P(pb2)])
                    x_update(b, gi == 0, bpair)
            load_ln(l, 1)
            layer_norm_all()

        for b in range(NB):
            dst = yp[n, b * 128:(b + 1) * 128, :] if isP else ys
            S.add('sp', lambda e, b=b, d=dst: e.dma_start(out=d, in_=X[:, b, :]), reads=[("X", b)], writes=[], dma=True)

    def sample_attention(l, pair, s_qk, s_v, wo_a, conv_rows, KT, VA, QT, PT, AT, CT, RC, OS, FA, FBb, FC, PKt, PQt, KC, CL,
                         decay_chain, proj, x_update):
        TW = 128
        qt = QT[0]
        proj(6, s_qk, 0, 128, 0, ("WA", s_qk))
        S.add('act', lambda e: e.activation(out=qt[0:64, 0, :], in_=PS[6][0:64, 0:128], func=AF.Copy), reads=[P(6)], writes=["QTs"])
        S.add('act', lambda e: e.activation(out=qt[0:64, 1, :], in_=PS[6][64:128, 0:128], func=AF.Copy), reads=[P(6)], writes=["QTs"])
        at = AT[0]
        for sq in range(4):
            S.add('pool', lambda e, sq=sq: e.dma_start(
                out=KC[:], in_=ck[l, sq, :, pair * 128:(pair + 1) * 128].rearrange("(b p) c -> p b c", p=128)),
                writes=["KC"], dma=True, force_barrier=True)
            for hl in range(2):
                S.add('pool', lambda e, sq=sq, hl=hl: e.dma_start(
                    out=VA[:, 0:16, hl, 0:64],
                    in_=cv[l, sq, :, pair * 128 + hl * 64:pair * 128 + (hl + 1) * 64].rearrange("(b p) c -> p b c", p=128)),
                    writes=[("VAs", hl)], dma=True, force_barrier=True)
            for g in range(4):
                pb = g % 2
                for i in range(4):
                    b = g * 4 + i
                    S.add('pe', lambda e, pb=pb, i=i, b=b: e.matmul(
                        PS[pb][:, i * 128:(i + 1) * 128], KC[:, b, :], IDB[:], start=True, stop=True),
                        reads=["KC", "IDB"], writes=[P(pb)])
                S.add('act', lambda e, pb=pb, g=g: e.activation(out=KT[0:64, 0, g * 512:(g + 1) * 512], in_=PS[pb][0:64, 0:512], func=AF.Copy),
                      reads=[P(pb)], writes=[("KTs", 0)])
                S.add('act', lambda e, pb=pb, g=g: e.activation(out=KT[0:64, 1, g * 512:(g + 1) * 512], in_=PS[pb][64:128, 0:512], func=AF.Copy),
                      reads=[P(pb)], writes=[("KTs", 1)])
            proj(7, s_qk, 128, 128, 0, ("WA", s_qk))
            S.add('act', lambda e, sq=sq: e.activation(out=KT[0:64, 0, T:T + 32], in_=PS[7][0:64, sq * 32:(sq + 1) * 32], func=AF.Copy),
                  reads=[P(7)], writes=[("KTs", 0)])
            S.add('act', lambda e, sq=sq: e.activation(out=KT[0:64, 1, T:T + 32], in_=PS[7][64:128, sq * 32:(sq + 1) * 32], func=AF.Copy),
                  reads=[P(7)], writes=[("KTs", 1)])
            for k in range(8):
                S.add('pe', lambda e, k=k, sq=sq: e.matmul(
                    PS[5][0:32, 0:128], XT[:, k, sq * 32:(sq + 1) * 32], WV[s_v][:, k, :], start=(k == 0), stop=(k == 7)),
                    reads=[("XT", 0), ("WV", s_v)], writes=[P(5)])
            vs_ = VST[sq % 2]
            S.add('act', lambda e, vs_=vs_: e.activation(out=vs_[0:32, :], in_=PS[5][0:32, 0:128], func=AF.Copy),
                  reads=[P(5)], writes=[("VST", sq % 2)])
            S.add('sp', lambda e, vs_=vs_, sq=sq: e.dma_start(out=vsd[l, sq, :, pair * 128:(pair + 1) * 128], in_=vs_[0:32, :]),
                  reads=[("VST", sq % 2)], writes=[], dma=True)
            S.add('act', lambda e: e.activation(out=VA[0:32, 16, :, 0:64], in_=PS[5][0:32, 0:128].rearrange("p (h c) -> p h c", h=2), func=AF.Copy),
                  reads=[P(5)], writes=[("VAs", 2)])
            for hl in range(2):
                hh = pair * 2 + hl
                S.add('sp', lambda e, hl=hl, hh=hh, sq=sq: e.dma_start(out=KT[67:70, hl, 0:T + 32], in_=pks[sq, hh, :, :]),
                      reads=["pkd"], writes=[("KTs", hl)], dma=True)
                S.add('sp', lambda e, hl=hl, hh=hh, sq=sq: e.dma_start(out=qt[64:67, hl, sq * 32:(sq + 1) * 32], in_=pqs[sq, hh, :, :]),
                      reads=["pqd"], writes=["QTs"], dma=True)
            for hl in range(2):
                ob = 4
                q_ap = qt[0:70, hl, sq * 32:(sq + 1) * 32]
                for b in range(16):
                    S.add('pe', lambda e, b=b, hl=hl, q_ap=q_ap: e.matmul(
                        PS[2][:, b * 32:(b + 1) * 32], KT[0:70, hl, b * 128:(b + 1) * 128], q_ap, start=True, stop=True),
                        reads=[("KTs", hl), "QTs", "KT"], writes=[P(2)])
                S.add('pe', lambda e, hl=hl, q_ap=q_ap: e.matmul(
                    PS[3][0:32, 0:32], KT[0:70, hl, T:T + 32], q_ap, start=True, stop=False),
                    reads=[("KTs", hl), "QTs", "KT"], writes=[P(3)])
                S.add('pe', lambda e: e.matmul(PS[3][0:32, 0:32], IDB[0:32, 0:32], MASK[0:32, 0:32], start=False, stop=True),
                      reads=["IDB", "MASK"], writes=[P(3)])
                S.add('act', lambda e: e.activation(out=PT[0][:, 0:512], in_=PS[2][:, 0:512], func=AF.Exp, scale=0.125),
                      reads=[P(2)], writes=[("PT", 0)])
                S.add('act', lambda e: e.activation(out=PT[1][0:32, 0:32], in_=PS[3][0:32, 0:32], func=AF.Exp, scale=0.125),
                      reads=[P(3)], writes=[("PT", 1)])
                for b in range(16):
                    S.add('pe', lambda e, b=b, hl=hl: e.matmul(
                        PS[ob][:, 0:32], VA[:, b, hl, :], PT[0][:, b * 32:(b + 1) * 32], start=(b == 0), stop=False),
                        reads=[("PT", 0), ("VAs", hl), "VA"], writes=[P(ob)])
                S.add('pe', lambda e, hl=hl: e.matmul(
                    PS[ob][:, 0:32], VA[0:32, 16, hl, :], PT[1][0:32, 0:32], start=False, stop=True),
                    reads=[("PT", 1), ("VAs", 2), "VA"], writes=[P(ob)])
                S.add('dve', lambda e: e.reciprocal(out=RC[64:128, 0:32], in_=PS[ob][64:128, 0:32]), reads=[P(ob)], writes=["RC"])
                S.add('dve', lambda e, hl=hl, sq=sq: e.tensor_tensor(
                    out=at[hl * 64:(hl + 1) * 64, sq * 32:(sq + 1) * 32], in0=PS[ob][0:64, 0:32], in1=RC[64:128, 0:32], op=ALU.mult),
                    reads=[P(ob), "RC"], writes=[("AT", 0)])
        terms = [(at[:, 0:128], WR[wo_a], [("AT", 0), ("WR", wo_a)])]
        if pair == 0:
            for c in range(4):
                terms.append((CT[:, c, 0:128], WR[conv_rows[c]], [("CT", c, 0), ("WR", conv_rows[c])]))
        for hf in range(2):
            pb2 = 6 + hf
            for ti, (lt, rw, rs) in enumerate(terms):
                S.add('pe', lambda e, pb2=pb2, lt=lt, rw=rw, hf=hf, ti=ti, nt_=len(terms): e.matmul(
                    PS[pb2][:, 0:512], lt, rw[:, hf * 512:(hf + 1) * 512], start=(ti == 0), stop=(ti == nt_ - 1)),
                    reads=rs, writes=[P(pb2)])
        x_update(0, pair == 0, (6, 7))

    for n in range(nps):
        S.phase = n
        run_sequence('p', n)
    if do_sample:
        S.phase = 4
        run_sequence('s', 0)

    sem_keys = S.finalize()
    with ExitStack() as es:
        sems = {k: es.enter_context(nc.semaphore("s_" + "_".join(str(x) for x in k))) for k in sem_keys}
        block = es.enter_context(nc.Block())

        @block.tensor
        def _(e):
            S.emit(e, 'pe', sems)

        @block.scalar
        def _(e):
            S.emit(e, 'act', sems)

        @block.vector
        def _(e):
            S.emit(e, 'dve', sems)

        @block.gpsimd
        def _(e):
            S.emit(e, 'pool', sems)

        @block.sync
        def _(e):
            S.emit(e, 'sp', sems)
    return nc


def _prep_weights(w_in, b_f, conv_w, conv_b, w_out, ln1_g, ln1_b, w_up, ffn_conv_w, ffn_conv_b, w_down, ln2_g, ln2_b):
    f = np.float32
    w_in = np.asarray(w_in, f)
    w_up = np.asarray(w_up, f)

    def unit(mat):
        C = mat.shape[1]
        return mat.reshape(8, 128, C).transpose(1, 0, 2)

    wa = np.empty((L, 32, 128, 8, 256), f)
    wv = np.empty((L, 4, 128, 8, 128), f)
    wf = np.empty((L, 128, 8, 8), f)
    for l in range(L):
        W = w_in[l]
        q, k, v = W[:, 0:512], W[:, 512:1024], W[:, 1024:1536]
        fl = W[:, 1536:1544]
        gb, gc, hh = W[:, 1544:2056], W[:, 2056:2568], W[:, 2568:3080]
        for p in range(4):
            wa[l, p, :, :, 0:128] = unit(q[:, p * 128:(p + 1) * 128])
            wa[l, p, :, :, 128:256] = unit(k[:, p * 128:(p + 1) * 128])
            wv[l, p] = unit(v[:, p * 128:(p + 1) * 128])
        for c in range(4):
            wa[l, 4 + c, :, :, 0:128] = unit(gb[:, c * 128:(c + 1) * 128])
            wa[l, 4 + c, :, :, 128:256] = unit(gc[:, c * 128:(c + 1) * 128])
        for c2 in range(2):
            wa[l, 8 + c2, :, :, 0:128] = unit(hh[:, (2 * c2) * 128:(2 * c2 + 1) * 128])
            wa[l, 8 + c2, :, :, 128:256] = unit(hh[:, (2 * c2 + 1) * 128:(2 * c2 + 2) * 128])
        for j in range(NCH):
            wa[l, 10 + j, :, :, 0:128] = unit(w_up[l][:, j * 128:(j + 1) * 128])
            wa[l, 10 + j, :, :, 128:256] = unit(w_up[l][:, DFF + j * 128:DFF + (j + 1) * 128])
        wf[l] = unit(fl)
    wo = np.ascontiguousarray(np.asarray(w_out, f).reshape(L, 8, 128, D))
    wdn = np.ascontiguousarray(np.asarray(w_down, f).reshape(L, NCH, 128, D))
    lnrep = np.empty((L, 2, 128, 2 * D), f)
    lnrep[:, 0, :, 0:D] = np.asarray(ln1_g, f)[:, None, :]
    lnrep[:, 0, :, D:] = np.asarray(ln1_b, f)[:, None, :]
    lnrep[:, 1, :, 0:D] = np.asarray(ln2_g, f)[:, None, :]
    lnrep[:, 1, :, D:] = np.asarray(ln2_b, f)[:, None, :]
    cwd = np.ascontiguousarray(np.asarray(conv_w, f).reshape(L, 3, 4, 128).transpose(3, 0, 2, 1)).reshape(128, L * 4 * 3)
    cbd = np.ascontiguousarray(np.asarray(conv_b, f).reshape(L, 4, 128).transpose(2, 0, 1)).reshape(128, L * 4)
    fwd = np.ascontiguousarray(np.asarray(ffn_conv_w, f).reshape(L, 3, NCH, 128).transpose(3, 0, 2, 1)).reshape(128, L * NCH * 3)
    fbd = np.ascontiguousarray(np.asarray(ffn_conv_b, f).reshape(L, NCH, 128).transpose(2, 0, 1)).reshape(128, L * NCH)
    bfd = np.ascontiguousarray(np.asarray(b_f, f).T)
    idfd = np.eye(128, dtype=f)
    idbd = np.eye(128).astype(ml_dtypes.bfloat16)
    kk = np.arange(128)[:, None]
    qq = np.arange(128)[None, :]
    maskd = np.where(kk <= qq, 0.0, NEG).astype(ml_dtypes.bfloat16)
    return dict(wa=wa.reshape(L, 32, 128, 8 * 256), wv=wv.reshape(L, 4, 128, 8 * 128), wf=wf.reshape(L, 128, 64), wo=wo, wdn=wdn,
                lnrep=lnrep, cwd=cwd, cbd=cbd, fwd=fwd, fbd=fbd, bfd=bfd, idfd=idfd, idbd=idbd, maskd=maskd)


_NC_CACHE = {}


def kernel(x_prompt, x_sample, cache_k, cache_v, cache_logf, state_mix_conv, state_ffn_conv,
           w_in, b_f, conv_w, conv_b, w_out, ln1_g, ln1_b, w_up, ffn_conv_w, ffn_conv_b, w_down, ln2_g, ln2_b):
    f = np.float32
    nps = int(os.environ.get("MK_NPS", NPS))
    nl = int(os.environ.get("MK_NL", L))
    do_s = int(os.environ.get("MK_SAMPLE", 1)) == 1
    ncores = 8
    wd = _prep_weights(w_in, b_f, conv_w, conv_b, w_out, ln1_g, ln1_b, w_up, ffn_conv_w, ffn_conv_b, w_down, ln2_g, ln2_b)
    x_prompt = np.asarray(x_prompt, f)
    x_sample = np.asarray(x_sample, f)
    cache_k = np.asarray(cache_k, f)
    cache_v = np.asarray(cache_v, f)
    cache_logf = np.asarray(cache_logf, f)
    smc = np.asarray(state_mix_conv, f)
    sfc = np.asarray(state_ffn_conv, f)
    in_maps = []
    for c in range(ncores):
        sl = slice(c * 4, (c + 1) * 4)
        m = dict(wd)
        m["xp"] = np.ascontiguousarray(x_prompt[sl])
        m["xs"] = np.ascontiguousarray(x_sample[sl].reshape(128, D))
        m["ck"] = np.ascontiguousarray(cache_k[:, sl].reshape(L, 4, T, 512))
        m["cv"] = np.ascontiguousarray(cache_v[:, sl].reshape(L, 4, T, 512))
        m["clf"] = np.ascontiguousarray(cache_logf[:, sl])
        m["smix"] = np.ascontiguousarray(smc[:, sl].reshape(L, 4, 2, 4, 128).transpose(4, 0, 3, 1, 2)).reshape(128, -1)
        m["sffn"] = np.ascontiguousarray(sfc[:, sl].reshape(L, 4, 2, NCH, 128).transpose(4, 0, 3, 1, 2)).reshape(128, -1)
        in_maps.append(m)
    key = (nps, nl, do_s)
    if key not in _NC_CACHE:
        _NC_CACHE[key] = build_program(nps, nl, do_s)
    nc = _NC_CACHE[key]
    res = run_bass_kernel_spmd(nc, in_maps, core_ids=list(range(ncores)))
    R = res.results
    B = 32
    y_prompt = np.empty((B, T, D), f)
    y_sample = np.empty((B, 32, D), f)
    k_prompt = np.empty((L, B, T, H, 64), f)
    v_prompt = np.empty((L, B, T, H, 64), f)
    logf_prompt = np.empty((L, B, T, H), f)
    mix_p = np.empty((L, B, 2, 512), f)
    ffn_p = np.empty((L, B, 2, DFF), f)
    k_sample = np.empty((L, B, 32, H, 64), f)
    v_sample = np.empty((L, B, 32, H, 64), f)
    logf_sample = np.empty((L, B, 32, H), f)
    mix_s = np.empty((L, B, 2, 512), f)
    ffn_s = np.empty((L, B, 2, DFF), f)
    for c in range(ncores):
        r = R[c]
        sl = slice(c * 4, (c + 1) * 4)
        y_prompt[sl] = r["yp"]
        y_sample[sl] = r["ys"].reshape(4, 32, D)
        k_prompt[:, sl] = r["kpd"].reshape(L, 4, H, 64, T).transpose(0, 1, 4, 2, 3)
        v_prompt[:, sl] = r["vpd"].reshape(L, 4, T, H, 64)
        logf_prompt[:, sl] = r["lfpd"].transpose(0, 1, 3, 2)
        mc = r["mcd"]
        fc = r["fcd"]
        mix_p[:, sl] = mc[:, 0:4].transpose(0, 1, 4, 2, 3).reshape(L, 4, 2, 512)
        mix_s[:, sl] = mc[:, 4:8].transpose(0, 1, 4, 2, 3).reshape(L, 4, 2, 512)
        ffn_p[:, sl] = fc[:, 0:4].transpose(0, 1, 4, 2, 3).reshape(L, 4, 2, DFF)
        ffn_s[:, sl] = fc[:, 4:8].transpose(0, 1, 4, 2, 3).reshape(L, 4, 2, DFF)
        k_sample[:, sl] = r["ksd"].reshape(L, H, 64, 4, 32).transpose(0, 3, 4, 1, 2)
        v_sample[:, sl] = r["vsd"].reshape(L, 4, 32, H, 64)
        logf_sample[:, sl] = r["lfsd"].reshape(L, H, 4, 32).transpose(0, 2, 3, 1)
    return (y_prompt, y_sample, k_prompt, v_prompt, logf_prompt, mix_p, ffn_p,
            k_sample, v_sample, logf_sample, mix_s, ffn_s)
```

```python
import os
import types
import numpy as np
import ml_dtypes
from contextlib import ExitStack
import concourse.bass as bass
import concourse.mybir as mybir
from concourse.bass_utils import run_bass_kernel_spmd

F32 = mybir.dt.float32
BF16 = mybir.dt.bfloat16
U8 = mybir.dt.uint8
AF = mybir.ActivationFunctionType
ALU = mybir.AluOpType

L = 4
D = 1024
H = 8
T = 2048
DFF = 2816
NCH = 22
NPS = 4
ALPHA = float(8 ** 0.25)
EPS = 1e-5
NEG = -30000.0
NS = 8
FFN_GROUPS = [(0, 6), (6, 12), (12, 17), (17, 22)]


def _freeze(fn):
    if fn is None or fn.__closure__ is None:
        return fn
    cells = []
    for c in fn.__closure__:
        try:
            cells.append(types.CellType(c.cell_contents))
        except ValueError:
            cells.append(c)
    return types.FunctionType(fn.__code__, fn.__globals__, fn.__name__, fn.__defaults__, tuple(cells))


class Sched:
    def __init__(self):
        self.ops = []
        self.last_w = {}
        self.readers = {}
        self.phase = 0
        self.dma_rr = {'sp': 0, 'pool': 0}
        self.dma_last = {}
        self.eng_last = {}
        self.barrier_deps = None
        self.barrier_seen = set()

    def stage(self, name):
        if os.environ.get("MK_STOP", "") == name:
            self.stopped = True

    def add(self, eng, fn, reads=(), writes=(), dma=False, force_barrier=False):
        if getattr(self, 'stopped', False):
            return -1
        i = len(self.ops)
        ps_reads = [r for r in reads if isinstance(r, tuple) and r[0] == 'ps']
        if ps_reads:
            reads = [r for r in reads if not (isinstance(r, tuple) and r[0] == 'ps')]
            writes = list(writes) + ps_reads
        deps = set()
        for r in reads:
            w = self.last_w.get(r)
            if w is not None:
                deps.add(w)
        for w_ in writes:
            w = self.last_w.get(w_)
            if w is not None:
                deps.add(w)
            deps.update(self.readers.get(w_, {}).values())
        op = dict(eng=eng, fn=_freeze(fn), deps=deps, dma=dma, phase=self.phase, signal=dma, sig=None)
        if dma:
            k = self.dma_rr[eng]
            self.dma_rr[eng] += 1
            s = ('d', eng, k % NS)
            prev = self.dma_last.get(s)
            if prev is not None:
                deps.add(prev)
            self.dma_last[s] = i
            op['dsem'] = s
            rkey = s
        else:
            rkey = eng
        if self.barrier_deps is not None and force_barrier:
            deps.update(self.barrier_deps)
        elif self.barrier_deps is not None and eng != 'pool' and eng not in self.barrier_seen:
            deps.update(self.barrier_deps)
            self.barrier_seen.add(eng)
        deps.discard(i)
        for r in reads:
            self.readers.setdefault(r, {})[rkey] = i
        for w_ in writes:
            self.last_w[w_] = i
            self.readers[w_] = {}
        self.eng_last[eng] = i
        self.ops.append(op)
        return i

    def barrier(self):
        deps = set()
        for eng in ('pe', 'act', 'dve'):
            if eng in self.eng_last:
                deps.add(self.eng_last[eng])
        for s, i in self.dma_last.items():
            if s[1] == 'sp':
                deps.add(i)
        self.barrier_deps = deps
        self.barrier_seen = set()

    def finalize(self):
        self.ops.append(dict(eng='sp', fn=None, deps=set(self.dma_last.values()), dma=False,
                             phase=self.phase, signal=False, sig=None))
        for op in self.ops:
            for d in op['deps']:
                p = self.ops[d]
                if op['eng'] == 'pe' and p['eng'] == 'pe' and not p['dma']:
                    continue
                p['signal'] = True
        cnt = {}
        for op in self.ops:
            if op['dma']:
                s = op['dsem']
                cnt[s] = cnt.get(s, 0) + 1
                op['sig'] = (s, 16 * cnt[s])
            elif op['signal']:
                s = (op['eng'], op['phase'])
                cnt[s] = cnt.get(s, 0) + 1
                op['sig'] = (s, cnt[s])
        return sorted(set(op['sig'][0] for op in self.ops if op['sig'] is not None), key=str)

    def emit(self, e, eng, sems):
        waited = {}
        ops = self.ops
        for op in ops:
            if op['eng'] != eng:
                continue
            need = {}
            for d in op['deps']:
                p = ops[d]
                if eng == 'pe' and p['eng'] == 'pe' and not p['dma']:
                    continue
                if p['sig'] is None:
                    raise RuntimeError("dep on non-signaling op")
                s, v = p['sig']
                if need.get(s, 0) < v:
                    need[s] = v
            for s, v in need.items():
                if waited.get(s, 0) < v:
                    e.wait_ge(sems[s], v)
                    waited[s] = v
            if op['fn'] is not None:
                ins = op['fn'](e)
                if op['sig'] is not None:
                    ins.then_inc(sems[op['sig'][0]], 16 if op['dma'] else 1)


def build_program(nps=NPS, nlayers=L, do_sample=True):
    nc = bass.Bass("TRN2", target_bir_lowering=False)
    S = Sched()

    def din(name, shape, dt=F32):
        return nc.dram_tensor(name, list(shape), dt, kind="ExternalInput").ap()

    def dout(name, shape, dt=F32):
        return nc.dram_tensor(name, list(shape), dt, kind="ExternalOutput").ap()

    xp = din("xp", [NPS, T, D])
    xs = din("xs", [128, D])
    ck = din("ck", [L, 4, T, 512])
    cv = din("cv", [L, 4, T, 512])
    clf = din("clf", [L, 4, T, 8])
    smix = din("smix", [128, L * 4 * 4 * 2])
    sffn = din("sffn", [128, L * NCH * 4 * 2])
    wa = din("wa", [L, 32, 128, 8 * 256])
    wv = din("wv", [L, 4, 128, 8 * 128])
    wf = din("wf", [L, 128, 8 * 8])
    wo = din("wo", [L, 8, 128, D])
    wdn = din("wdn", [L, NCH, 128, D])
    lnrep = din("lnrep", [L, 2, 128, 2 * D])
    cwd = din("cwd", [128, L * 4 * 3])
    cbd = din("cbd", [128, L * 4])
    fwd = din("fwd", [128, L * NCH * 3])
    fbd = din("fbd", [128, L * NCH])
    bfd = din("bfd", [8, L])
    idfd = din("idfd", [128, 128])
    idbd = din("idbd", [128, 128], BF16)
    maskd = din("maskd", [128, 128], BF16)

    yp = dout("yp", [NPS, T, D])
    ys = dout("ys", [128, D])
    kpd = dout("kpd", [L, NPS, 512, T])
    vpd = dout("vpd", [L, NPS, T, 512])
    lfpd = dout("lfpd", [L, NPS, 8, T])
    mcd = dout("mcd", [L, 8, 4, 128, 2])
    fcd = dout("fcd", [L, 8, NCH, 128, 2])
    ksd = dout("ksd", [L, 512, 128])
    vsd = dout("vsd", [L, 4, 32, 512])
    lfsd = dout("lfsd", [L, 8, 128])
    pkd = nc.dram_tensor("pkd", [8, 3, T], BF16).ap()
    pqd = nc.dram_tensor("pqd", [8, 3, T], BF16).ap()
    pks = nc.dram_tensor("pks", [4, 8, 3, T + 32], BF16).ap()
    pqs = nc.dram_tensor("pqs", [4, 8, 3, 32], BF16).ap()

    def sb(name, shape, dt):
        return nc.alloc_sbuf_tensor(name, list(shape), dt)

    X = sb("X", [128, 16, D], F32)
    XT = sb("XT", [128, 8, T], BF16)
    IDF = sb("IDF", [128, 128], F32)
    IDB = sb("IDB", [128, 128], BF16)
    MASK = sb("MASK", [128, 128], BF16)
    ONES = sb("ONES", [8, 512], F32)
    CW = sb("CW", [128, L * 4 * 3], F32)
    CB = sb("CB", [128, L * 4], F32)
    FW = sb("FW", [128, L * NCH * 3], F32)
    FB = sb("FB", [128, L * NCH], F32)
    BFs = sb("BFs", [8, L], F32)
    NBF = sb("NBF", [8, L], F32)
    SMIX = sb("SMIX", [128, L * 4 * 4 * 2], F32)
    SFFN = sb("SFFN", [128, L * NCH * 4 * 2], F32)
    EPSC = sb("EPSC", [128, 1], F32)
    NWA = 4
    WA = [sb(f"WA{i}", [128, 8, 256], BF16) for i in range(NWA)]
    WV = [sb(f"WV{i}", [128, 8, 128], BF16) for i in range(2)]
    WF = sb("WF", [128, 8, 8], BF16)
    NWR = 6
    WR = [sb(f"WR{i}", [128, D], BF16) for i in range(NWR)]
    GBt = sb("GBt", [128, 2 * D], F32)
    KST = [sb(f"KST{i}", [128, 512], F32) for i in range(2)]
    VST = [sb(f"VST{i}", [128, 128], F32) for i in range(2)]
    SST = [sb(f"SST{i}", [128, 2], F32) for i in range(4)]
    LNS = sb("LNS", [128, 16, 16], F32)
    LNR = sb("LNR", [128, 16, 4], F32)
    ARENA_BYTES = 55 * 1024
    AR = sb("AR", [128, ARENA_BYTES], U8)

    class Arena:
        def __init__(self):
            self.off = 0

        def reset(self):
            self.off = 0

        def get(self, nfree, dt):
            nbytes = nfree * (2 if dt == BF16 else 4)
            nbytes = (nbytes + 31) // 32 * 32
            v = AR[:, self.off:self.off + nbytes].bitcast(dt)
            self.off += nbytes
            assert self.off <= ARENA_BYTES, self.off
            return v

    arena = Arena()
    PS = [nc.alloc_psum_tensor(f"ps{i}", [128, 512], F32) for i in range(8)]

    def P(i):
        return ("ps", i)

    wa_rr = [0]
    wr_rr = [0]
    wv_rr = [0]

    def load_wa(l, unit):
        slot = wa_rr[0] % NWA
        wa_rr[0] += 1
        S.add('pool', lambda e, s=slot, l=l, u=unit: e.dma_start(
            out=WA[s][:].rearrange("p k c -> p (k c)"), in_=wa[l, u]), writes=[("WA", slot)], dma=True)
        return slot

    def load_wv(l, pair):
        slot = wv_rr[0] % 2
        wv_rr[0] += 1
        S.add('pool', lambda e, s=slot, l=l, u=pair: e.dma_start(
            out=WV[s][:].rearrange("p k c -> p (k c)"), in_=wv[l, u]), writes=[("WV", slot)], dma=True)
        return slot

    def load_wr(src_ap):
        slot = wr_rr[0] % NWR
        wr_rr[0] += 1
        S.add('pool', lambda e, s=slot, a=src_ap: e.dma_start(out=WR[s][:], in_=a), writes=[("WR", slot)], dma=True)
        return slot

    def load_ln(l, which):
        S.add('sp', lambda e, l=l, w=which: e.dma_start(out=GBt[:], in_=lnrep[l, w]), writes=["GBt"], dma=True)

    for dst, src, nm in [(IDF, idfd, "IDF"), (IDB, idbd, "IDB"), (MASK, maskd, "MASK"), (CW, cwd, "CW"), (CB, cbd, "CB"),
                         (FW, fwd, "FW"), (FB, fbd, "FB"), (BFs, bfd, "BFs"), (SMIX, smix, "SMIX"), (SFFN, sffn, "SFFN")]:
        S.add('sp', lambda e, d=dst, s=src: e.dma_start(out=d[:], in_=s), writes=[nm], dma=True)
    S.add('dve', lambda e: e.memset(ONES[:], 1.0), writes=["ONES"])
    S.add('dve', lambda e: e.memset(EPSC[:], EPS), writes=["EPSC"])
    S.add('dve', lambda e: e.tensor_scalar(out=NBF[:], in0=BFs[:], scalar1=-1.0, scalar2=None, op0=ALU.mult),
          reads=["BFs"], writes=["NBF"])

    def run_sequence(kind, n):
        isP = kind == 'p'
        TT = T if isP else 128
        TW = 512 if isP else 128
        NT = TT // TW
        NB = TT // 128
        sidx = n if isP else None

        for b in range(NB):
            src = xp[n, b * 128:(b + 1) * 128, :] if isP else xs
            S.add('sp', lambda e, b=b, s=src: e.dma_start(out=X[:, b, :], in_=s), writes=[("X", b)], dma=True)

        def build_xt():
            bank = [0]
            for b in range(NB):
                for half in range(2):
                    pb = bank[0] % 8
                    bank[0] += 1
                    for i in range(4):
                        k = half * 4 + i
                        S.add('pe', lambda e, pb=pb, i=i, b=b, k=k: e.transpose(
                            PS[pb][:, i * 128:(i + 1) * 128], X[:, b, k * 128:(k + 1) * 128], IDF[:]),
                            reads=[("X", b), "IDF"], writes=[P(pb)])
                    S.add('act', lambda e, pb=pb, b=b, half=half: e.activation(
                        out=XT[:, half * 4:(half + 1) * 4, b * 128:(b + 1) * 128],
                        in_=PS[pb][:].rearrange("p (a c) -> p a c", a=4), func=AF.Copy),
                        reads=[P(pb)], writes=[("XT", b)])

        def xt_res(t):
            return [("XT", b) for b in range(t * TW // 128, (t + 1) * TW // 128)]

        def proj(ps_i, wslot, col0, ncols, t, wres, wt=None):
            W = WA[wslot] if wt is None else wt
            for k in range(8):
                S.add('pe', lambda e, k=k, W=W: e.matmul(
                    PS[ps_i][0:ncols, 0:TW], W[:, k, col0:col0 + ncols], XT[:, k, t * TW:(t + 1) * TW],
                    start=(k == 0), stop=(k == 7)),
                    reads=[wres] + xt_res(t), writes=[P(ps_i)])

        def x_update(b, first, banks):
            for hf in range(2):
                xs_ = X[:, b, hf * 512:(hf + 1) * 512]
                pb = banks[hf]
                if first:
                    S.add('dve', lambda e, xs_=xs_, pb=pb: e.scalar_tensor_tensor(
                        out=xs_, in0=xs_, scalar=ALPHA, in1=PS[pb][:, 0:512], op0=ALU.mult, op1=ALU.add),
                        reads=[P(pb)], writes=[("X", b)])
                else:
                    S.add('dve', lambda e, xs_=xs_, pb=pb: e.tensor_tensor(
                        out=xs_, in0=xs_, in1=PS[pb][:, 0:512], op=ALU.add),
                        reads=[P(pb)], writes=[("X", b)])

        def layer_norm_all():
            GRP = 4
            for g0 in range(0, NB, GRP):
                blks = list(range(g0, min(NB, g0 + GRP)))
                for b in blks:
                    for hf in range(2):
                        S.add('dve', lambda e, hf=hf, b=b: e.bn_stats(out=LNS[:, b, hf * 6:(hf + 1) * 6], in_=X[:, b, hf * 512:(hf + 1) * 512]),
                              reads=[("X", b)], writes=[("LNS", b)])
                    S.add('dve', lambda e, b=b: e.bn_aggr(out=LNS[:, b, 12:14], in_=LNS[:, b, 0:12]), reads=[("LNS", b)], writes=[("LNS", b)])
                for b in blks:
                    S.add('act', lambda e, b=b: e.activation(out=LNR[:, b, 0:1], in_=LNS[:, b, 13:14], func=AF.Sqrt, bias=EPSC[:, 0:1], scale=1.0),
                          reads=[("LNS", b), "EPSC"], writes=[("LNR", b)])
                for b in blks:
                    S.add('dve', lambda e, b=b: e.reciprocal(out=LNR[:, b, 1:2], in_=LNR[:, b, 0:1]), reads=[("LNR", b)], writes=[("LNR", b)])
                for b in blks:
                    S.add('dve', lambda e, b=b: e.scalar_tensor_tensor(out=X[:, b, :], in0=X[:, b, :], scalar=LNS[:, b, 12:13], in1=GBt[:, 0:D],
                                                                      op0=ALU.subtract, op1=ALU.mult),
                          reads=["GBt", ("LNS", b)], writes=[("X", b)])
                    S.add('dve', lambda e, b=b: e.scalar_tensor_tensor(out=X[:, b, :], in0=X[:, b, :], scalar=LNR[:, b, 1:2], in1=GBt[:, D:2 * D],
                                                                      op0=ALU.mult, op1=ALU.add),
                          reads=["GBt", ("LNR", b)], writes=[("X", b)])

        def seg(ap, w):
            return ap

        for l in range(nlayers):
            build_xt()
            S.stage("xt")
            S.barrier()
            arena.reset()
            KT = arena.get(2 * (T if isP else T + 32), BF16).rearrange("p (h t) -> p h t", h=2)
            NVB = NB if isP else 17
            VA = arena.get(NVB * 2 * 128, BF16).rearrange("p (b h c) -> p b h c", b=NVB, h=2)
            QT = [arena.get(2 * TW, BF16).rearrange("p (h t) -> p h t", h=2) for _ in range(2)]
            PT = [arena.get(1024, BF16) for _ in range(2)]
            AT = [arena.get(TW, BF16) for _ in range(2)]
            ct_off = arena.off
            CT = arena.get(4 * TT, BF16).rearrange("p (c t) -> p c t", c=4)
            UT = [arena.get(TW + 8, F32) for _ in range(2)]
            HS = arena.get(512, F32)
            T1 = arena.get(512, F32)
            RC = arena.get(512, F32)
            OS = arena.get(512, F32)
            end_off = arena.off
            if isP:
                arena.off = ct_off
            FA = arena.get(512, F32)
            FBb = [arena.get(512, F32) for _ in range(2)]
            FC = arena.get(512, F32)
            PKt = arena.get(3 * 512, BF16).rearrange("p (j t) -> p j t", j=3)
            PQt = arena.get(3 * 512, BF16).rearrange("p (j t) -> p j t", j=3)
            if isP:
                arena.off = end_off
            KC = CL = LPS = None
            if not isP:
                KC = arena.get(16 * 128, BF16).rearrange("p (b c) -> p b c", b=16)
                CL = arena.get(16 * 8, F32).rearrange("p (b c) -> p b c", b=16)
                LPS = arena.get(128, F32)

            S.add('dve', lambda e, KT=KT: e.memset(KT[64:70, :, :], 1.0),
                  writes=["KT", ("KTaug", 0), ("KTaug", 1), ("KTs", 0), ("KTs", 1)])
            for qi, q in enumerate(QT):
                S.add('dve', lambda e, q=q: e.memset(q[64:70, :, :], 1.0), writes=[("QT", qi), "QTs"])
            S.add('dve', lambda e, VA=VA: e.memset(VA[:, :, :, 64:128], 1.0), writes=["VA"])

            S.add('pool', lambda e, l=l: e.dma_start(out=WF[:].rearrange("p k c -> p (k c)"), in_=wf[l]), writes=["WF"], dma=True)

            def decay_chain(src_fa_ready_res, width, carry, slot, kdst, qdst):
                w = width
                cur = FBb[slot]
                ini = 0.0 if carry is None else carry
                S.add('dve', lambda e, cur=cur, ini=ini, w=w: e.tensor_tensor_scan(
                    out=cur[0:8, 0:w], data0=ONES[0:8, 0:w], data1=FA[0:8, 0:w], initial=ini, op0=ALU.mult, op1=ALU.add),
                    reads=["FA", "ONES", ("FB", 1 - slot)], writes=[("FB", slot)])
                S.add('dve', lambda e, cur=cur, w=w: e.tensor_scalar(out=FC[0:8, 0:w], in0=cur[0:8, 0:w], scalar1=8.0, scalar2=None, op0=ALU.mult),
                      reads=[("FB", slot)], writes=["FC"])
                for j in range(3):
                    S.add('dve', lambda e, j=j, w=w: e.tensor_copy(out=PKt[0:8, j, 0:w], in_=FC[0:8, 0:w]), reads=["FC"], writes=["PKt"])
                    if j < 2:
                        S.add('dve', lambda e, j=j, w=w: e.tensor_tensor(out=FC[0:8, 0:w], in0=FC[0:8, 0:w], in1=PKt[0:8, j, 0:w], op=ALU.subtract),
                              reads=["PKt"], writes=["FC"])
                S.add('sp', lambda e, w=w, kdst=kdst: e.dma_start(out=kdst, in_=PKt[0:8, :, 0:w]), reads=["PKt"], writes=["pkd"], dma=True)
                if qdst is not None:
                    S.add('dve', lambda e, w=w: e.tensor_scalar(out=PQt[0:8, :, 0:w], in0=PKt[0:8, :, 0:w], scalar1=-1.0, scalar2=None, op0=ALU.mult),
                          reads=["PKt"], writes=["PQt"])
                    S.add('sp', lambda e, w=w, qdst=qdst: e.dma_start(out=qdst, in_=PQt[0:8, :, 0:w]), reads=["PQt"], writes=["pqd"], dma=True)
                return cur[0:8, w - 1:w]

            def lp_from_psum(pb, w, outdst):
                S.add('act', lambda e, pb=pb, w=w: e.activation(out=FA[0:8, 0:w], in_=PS[pb][0:8, 0:w], func=AF.Exp,
                                                              bias=NBF[:, l:l + 1], scale=-1.0), reads=[P(pb), "NBF"], writes=["FA"])
                S.add('act', lambda e, w=w: e.activation(out=FA[0:8, 0:w], in_=FA[0:8, 0:w], func=AF.Ln, bias=1.0, scale=1.0),
                      reads=["FA"], writes=["FA"])
                S.add('dve', lambda e, w=w: e.tensor_scalar(out=FC[0:8, 0:w], in0=FA[0:8, 0:w], scalar1=-1.0, scalar2=None, op0=ALU.mult),
                      reads=["FA"], writes=["FC"])
                S.add('sp', lambda e, w=w, o=outdst: e.dma_start(out=o, in_=FC[0:8, 0:w]), reads=["FC"], writes=[], dma=True)

            if isP:
                carry = None
                for t in range(NT):
                    pb = t % 4
                    proj(pb, None, 0, 8, t, "WF", wt=WF)
                    lp_from_psum(pb, TW, lfpd[l, n, :, t * TW:(t + 1) * TW])
                    carry = decay_chain(None, TW, carry, t % 2, pkd[:, :, t * TW:(t + 1) * TW], pqd[:, :, t * TW:(t + 1) * TW])
            else:
                proj(0, None, 0, 8, 0, "WF", wt=WF)
                lp_from_psum(0, 128, lfsd[l])
                S.add('dve', lambda e: e.tensor_copy(out=LPS[0:8, 0:128], in_=FA[0:8, 0:128]), reads=["FA"], writes=["LPS"])
                for sq in range(4):
                    S.add('sp', lambda e, sq=sq: e.dma_start(out=CL[:], in_=clf[l, sq].rearrange("(b p) h -> p b h", p=128)),
                          writes=["CL"], dma=True)
                    carry = None
                    for t4 in range(4):
                        pb = t4 % 4
                        for i in range(4):
                            b = t4 * 4 + i
                            S.add('pe', lambda e, pb=pb, i=i, b=b: e.transpose(PS[pb][0:8, i * 128:(i + 1) * 128], CL[:, b, :], IDF[:]),
                                  reads=["CL", "IDF"], writes=[P(pb)])
                        S.add('act', lambda e, pb=pb: e.activation(out=FA[0:8, 0:512], in_=PS[pb][0:8, 0:512], func=AF.Copy, scale=-1.0),
                              reads=[P(pb)], writes=["FA"])
                        carry = decay_chain(None, 512, carry, t4 % 2, pks[sq, :, :, t4 * 512:(t4 + 1) * 512], None)
                    S.add('dve', lambda e, sq=sq: e.tensor_copy(out=FA[0:8, 0:32], in_=LPS[0:8, sq * 32:(sq + 1) * 32]), reads=["LPS"], writes=["FA"])
                    decay_chain(None, 32, carry, 0, pks[sq, :, :, T:T + 32], pqs[sq])

            S.stage("F")
            if isP:
                S.barrier()
            cslot = {}

            def conv_views(buf):
                if isP:
                    return buf[:, 2:2 + TW], buf[:, 1:1 + TW], buf[:, 0:TW], None
                v = buf[:, 0:4 * 34].rearrange("p (s c) -> p s c", s=4)
                return v[:, :, 2:34], v[:, :, 1:33], v[:, :, 0:32], v[:, :, 0:2]

            def psv(pb):
                if isP:
                    return PS[pb][:, 0:TW]
                return PS[pb][:, 0:128].rearrange("p (s c) -> p s c", s=4)

            def sbv(ap):
                if isP:
                    return ap
                return ap.rearrange("p (s c) -> p s c", s=4)

            h_slot = None
            for c in range(4):
                s_bc = load_wa(l, 4 + c)
                if c % 2 == 0:
                    h_slot = load_wa(l, 8 + c // 2)
                for t in range(NT):
                    base = (3 * (c * NT + t)) % 6
                    pgb, pgc, ph = base, base + 1, base + 2
                    proj(pgb, s_bc, 0, 128, t, ("WA", s_bc))
                    proj(pgc, s_bc, 128, 128, t, ("WA", s_bc))
                    proj(ph, h_slot, (c % 2) * 128, 128, t, ("WA", h_slot))
                    ub = UT[t % 2]
                    cur, m1, m2, halo = conv_views(ub)
                    S.add('act', lambda e, ph=ph: e.activation(out=sbv(HS[:, 0:TW]), in_=psv(ph), func=AF.Copy),
                          reads=[P(ph)], writes=["HS"])
                    S.add('dve', lambda e, pgc=pgc, cur=cur: e.tensor_tensor(out=cur, in0=psv(pgc), in1=sbv(HS[:, 0:TW]), op=ALU.mult),
                          reads=[P(pgc), "HS"], writes=[("UT", t % 2)])
                    if isP:
                        if t == 0:
                            S.add('dve', lambda e, ub=ub: e.memset(ub[:, 0:2], 0.0), writes=[("UT", t % 2)])
                        else:
                            pu = UT[(t - 1) % 2]
                            S.add('dve', lambda e, ub=ub, pu=pu: e.tensor_copy(out=ub[:, 0:2], in_=pu[:, TW:TW + 2]),
                                  reads=[("UT", (t - 1) % 2)], writes=[("UT", t % 2)])
                    else:
                        o0 = ((l * 4 + c) * 4) * 2
                        S.add('dve', lambda e, halo=halo, o0=o0: e.tensor_copy(
                            out=halo, in_=SMIX[:, o0:o0 + 8].rearrange("p (s c) -> p s c", s=4)),
                            reads=["SMIX"], writes=[("UT", t % 2)])
                    wi = (l * 4 + c) * 3
                    S.add('act', lambda e, cur=cur, wi=wi, c=c: e.activation(
                        out=sbv(T1[:, 0:TW]), in_=cur, func=AF.Identity, bias=CB[:, l * 4 + c:l * 4 + c + 1], scale=CW[:, wi + 2:wi + 3]),
                        reads=[("UT", t % 2), "CW", "CB"], writes=["T1"])
                    S.add('dve', lambda e, m1=m1, wi=wi: e.scalar_tensor_tensor(
                        out=sbv(T1[:, 0:TW]), in0=m1, scalar=CW[:, wi + 1:wi + 2], in1=sbv(T1[:, 0:TW]), op0=ALU.mult, op1=ALU.add),
                        reads=[("UT", t % 2), "CW"], writes=["T1"])
                    S.add('dve', lambda e, m2=m2, wi=wi: e.scalar_tensor_tensor(
                        out=sbv(T1[:, 0:TW]), in0=m2, scalar=CW[:, wi:wi + 1], in1=sbv(T1[:, 0:TW]), op0=ALU.mult, op1=ALU.add),
                        reads=[("UT", t % 2), "CW"], writes=["T1"])
                    S.add('dve', lambda e, pgb=pgb, c=c, t=t: e.tensor_tensor(
                        out=sbv(CT[:, c, t * TW:(t + 1) * TW]), in0=psv(pgb), in1=sbv(T1[:, 0:TW]), op=ALU.mult),
                        reads=[P(pgb), "T1"], writes=[("CT", c, t)])
                    if isP and t == NT - 1:
                        S.add('sp', lambda e, ub=ub, c=c: e.dma_start(out=mcd[l, n, c], in_=ub[:, TW:TW + 2]),
                              reads=[("UT", t % 2)], writes=[], dma=True)
                    if not isP:
                        v = ub[:, 0:4 * 34].rearrange("p (s c) -> p s c", s=4)
                        for sq in range(4):
                            S.add('sp', lambda e, v=v, c=c, sq=sq: e.dma_start(out=mcd[l, 4 + sq, c], in_=v[:, sq, 32:34]),
                                  reads=[("UT", t % 2)], writes=[], dma=True)

            S.stage("C")
            conv_rows = None
            for pair in range(4):
                s_qk = load_wa(l, pair)
                s_v = load_wv(l, pair)
                wo_a = load_wr(wo[l, pair])
                if pair == 0:
                    conv_rows = [load_wr(wo[l, 4 + c]) for c in range(4)]
                S.stage("w1")
                for t in range(NT):
                    pb = (t % 2) * 1 + 6
                    proj(pb, s_qk, 128, 128, t, ("WA", s_qk))
                    S.stage("ka")
                    S.stage(f"ka{t}")
                    ks = KST[t % 2]
                    S.add('act', lambda e, pb=pb, ks=ks: e.activation(out=ks[:, 0:TW], in_=PS[pb][:, 0:TW], func=AF.Copy),
                          reads=[P(pb)], writes=[("KST", t % 2)])
                    S.stage("kb")
                    S.stage(f"kb{t}")
                    kdst = (kpd[l, n, pair * 128:(pair + 1) * 128, t * TW:(t + 1) * TW] if isP
                            else ksd[l, pair * 128:(pair + 1) * 128, :])
                    S.add('sp', lambda e, ks=ks, kdst=kdst: e.dma_start(out=kdst, in_=ks[:, 0:TW]),
                          reads=[("KST", t % 2)], writes=[], dma=True)
                    S.stage("kc")
                    S.stage(f"kc{t}")
                    if isP:
                        for hl in range(2):
                            if hl == 1:
                                S.stage("kd")
                                S.stage(f"kd{t}")
                            mkv = os.environ.get("MK_V", "")
                            if hl == 0:
                                tt_ = (3 - t) if mkv == "addr" else t
                                if True:
                                    S.add('dve', lambda e, ks=ks, tt_=tt_: e.tensor_copy(out=KT[0:64, 0, tt_ * TW:(tt_ + 1) * TW], in_=ks[0:64, 0:TW]),
                                          reads=[("KST", t % 2)], writes=[("KT", hl, t)])
                                else:
                                    S.add('dve', lambda e, pb=pb, tt_=tt_: e.tensor_copy(out=KT[0:64, 0, tt_ * TW:(tt_ + 1) * TW], in_=PS[pb][0:64, 0:TW]),
                                          reads=[P(pb)], writes=[("KT", hl, t)])
                            else:
                                S.add('act', lambda e, pb=pb, t=t: e.activation(out=KT[0:64, 1, t * TW:(t + 1) * TW], in_=PS[pb][64:128, 0:TW], func=AF.Copy),
                                      reads=[P(pb)], writes=[("KT", hl, t)])
                    S.stage("ke")
                    S.stage(f"ke{t}")
                S.stage("k1")
                if isP:
                    for b in range(NB):
                        pb = b % 2 + 4
                        for k in range(8):
                            S.add('pe', lambda e, pb=pb, k=k, b=b: e.matmul(
                                PS[pb][:, 0:128], XT[:, k, b * 128:(b + 1) * 128], WV[s_v][:, k, :], start=(k == 0), stop=(k == 7)),
                                reads=[("XT", b), ("WV", s_v)], writes=[P(pb)])
                        vs_ = VST[b % 2]
                        S.add('act', lambda e, pb=pb, vs_=vs_: e.activation(out=vs_[:], in_=PS[pb][:, 0:128], func=AF.Copy),
                              reads=[P(pb)], writes=[("VST", b % 2)])
                        S.add('sp', lambda e, vs_=vs_, b=b: e.dma_start(out=vpd[l, n, b * 128:(b + 1) * 128, pair * 128:(pair + 1) * 128], in_=vs_[:]),
                              reads=[("VST", b % 2)], writes=[], dma=True)
                        S.add('dve', lambda e, pb=pb, b=b: e.tensor_copy(
                            out=VA[:, b, :, 0:64], in_=PS[pb][:, 0:128].rearrange("p (h c) -> p h c", h=2)),
                            reads=[P(pb)], writes=[("VA", b)])
                    S.stage("v1")
                    for hl in range(2):
                        hh = pair * 2 + hl
                        S.add('sp', lambda e, hl=hl, hh=hh: e.dma_start(out=KT[67:70, hl, 0:TT], in_=pkd[hh, :, 0:TT]),
                              reads=["pkd"], writes=[("KTaug", hl)], dma=True)

                S.stage("kv")
                if isP:
                    for t in range(NT):
                        qt = QT[t % 2]
                        qres = ("QT", t % 2)
                        pb = 6 + (t % 2)
                        proj(pb, s_qk, 0, 128, t, ("WA", s_qk))
                        S.add('act', lambda e, pb=pb, qt=qt: e.activation(out=qt[0:64, 0, :], in_=PS[pb][0:64, 0:TW], func=AF.Copy),
                              reads=[P(pb)], writes=[qres])
                        S.add('act', lambda e, pb=pb, qt=qt: e.activation(out=qt[0:64, 1, :], in_=PS[pb][64:128, 0:TW], func=AF.Copy),
                              reads=[P(pb)], writes=[qres])
                        for hl in range(2):
                            hh = pair * 2 + hl
                            S.add('sp', lambda e, hl=hl, hh=hh, qt=qt, t=t: e.dma_start(out=qt[64:67, hl, :], in_=pqd[hh, :, t * TW:(t + 1) * TW]),
                                  reads=["pqd"], writes=[qres], dma=True)
                        at = AT[t % 2]
                        ares = ("AT", t % 2)
                        nkb = 4 * t + 4
                        ng = nkb // 2
                        glist = [(hl, g) for hl in range(2) for g in range(ng)]

                        def qk_exp(idx):
                            hl, g = glist[idx]
                            par = idx % 2
                            sb0 = par * 2
                            ptb = PT[par]
                            pres = ("PT", par)
                            kres = [("KT", hl, tt) for tt in range(t + 1)] + [("KTaug", hl), "KT"]
                            for i in range(2):
                                kb = g * 2 + i
                                j = kb - 4 * t
                                c0 = 0 if j < 0 else j * 128
                                S.add('pe', lambda e, sbk=sb0 + i, kb=kb, hl=hl, qt=qt, c0=c0, j=j: e.matmul(
                                    PS[sbk][:, c0:TW], KT[0:70, hl, kb * 128:(kb + 1) * 128], qt[0:70, hl, c0:TW],
                                    start=True, stop=(j < 0)),
                                    reads=kres + [qres], writes=[P(sb0 + i)])
                                if j >= 0:
                                    S.add('pe', lambda e, sbk=sb0 + i, c0=c0: e.matmul(
                                        PS[sbk][:, c0:c0 + 128], IDB[:], MASK[:], start=False, stop=True),
                                        reads=["IDB", "MASK"], writes=[P(sb0 + i)])
                                S.add('act', lambda e, sbk=sb0 + i, ptb=ptb, i=i, c0=c0: e.activation(
                                    out=ptb[:, i * 512 + c0:i * 512 + TW], in_=PS[sbk][:, c0:TW], func=AF.Exp, scale=0.125),
                                    reads=[P(sb0 + i)], writes=[pres])

                        def pv_norm(idx):
                            hl, g = glist[idx]
                            par = idx % 2
                            ptb = PT[par]
                            pres = ("PT", par)
                            ob = 4 + hl
                            for i in range(2):
                                kb = g * 2 + i
                                j = kb - 4 * t
                                c0 = 0 if j < 0 else j * 128
                                S.add('pe', lambda e, ob=ob, kb=kb, hl=hl, ptb=ptb, i=i, c0=c0, nkb=nkb: e.matmul(
                                    PS[ob][:, c0:TW], VA[:, kb, hl, :], ptb[:, i * 512 + c0:i * 512 + TW],
                                    start=(kb == 0), stop=(kb == nkb - 1)),
                                    reads=[pres, ("VA", kb), "VA"], writes=[P(ob)])
                            if g == ng - 1:
                                S.add('dve', lambda e, ob=ob: e.reciprocal(out=RC[64:128, 0:TW], in_=PS[ob][64:128, 0:TW]),
                                      reads=[P(ob)], writes=["RC"])
                                S.add('dve', lambda e, ob=ob, hl=hl, at=at: e.tensor_tensor(
                                    out=at[hl * 64:(hl + 1) * 64, 0:TW], in0=PS[ob][0:64, 0:TW], in1=RC[64:128, 0:TW], op=ALU.mult),
                                    reads=[P(ob), "RC"], writes=[ares])

                        qk_exp(0)
                        for idx in range(len(glist)):
                            if idx + 1 < len(glist):
                                qk_exp(idx + 1)
                            pv_norm(idx)
                        S.stage("att")
                        for bb in range(TW // 128):
                            b = t * (TW // 128) + bb
                            terms = [(at[:, bb * 128:(bb + 1) * 128], WR[wo_a], [ares, ("WR", wo_a)])]
                            if pair == 0:
                                for c in range(4):
                                    terms.append((CT[:, c, b * 128:(b + 1) * 128], WR[conv_rows[c]], [("CT", c, t), ("WR", conv_rows[c])]))
                            bpair = [(6, 7), (4, 5)][bb % 2]
                            for hf in range(2):
                                pb2 = bpair[hf]
                                for ti, (lt, rw, rs) in enumerate(terms):
                                    S.add('pe', lambda e, pb2=pb2, lt=lt, rw=rw, hf=hf, ti=ti, nt_=len(terms): e.matmul(
                                        PS[pb2][:, 0:512], lt, rw[:, hf * 512:(hf + 1) * 512], start=(ti == 0), stop=(ti == nt_ - 1)),
                                        reads=rs, writes=[P(pb2)])
                            x_update(b, pair == 0, bpair)
                else:
                    sample_attention(l, pair, s_qk, s_v, wo_a, conv_rows, KT, VA, QT, PT, AT, CT, RC, OS, FA, FBb, FC, PKt, PQt, KC, CL,
                                     decay_chain, proj, x_update)

            S.stage("pairs")
            load_ln(l, 0)
            layer_norm_all()
            S.stage("ln1")

            build_xt()
            S.barrier()
            arena.reset()
            A2 = arena.get(6 * TT, BF16).rearrange("p (c t) -> p c t", c=6)
            GT = [arena.get(TW + 8, F32) for _ in range(2)]
            T2 = [arena.get(512, F32) for _ in range(2)]
            T3 = [arena.get(512, F32) for _ in range(2)]
            for gi, (j0, j1) in enumerate(FFN_GROUPS):
                for j in range(j0, j1):
                    jl = j - j0
                    s_up = load_wa(l, 10 + j)
                    for t in range(NT):
                        base = 2 * ((j * NT + t) % 3)
                        pg, pv = base, base + 1
                        proj(pg, s_up, 0, 128, t, ("WA", s_up))
                        proj(pv, s_up, 128, 128, t, ("WA", s_up))
                        gb_ = GT[t % 2]
                        cur, m1, m2, halo = conv_views(gb_)
                        t2 = T2[t % 2]
                        t3 = T3[t % 2]
                        S.add('act', lambda e, pg=pg, cur=cur: e.activation(out=cur, in_=psv(pg), func=AF.Copy),
                              reads=[P(pg)], writes=[("GT", t % 2)])
                        if isP:
                            if t == 0:
                                S.add('dve', lambda e, gb_=gb_: e.memset(gb_[:, 0:2], 0.0), writes=[("GT", t % 2)])
                            else:
                                pu = GT[(t - 1) % 2]
                                S.add('dve', lambda e, gb_=gb_, pu=pu: e.tensor_copy(out=gb_[:, 0:2], in_=pu[:, TW:TW + 2]),
                                      reads=[("GT", (t - 1) % 2)], writes=[("GT", t % 2)])
                        else:
                            o0 = ((l * NCH + j) * 4) * 2
                            S.add('dve', lambda e, halo=halo, o0=o0: e.tensor_copy(
                                out=halo, in_=SFFN[:, o0:o0 + 8].rearrange("p (s c) -> p s c", s=4)),
                                reads=["SFFN"], writes=[("GT", t % 2)])
                        wi = (l * NCH + j) * 3
                        bi = l * NCH + j
                        S.add('act', lambda e, pg=pg, t2=t2, wi=wi, bi=bi: e.activation(
                            out=sbv(t2[:, 0:TW]), in_=psv(pg), func=AF.Identity, bias=FB[:, bi:bi + 1], scale=FW[:, wi + 2:wi + 3]),
                            reads=[P(pg), "FW", "FB"], writes=[("T2", t % 2)])
                        S.add('dve', lambda e, m1=m1, t2=t2, wi=wi: e.scalar_tensor_tensor(
                            out=sbv(t2[:, 0:TW]), in0=m1, scalar=FW[:, wi + 1:wi + 2], in1=sbv(t2[:, 0:TW]), op0=ALU.mult, op1=ALU.add),
                            reads=[("GT", t % 2), "FW"], writes=[("T2", t % 2)])
                        S.add('dve', lambda e, m2=m2, t2=t2, wi=wi: e.scalar_tensor_tensor(
                            out=sbv(t2[:, 0:TW]), in0=m2, scalar=FW[:, wi:wi + 1], in1=sbv(t2[:, 0:TW]), op0=ALU.mult, op1=ALU.add),
                            reads=[("GT", t % 2), "FW"], writes=[("T2", t % 2)])
                        S.add('act', lambda e, t2=t2, t3=t3: e.activation(out=t3[:, 0:TW], in_=t2[:, 0:TW], func=AF.Silu),
                              reads=[("T2", t % 2)], writes=[("T3", t % 2)])
                        S.add('dve', lambda e, pv=pv, t3=t3, jl=jl, t=t: e.tensor_tensor(
                            out=A2[:, jl, t * TW:(t + 1) * TW], in0=PS[pv][:, 0:TW], in1=t3[:, 0:TW], op=ALU.mult),
                            reads=[P(pv), ("T3", t % 2)], writes=[("A2", jl, t)])
                        if isP and t == NT - 1:
                            S.add('sp', lambda e, gb_=gb_, j=j: e.dma_start(out=fcd[l, n, j], in_=gb_[:, TW:TW + 2]),
                                  reads=[("GT", t % 2)], writes=[], dma=True)
                        if not isP:
                            v = gb_[:, 0:4 * 34].rearrange("p (s c) -> p s c", s=4)
                            for sq in range(4):
                                S.add('sp', lambda e, v=v, j=j, sq=sq: e.dma_start(out=fcd[l, 4 + sq, j], in_=v[:, sq, 32:34]),
                                      reads=[("GT", t % 2)], writes=[], dma=True)
                rows = [load_wr(wdn[l, j]) for j in range(j0, j1)]
                for b in range(NB):
                    t = b * 128 // TW
                    bpair = [(6, 7), (4, 5), (2, 3), (0, 1)][b % 4]
                    for hf in range(2):
                        pb2 = bpair[hf]
                        for ti, j in enumerate(range(j0, j1)):
                            jl = j - j0
                            S.add('pe', lambda e, pb2=pb2, jl=jl, b=b, r=rows[ti], hf=hf, ti=ti, n_=j1 - j0: e.matmul(
                                PS[pb2][:, 0:512], A2[:, jl, b * 128:(b + 1) * 128], WR[r][:, hf * 512:(hf + 1) * 512],
                                start=(ti == 0), stop=(ti == n_ - 1)),
                                reads=[("A2", jl, t), ("WR", rows[ti])], writes=[P(pb2)])
                    x_update(b, gi == 0, bpair)
            load_ln(l, 1)
            layer_norm_all()

        for b in range(NB):
            dst = yp[n, b * 128:(b + 1) * 128, :] if isP else ys
            S.add('sp', lambda e, b=b, d=dst: e.dma_start(out=d, in_=X[:, b, :]), reads=[("X", b)], writes=[], dma=True)

    def sample_attention(l, pair, s_qk, s_v, wo_a, conv_rows, KT, VA, QT, PT, AT, CT, RC, OS, FA, FBb, FC, PKt, PQt, KC, CL,
                         decay_chain, proj, x_update):
        TW = 128
        qt = QT[0]
        proj(6, s_qk, 0, 128, 0, ("WA", s_qk))
        S.add('act', lambda e: e.activation(out=qt[0:64, 0, :], in_=PS[6][0:64, 0:128], func=AF.Copy), reads=[P(6)], writes=["QTs"])
        S.add('act', lambda e: e.activation(out=qt[0:64, 1, :], in_=PS[6][64:128, 0:128], func=AF.Copy), reads=[P(6)], writes=["QTs"])
        at = AT[0]
        for sq in range(4):
            S.add('pool', lambda e, sq=sq: e.dma_start(
                out=KC[:], in_=ck[l, sq, :, pair * 128:(pair + 1) * 128].rearrange("(b p) c -> p b c", p=128)),
                writes=["KC"], dma=True, force_barrier=True)
            for hl in range(2):
                S.add('pool', lambda e, sq=sq, hl=hl: e.dma_start(
                    out=VA[:, 0:16, hl, 0:64],
                    in_=cv[l, sq, :, pair * 128 + hl * 64:pair * 128 + (hl + 1) * 64].rearrange("(b p) c -> p b c", p=128)),
                    writes=[("VAs", hl)], dma=True, force_barrier=True)
            for g in range(4):
                pb = g % 2
                for i in range(4):
                    b = g * 4 + i
                    S.add('pe', lambda e, pb=pb, i=i, b=b: e.matmul(
                        PS[pb][:, i * 128:(i + 1) * 128], KC[:, b, :], IDB[:], start=True, stop=True),
                        reads=["KC", "IDB"], writes=[P(pb)])
                S.add('act', lambda e, pb=pb, g=g: e.activation(out=KT[0:64, 0, g * 512:(g + 1) * 512], in_=PS[pb][0:64, 0:512], func=AF.Copy),
                      reads=[P(pb)], writes=[("KTs", 0)])
                S.add('act', lambda e, pb=pb, g=g: e.activation(out=KT[0:64, 1, g * 512:(g + 1) * 512], in_=PS[pb][64:128, 0:512], func=AF.Copy),
                      reads=[P(pb)], writes=[("KTs", 1)])
            proj(7, s_qk, 128, 128, 0, ("WA", s_qk))
            S.add('act', lambda e, sq=sq: e.activation(out=KT[0:64, 0, T:T + 32], in_=PS[7][0:64, sq * 32:(sq + 1) * 32], func=AF.Copy),
                  reads=[P(7)], writes=[("KTs", 0)])
            S.add('act', lambda e, sq=sq: e.activation(out=KT[0:64, 1, T:T + 32], in_=PS[7][64:128, sq * 32:(sq + 1) * 32], func=AF.Copy),
                  reads=[P(7)], writes=[("KTs", 1)])
            for k in range(8):
                S.add('pe', lambda e, k=k, sq=sq: e.matmul(
                    PS[5][0:32, 0:128], XT[:, k, sq * 32:(sq + 1) * 32], WV[s_v][:, k, :], start=(k == 0), stop=(k == 7)),
                    reads=[("XT", 0), ("WV", s_v)], writes=[P(5)])
            vs_ = VST[sq % 2]
            S.add('act', lambda e, vs_=vs_: e.activation(out=vs_[0:32, :], in_=PS[5][0:32, 0:128], func=AF.Copy),
                  reads=[P(5)], writes=[("VST", sq % 2)])
            S.add('sp', lambda e, vs_=vs_, sq=sq: e.dma_start(out=vsd[l, sq, :, pair * 128:(pair + 1) * 128], in_=vs_[0:32, :]),
                  reads=[("VST", sq % 2)], writes=[], dma=True)
            S.add('act', lambda e: e.activation(out=VA[0:32, 16, :, 0:64], in_=PS[5][0:32, 0:128].rearrange("p (h c) -> p h c", h=2), func=AF.Copy),
                  reads=[P(5)], writes=[("VAs", 2)])
            for hl in range(2):
                hh = pair * 2 + hl
                S.add('sp', lambda e, hl=hl, hh=hh, sq=sq: e.dma_start(out=KT[67:70, hl, 0:T + 32], in_=pks[sq, hh, :, :]),
                      reads=["pkd"], writes=[("KTs", hl)], dma=True)
                S.add('sp', lambda e, hl=hl, hh=hh, sq=sq: e.dma_start(out=qt[64:67, hl, sq * 32:(sq + 1) * 32], in_=pqs[sq, hh, :, :]),
                      reads=["pqd"], writes=["QTs"], dma=True)
            for hl in range(2):
                ob = 4
                q_ap = qt[0:70, hl, sq * 32:(sq + 1) * 32]
                for b in range(16):
                    S.add('pe', lambda e, b=b, hl=hl, q_ap=q_ap: e.matmul(
                        PS[2][:, b * 32:(b + 1) * 32], KT[0:70, hl, b * 128:(b + 1) * 128], q_ap, start=True, stop=True),
                        reads=[("KTs", hl), "QTs", "KT"], writes=[P(2)])
                S.add('pe', lambda e, hl=hl, q_ap=q_ap: e.matmul(
                    PS[3][0:32, 0:32], KT[0:70, hl, T:T + 32], q_ap, start=True, stop=False),
                    reads=[("KTs", hl), "QTs", "KT"], writes=[P(3)])
                S.add('pe', lambda e: e.matmul(PS[3][0:32, 0:32], IDB[0:32, 0:32], MASK[0:32, 0:32], start=False, stop=True),
                      reads=["IDB", "MASK"], writes=[P(3)])
                S.add('act', lambda e: e.activation(out=PT[0][:, 0:512], in_=PS[2][:, 0:512], func=AF.Exp, scale=0.125),
                      reads=[P(2)], writes=[("PT", 0)])
                S.add('act', lambda e: e.activation(out=PT[1][0:32, 0:32], in_=PS[3][0:32, 0:32], func=AF.Exp, scale=0.125),
                      reads=[P(3)], writes=[("PT", 1)])
                for b in range(16):
                    S.add('pe', lambda e, b=b, hl=hl: e.matmul(
                        PS[ob][:, 0:32], VA[:, b, hl, :], PT[0][:, b * 32:(b + 1) * 32], start=(b == 0), stop=False),
                        reads=[("PT", 0), ("VAs", hl), "VA"], writes=[P(ob)])
                S.add('pe', lambda e, hl=hl: e.matmul(
                    PS[ob][:, 0:32], VA[0:32, 16, hl, :], PT[1][0:32, 0:32], start=False, stop=True),
                    reads=[("PT", 1), ("VAs", 2), "VA"], writes=[P(ob)])
                S.add('dve', lambda e: e.reciprocal(out=RC[64:128, 0:32], in_=PS[ob][64:128, 0:32]), reads=[P(ob)], writes=["RC"])
                S.add('dve', lambda e, hl=hl, sq=sq: e.tensor_tensor(
                    out=at[hl * 64:(hl + 1) * 64, sq * 32:(sq + 1) * 32], in0=PS[ob][0:64, 0:32], in1=RC[64:128, 0:32], op=ALU.mult),
                    reads=[P(ob), "RC"], writes=[("AT", 0)])
        terms = [(at[:, 0:128], WR[wo_a], [("AT", 0), ("WR", wo_a)])]
        if pair == 0:
            for c in range(4):
                terms.append((CT[:, c, 0:128], WR[conv_rows[c]], [("CT", c, 0), ("WR", conv_rows[c])]))
        for hf in range(2):
            pb2 = 6 + hf
            for ti, (lt, rw, rs) in enumerate(terms):
                S.add('pe', lambda e, pb2=pb2, lt=lt, rw=rw, hf=hf, ti=ti, nt_=len(terms): e.matmul(
                    PS[pb2][:, 0:512], lt, rw[:, hf * 512:(hf + 1) * 512], start=(ti == 0), stop=(ti == nt_ - 1)),
                    reads=rs, writes=[P(pb2)])
        x_update(0, pair == 0, (6, 7))

    for n in range(nps):
        S.phase = n
        run_sequence('p', n)
    if do_sample:
        S.phase = 4
        run_sequence('s', 0)

    sem_keys = S.finalize()
    with ExitStack() as es:
        sems = {k: es.enter_context(nc.semaphore("s_" + "_".join(str(x) for x in k))) for k in sem_keys}
        block = es.enter_context(nc.Block())

        @block.tensor
        def _(e):
            S.emit(e, 'pe', sems)

        @block.scalar
        def _(e):
            S.emit(e, 'act', sems)

        @block.vector
        def _(e):
            S.emit(e, 'dve', sems)

        @block.gpsimd
        def _(e):
            S.emit(e, 'pool', sems)

        @block.sync
        def _(e):
            S.emit(e, 'sp', sems)
    return nc


def _prep_weights(w_in, b_f, conv_w, conv_b, w_out, ln1_g, ln1_b, w_up, ffn_conv_w, ffn_conv_b, w_down, ln2_g, ln2_b):
    f = np.float32
    w_in = np.asarray(w_in, f)
    w_up = np.asarray(w_up, f)

    def unit(mat):
        C = mat.shape[1]
        return mat.reshape(8, 128, C).transpose(1, 0, 2)

    wa = np.empty((L, 32, 128, 8, 256), f)
    wv = np.empty((L, 4, 128, 8, 128), f)
    wf = np.empty((L, 128, 8, 8), f)
    for l in range(L):
        W = w_in[l]
        q, k, v = W[:, 0:512], W[:, 512:1024], W[:, 1024:1536]
        fl = W[:, 1536:1544]
        gb, gc, hh = W[:, 1544:2056], W[:, 2056:2568], W[:, 2568:3080]
        for p in range(4):
            wa[l, p, :, :, 0:128] = unit(q[:, p * 128:(p + 1) * 128])
            wa[l, p, :, :, 128:256] = unit(k[:, p * 128:(p + 1) * 128])
            wv[l, p] = unit(v[:, p * 128:(p + 1) * 128])
        for c in range(4):
            wa[l, 4 + c, :, :, 0:128] = unit(gb[:, c * 128:(c + 1) * 128])
            wa[l, 4 + c, :, :, 128:256] = unit(gc[:, c * 128:(c + 1) * 128])
        for c2 in range(2):
            wa[l, 8 + c2, :, :, 0:128] = unit(hh[:, (2 * c2) * 128:(2 * c2 + 1) * 128])
            wa[l, 8 + c2, :, :, 128:256] = unit(hh[:, (2 * c2 + 1) * 128:(2 * c2 + 2) * 128])
        for j in range(NCH):
            wa[l, 10 + j, :, :, 0:128] = unit(w_up[l][:, j * 128:(j + 1) * 128])
            wa[l, 10 + j, :, :, 128:256] = unit(w_up[l][:, DFF + j * 128:DFF + (j + 1) * 128])
        wf[l] = unit(fl)
    wo = np.ascontiguousarray(np.asarray(w_out, f).reshape(L, 8, 128, D))
    wdn = np.ascontiguousarray(np.asarray(w_down, f).reshape(L, NCH, 128, D))
    lnrep = np.empty((L, 2, 128, 2 * D), f)
    lnrep[:, 0, :, 0:D] = np.asarray(ln1_g, f)[:, None, :]
    lnrep[:, 0, :, D:] = np.asarray(ln1_b, f)[:, None, :]
    lnrep[:, 1, :, 0:D] = np.asarray(ln2_g, f)[:, None, :]
    lnrep[:, 1, :, D:] = np.asarray(ln2_b, f)[:, None, :]
    cwd = np.ascontiguousarray(np.asarray(conv_w, f).reshape(L, 3, 4, 128).transpose(3, 0, 2, 1)).reshape(128, L * 4 * 3)
    cbd = np.ascontiguousarray(np.asarray(conv_b, f).reshape(L, 4, 128).transpose(2, 0, 1)).reshape(128, L * 4)
    fwd = np.ascontiguousarray(np.asarray(ffn_conv_w, f).reshape(L, 3, NCH, 128).transpose(3, 0, 2, 1)).reshape(128, L * NCH * 3)
    fbd = np.ascontiguousarray(np.asarray(ffn_conv_b, f).reshape(L, NCH, 128).transpose(2, 0, 1)).reshape(128, L * NCH)
    bfd = np.ascontiguousarray(np.asarray(b_f, f).T)
    idfd = np.eye(128, dtype=f)
    idbd = np.eye(128).astype(ml_dtypes.bfloat16)
    kk = np.arange(128)[:, None]
    qq = np.arange(128)[None, :]
    maskd = np.where(kk <= qq, 0.0, NEG).astype(ml_dtypes.bfloat16)
    return dict(wa=wa.reshape(L, 32, 128, 8 * 256), wv=wv.reshape(L, 4, 128, 8 * 128), wf=wf.reshape(L, 128, 64), wo=wo, wdn=wdn,
                lnrep=lnrep, cwd=cwd, cbd=cbd, fwd=fwd, fbd=fbd, bfd=bfd, idfd=idfd, idbd=idbd, maskd=maskd)


_NC_CACHE = {}


def kernel(x_prompt, x_sample, cache_k, cache_v, cache_logf, state_mix_conv, state_ffn_conv,
           w_in, b_f, conv_w, conv_b, w_out, ln1_g, ln1_b, w_up, ffn_conv_w, ffn_conv_b, w_down, ln2_g, ln2_b):
    f = np.float32
    nps = int(os.environ.get("MK_NPS", NPS))
    nl = int(os.environ.get("MK_NL", L))
    do_s = int(os.environ.get("MK_SAMPLE", 1)) == 1
    ncores = 8
    wd = _prep_weights(w_in, b_f, conv_w, conv_b, w_out, ln1_g, ln1_b, w_up, ffn_conv_w, ffn_conv_b, w_down, ln2_g, ln2_b)
    x_prompt = np.asarray(x_prompt, f)
    x_sample = np.asarray(x_sample, f)
    cache_k = np.asarray(cache_k, f)
    cache_v = np.asarray(cache_v, f)
    cache_logf = np.asarray(cache_logf, f)
    smc = np.asarray(state_mix_conv, f)
    sfc = np.asarray(state_ffn_conv, f)
    in_maps = []
    for c in range(ncores):
        sl = slice(c * 4, (c + 1) * 4)
        m = dict(wd)
        m["xp"] = np.ascontiguousarray(x_prompt[sl])
        m["xs"] = np.ascontiguousarray(x_sample[sl].reshape(128, D))
        m["ck"] = np.ascontiguousarray(cache_k[:, sl].reshape(L, 4, T, 512))
        m["cv"] = np.ascontiguousarray(cache_v[:, sl].reshape(L, 4, T, 512))
        m["clf"] = np.ascontiguousarray(cache_logf[:, sl])
        m["smix"] = np.ascontiguousarray(smc[:, sl].reshape(L, 4, 2, 4, 128).transpose(4, 0, 3, 1, 2)).reshape(128, -1)
        m["sffn"] = np.ascontiguousarray(sfc[:, sl].reshape(L, 4, 2, NCH, 128).transpose(4, 0, 3, 1, 2)).reshape(128, -1)
        in_maps.append(m)
    key = (nps, nl, do_s)
    if key not in _NC_CACHE:
        _NC_CACHE[key] = build_program(nps, nl, do_s)
    nc = _NC_CACHE[key]
    res = run_bass_kernel_spmd(nc, in_maps, core_ids=list(range(ncores)))
    R = res.results
    B = 32
    y_prompt = np.empty((B, T, D), f)
    y_sample = np.empty((B, 32, D), f)
    k_prompt = np.empty((L, B, T, H, 64), f)
    v_prompt = np.empty((L, B, T, H, 64), f)
    logf_prompt = np.empty((L, B, T, H), f)
    mix_p = np.empty((L, B, 2, 512), f)
    ffn_p = np.empty((L, B, 2, DFF), f)
    k_sample = np.empty((L, B, 32, H, 64), f)
    v_sample = np.empty((L, B, 32, H, 64), f)
    logf_sample = np.empty((L, B, 32, H), f)
    mix_s = np.empty((L, B, 2, 512), f)
    ffn_s = np.empty((L, B, 2, DFF), f)
    for c in range(ncores):
        r = R[c]
        sl = slice(c * 4, (c + 1) * 4)
        y_prompt[sl] = r["yp"]
        y_sample[sl] = r["ys"].reshape(4, 32, D)
        k_prompt[:, sl] = r["kpd"].reshape(L, 4, H, 64, T).transpose(0, 1, 4, 2, 3)
        v_prompt[:, sl] = r["vpd"].reshape(L, 4, T, H, 64)
        logf_prompt[:, sl] = r["lfpd"].transpose(0, 1, 3, 2)
        mc = r["mcd"]
        fc = r["fcd"]
        mix_p[:, sl] = mc[:, 0:4].transpose(0, 1, 4, 2, 3).reshape(L, 4, 2, 512)
        mix_s[:, sl] = mc[:, 4:8].transpose(0, 1, 4, 2, 3).reshape(L, 4, 2, 512)
        ffn_p[:, sl] = fc[:, 0:4].transpose(0, 1, 4, 2, 3).reshape(L, 4, 2, DFF)
        ffn_s[:, sl] = fc[:, 4:8].transpose(0, 1, 4, 2, 3).reshape(L, 4, 2, DFF)
        k_sample[:, sl] = r["ksd"].reshape(L, H, 64, 4, 32).transpose(0, 3, 4, 1, 2)
        v_sample[:, sl] = r["vsd"].reshape(L, 4, 32, H, 64)
        logf_sample[:, sl] = r["lfsd"].reshape(L, H, 4, 32).transpose(0, 2, 3, 1)
    return (y_prompt, y_sample, k_prompt, v_prompt, logf_prompt, mix_p, ffn_p,
            k_sample, v_sample, logf_sample, mix_s, ffn_s)
```

```python
import os
import types
import numpy as np
import ml_dtypes
from contextlib import ExitStack
import concourse.bass as bass
import concourse.mybir as mybir
from concourse.bass_utils import run_bass_kernel_spmd

F32 = mybir.dt.float32
BF16 = mybir.dt.bfloat16
U8 = mybir.dt.uint8
AF = mybir.ActivationFunctionType
ALU = mybir.AluOpType

L = 4
D = 1024
H = 8
T = 2048
DFF = 2816
NCH = 22
NPS = 4
ALPHA = float(8 ** 0.25)
EPS = 1e-5
NEG = -30000.0
NS = 8
FFN_GROUPS = [(0, 6), (6, 12), (12, 17), (17, 22)]


def _freeze(fn):
    if fn is None or fn.__closure__ is None:
        return fn
    cells = []
    for c in fn.__closure__:
        try:
            cells.append(types.CellType(c.cell_contents))
        except ValueError:
            cells.append(c)
    return types.FunctionType(fn.__code__, fn.__globals__, fn.__name__, fn.__defaults__, tuple(cells))


class Sched:
    def __init__(self):
        self.ops = []
        self.last_w = {}
        self.readers = {}
        self.phase = 0
        self.dma_rr = {'sp': 0, 'pool': 0}
        self.dma_last = {}
        self.eng_last = {}
        self.barrier_deps = None
        self.barrier_seen = set()

    def stage(self, name):
        if os.environ.get("MK_STOP", "") == name:
            self.stopped = True

    def add(self, eng, fn, reads=(), writes=(), dma=False, force_barrier=False):
        if getattr(self, 'stopped', False):
            return -1
        i = len(self.ops)
        ps_reads = [r for r in reads if isinstance(r, tuple) and r[0] == 'ps']
        if ps_reads:
            reads = [r for r in reads if not (isinstance(r, tuple) and r[0] == 'ps')]
            writes = list(writes) + ps_reads
        deps = set()
        for r in reads:
            w = self.last_w.get(r)
            if w is not None:
                deps.add(w)
        for w_ in writes:
            w = self.last_w.get(w_)
            if w is not None:
                deps.add(w)
            deps.update(self.readers.get(w_, {}).values())
        op = dict(eng=eng, fn=_freeze(fn), deps=deps, dma=dma, phase=self.phase, signal=dma, sig=None)
        if dma:
            k = self.dma_rr[eng]
            self.dma_rr[eng] += 1
            s = ('d', eng, k % NS)
            prev = self.dma_last.get(s)
            if prev is not None:
                deps.add(prev)
            self.dma_last[s] = i
            op['dsem'] = s
            rkey = s
        else:
            rkey = eng
        if self.barrier_deps is not None and force_barrier:
            deps.update(self.barrier_deps)
        elif self.barrier_deps is not None and eng != 'pool' and eng not in self.barrier_seen:
            deps.update(self.barrier_deps)
            self.barrier_seen.add(eng)
        deps.discard(i)
        for r in reads:
            self.readers.setdefault(r, {})[rkey] = i
        for w_ in writes:
            self.last_w[w_] = i
            self.readers[w_] = {}
        self.eng_last[eng] = i
        self.ops.append(op)
        return i

    def barrier(self):
        deps = set()
        for eng in ('pe', 'act', 'dve'):
            if eng in self.eng_last:
                deps.add(self.eng_last[eng])
        for s, i in self.dma_last.items():
            if s[1] == 'sp':
                deps.add(i)
        self.barrier_deps = deps
        self.barrier_seen = set()

    def finalize(self):
        self.ops.append(dict(eng='sp', fn=None, deps=set(self.dma_last.values()), dma=False,
                             phase=self.phase, signal=False, sig=None))
        for op in self.ops:
            for d in op['deps']:
                p = self.ops[d]
                if op['eng'] == 'pe' and p['eng'] == 'pe' and not p['dma']:
                    continue
                p['signal'] = True
        cnt = {}
        for op in self.ops:
            if op['dma']:
                s = op['dsem']
                cnt[s] = cnt.get(s, 0) + 1
                op['sig'] = (s, 16 * cnt[s])
            elif op['signal']:
                s = (op['eng'], op['phase'])
                cnt[s] = cnt.get(s, 0) + 1
                op['sig'] = (s, cnt[s])
        return sorted(set(op['sig'][0] for op in self.ops if op['sig'] is not None), key=str)

    def emit(self, e, eng, sems):
        waited = {}
        ops = self.ops
        for op in ops:
            if op['eng'] != eng:
                continue
            need = {}
            for d in op['deps']:
                p = ops[d]
                if eng == 'pe' and p['eng'] == 'pe' and not p['dma']:
                    continue
                if p['sig'] is None:
                    raise RuntimeError("dep on non-signaling op")
                s, v = p['sig']
                if need.get(s, 0) < v:
                    need[s] = v
            for s, v in need.items():
                if waited.get(s, 0) < v:
                    e.wait_ge(sems[s], v)
                    waited[s] = v
            if op['fn'] is not None:
                ins = op['fn'](e)
                if op['sig'] is not None:
                    ins.then_inc(sems[op['sig'][0]], 16 if op['dma'] else 1)


def build_program(nps=NPS, nlayers=L, do_sample=True):
    nc = bass.Bass("TRN2", target_bir_lowering=False)
    S = Sched()

    def din(name, shape, dt=F32):
        return nc.dram_tensor(name, list(shape), dt, kind="ExternalInput").ap()

    def dout(name, shape, dt=F32):
        return nc.dram_tensor(name, list(shape), dt, kind="ExternalOutput").ap()

    xp = din("xp", [NPS, T, D])
    xs = din("xs", [128, D])
    ck = din("ck", [L, 4, T, 512])
    cv = din("cv", [L, 4, T, 512])
    clf = din("clf", [L, 4, T, 8])
    smix = din("smix", [128, L * 4 * 4 * 2])
    sffn = din("sffn", [128, L * NCH * 4 * 2])
    wa = din("wa", [L, 32, 128, 8 * 256])
    wv = din("wv", [L, 4, 128, 8 * 128])
    wf = din("wf", [L, 128, 8 * 8])
    wo = din("wo", [L, 8, 128, D])
    wdn = din("wdn", [L, NCH, 128, D])
    lnrep = din("lnrep", [L, 2, 128, 2 * D])
    cwd = din("cwd", [128, L * 4 * 3])
    cbd = din("cbd", [128, L * 4])
    fwd = din("fwd", [128, L * NCH * 3])
    fbd = din("fbd", [128, L * NCH])
    bfd = din("bfd", [8, L])
    idfd = din("idfd", [128, 128])
    idbd = din("idbd", [128, 128], BF16)
    maskd = din("maskd", [128, 128], BF16)

    yp = dout("yp", [NPS, T, D])
    ys = dout("ys", [128, D])
    kpd = dout("kpd", [L, NPS, 512, T])
    vpd = dout("vpd", [L, NPS, T, 512])
    lfpd = dout("lfpd", [L, NPS, 8, T])
    mcd = dout("mcd", [L, 8, 4, 128, 2])
    fcd = dout("fcd", [L, 8, NCH, 128, 2])
    ksd = dout("ksd", [L, 512, 128])
    vsd = dout("vsd", [L, 4, 32, 512])
    lfsd = dout("lfsd", [L, 8, 128])
    pkd = nc.dram_tensor("pkd", [8, 3, T], BF16).ap()
    pqd = nc.dram_tensor("pqd", [8, 3, T], BF16).ap()
    pks = nc.dram_tensor("pks", [4, 8, 3, T + 32], BF16).ap()
    pqs = nc.dram_tensor("pqs", [4, 8, 3, 32], BF16).ap()

    def sb(name, shape, dt):
        return nc.alloc_sbuf_tensor(name, list(shape), dt)

    X = sb("X", [128, 16, D], F32)
    XT = sb("XT", [128, 8, T], BF16)
    IDF = sb("IDF", [128, 128], F32)
    IDB = sb("IDB", [128, 128], BF16)
    MASK = sb("MASK", [128, 128], BF16)
    ONES = sb("ONES", [8, 512], F32)
    CW = sb("CW", [128, L * 4 * 3], F32)
    CB = sb("CB", [128, L * 4], F32)
    FW = sb("FW", [128, L * NCH * 3], F32)
    FB = sb("FB", [128, L * NCH], F32)
    BFs = sb("BFs", [8, L], F32)
    NBF = sb("NBF", [8, L], F32)
    SMIX = sb("SMIX", [128, L * 4 * 4 * 2], F32)
    SFFN = sb("SFFN", [128, L * NCH * 4 * 2], F32)
    EPSC = sb("EPSC", [128, 1], F32)
    NWA = 4
    WA = [sb(f"WA{i}", [128, 8, 256], BF16) for i in range(NWA)]
    WV = [sb(f"WV{i}", [128, 8, 128], BF16) for i in range(2)]
    WF = sb("WF", [128, 8, 8], BF16)
    NWR = 6
    WR = [sb(f"WR{i}", [128, D], BF16) for i in range(NWR)]
    GBt = sb("GBt", [128, 2 * D], F32)
    KST = [sb(f"KST{i}", [128, 512], F32) for i in range(2)]
    VST = [sb(f"VST{i}", [128, 128], F32) for i in range(2)]
    SST = [sb(f"SST{i}", [128, 2], F32) for i in range(4)]
    LNS = sb("LNS", [128, 16, 16], F32)
    LNR = sb("LNR", [128, 16, 4], F32)
    ARENA_BYTES = 55 * 1024
    AR = sb("AR", [128, ARENA_BYTES], U8)

    class Arena:
        def __init__(self):
            self.off = 0

        def reset(self):
            self.off = 0

        def get(self, nfree, dt):
            nbytes = nfree * (2 if dt == BF16 else 4)
            nbytes = (nbytes + 31) // 32 * 32
            v = AR[:, self.off:self.off + nbytes].bitcast(dt)
            self.off += nbytes
            assert self.off <= ARENA_BYTES, self.off
            return v

    arena = Arena()
    PS = [nc.alloc_psum_tensor(f"ps{i}", [128, 512], F32) for i in range(8)]

    def P(i):
        return ("ps", i)

    wa_rr = [0]
    wr_rr = [0]
    wv_rr = [0]

    def load_wa(l, unit):
        slot = wa_rr[0] % NWA
        wa_rr[0] += 1
        S.add('pool', lambda e, s=slot, l=l, u=unit: e.dma_start(
            out=WA[s][:].rearrange("p k c -> p (k c)"), in_=wa[l, u]), writes=[("WA", slot)], dma=True)
        return slot

    def load_wv(l, pair):
        slot = wv_rr[0] % 2
        wv_rr[0] += 1
        S.add('pool', lambda e, s=slot, l=l, u=pair: e.dma_start(
            out=WV[s][:].rearrange("p k c -> p (k c)"), in_=wv[l, u]), writes=[("WV", slot)], dma=True)
        return slot

    def load_wr(src_ap):
        slot = wr_rr[0] % NWR
        wr_rr[0] += 1
        S.add('pool', lambda e, s=slot, a=src_ap: e.dma_start(out=WR[s][:], in_=a), writes=[("WR", slot)], dma=True)
        return slot

    def load_ln(l, which):
        S.add('sp', lambda e, l=l, w=which: e.dma_start(out=GBt[:], in_=lnrep[l, w]), writes=["GBt"], dma=True)

    for dst, src, nm in [(IDF, idfd, "IDF"), (IDB, idbd, "IDB"), (MASK, maskd, "MASK"), (CW, cwd, "CW"), (CB, cbd, "CB"),
                         (FW, fwd, "FW"), (FB, fbd, "FB"), (BFs, bfd, "BFs"), (SMIX, smix, "SMIX"), (SFFN, sffn, "SFFN")]:
        S.add('sp', lambda e, d=dst, s=src: e.dma_start(out=d[:], in_=s), writes=[nm], dma=True)
    S.add('dve', lambda e: e.memset(ONES[:], 1.0), writes=["ONES"])
    S.add('dve', lambda e: e.memset(EPSC[:], EPS), writes=["EPSC"])
    S.add('dve', lambda e: e.tensor_scalar(out=NBF[:], in0=BFs[:], scalar1=-1.0, scalar2=None, op0=ALU.mult),
          reads=["BFs"], writes=["NBF"])

    def run_sequence(kind, n):
        isP = kind == 'p'
        TT = T if isP else 128
        TW = 512 if isP else 128
        NT = TT // TW
        NB = TT // 128
        sidx = n if isP else None

        for b in range(NB):
            src = xp[n, b * 128:(b + 1) * 128, :] if isP else xs
            S.add('sp', lambda e, b=b, s=src: e.dma_start(out=X[:, b, :], in_=s), writes=[("X", b)], dma=True)

        def build_xt():
            bank = [0]
            for b in range(NB):
                for half in range(2):
                    pb = bank[0] % 8
                    bank[0] += 1
                    for i in range(4):
                        k = half * 4 + i
                        S.add('pe', lambda e, pb=pb, i=i, b=b, k=k: e.transpose(
                            PS[pb][:, i * 128:(i + 1) * 128], X[:, b, k * 128:(k + 1) * 128], IDF[:]),
                            reads=[("X", b), "IDF"], writes=[P(pb)])
                    S.add('act', lambda e, pb=pb, b=b, half=half: e.activation(
                        out=XT[:, half * 4:(half + 1) * 4, b * 128:(b + 1) * 128],
                        in_=PS[pb][:].rearrange("p (a c) -> p a c", a=4), func=AF.Copy),
                        reads=[P(pb)], writes=[("XT", b)])

        def xt_res(t):
            return [("XT", b) for b in range(t * TW // 128, (t + 1) * TW // 128)]

        def proj(ps_i, wslot, col0, ncols, t, wres, wt=None):
            W = WA[wslot] if wt is None else wt
            for k in range(8):
                S.add('pe', lambda e, k=k, W=W: e.matmul(
                    PS[ps_i][0:ncols, 0:TW], W[:, k, col0:col0 + ncols], XT[:, k, t * TW:(t + 1) * TW],
                    start=(k == 0), stop=(k == 7)),
                    reads=[wres] + xt_res(t), writes=[P(ps_i)])

        def x_update(b, first, banks):
            for hf in range(2):
                xs_ = X[:, b, hf * 512:(hf + 1) * 512]
                pb = banks[hf]
                if first:
                    S.add('dve', lambda e, xs_=xs_, pb=pb: e.scalar_tensor_tensor(
                        out=xs_, in0=xs_, scalar=ALPHA, in1=PS[pb][:, 0:512], op0=ALU.mult, op1=ALU.add),
                        reads=[P(pb)], writes=[("X", b)])
                else:
                    S.add('dve', lambda e, xs_=xs_, pb=pb: e.tensor_tensor(
                        out=xs_, in0=xs_, in1=PS[pb][:, 0:512], op=ALU.add),
                        reads=[P(pb)], writes=[("X", b)])

        def layer_norm_all():
            GRP = 4
            for g0 in range(0, NB, GRP):
                blks = list(range(g0, min(NB, g0 + GRP)))
                for b in blks:
                    for hf in range(2):
                        S.add('dve', lambda e, hf=hf, b=b: e.bn_stats(out=LNS[:, b, hf * 6:(hf + 1) * 6], in_=X[:, b, hf * 512:(hf + 1) * 512]),
                              reads=[("X", b)], writes=[("LNS", b)])
                    S.add('dve', lambda e, b=b: e.bn_aggr(out=LNS[:, b, 12:14], in_=LNS[:, b, 0:12]), reads=[("LNS", b)], writes=[("LNS", b)])
                for b in blks:
                    S.add('act', lambda e, b=b: e.activation(out=LNR[:, b, 0:1], in_=LNS[:, b, 13:14], func=AF.Sqrt, bias=EPSC[:, 0:1], scale=1.0),
                          reads=[("LNS", b), "EPSC"], writes=[("LNR", b)])
                for b in blks:
                    S.add('dve', lambda e, b=b: e.reciprocal(out=LNR[:, b, 1:2], in_=LNR[:, b, 0:1]), reads=[("LNR", b)], writes=[("LNR", b)])
                for b in blks:
                    S.add('dve', lambda e, b=b: e.scalar_tensor_tensor(out=X[:, b, :], in0=X[:, b, :], scalar=LNS[:, b, 12:13], in1=GBt[:, 0:D],
                                                                      op0=ALU.subtract, op1=ALU.mult),
                          reads=["GBt", ("LNS", b)], writes=[("X", b)])
                    S.add('dve', lambda e, b=b: e.scalar_tensor_tensor(out=X[:, b, :], in0=X[:, b, :], scalar=LNR[:, b, 1:2], in1=GBt[:, D:2 * D],
                                                                      op0=ALU.mult, op1=ALU.add),
                          reads=["GBt", ("LNR", b)], writes=[("X", b)])

        def seg(ap, w):
            return ap

        for l in range(nlayers):
            build_xt()
            S.stage("xt")
            S.barrier()
            arena.reset()
            KT = arena.get(2 * (T if isP else T + 32), BF16).rearrange("p (h t) -> p h t", h=2)
            NVB = NB if isP else 17
            VA = arena.get(NVB * 2 * 128, BF16).rearrange("p (b h c) -> p b h c", b=NVB, h=2)
            QT = [arena.get(2 * TW, BF16).rearrange("p (h t) -> p h t", h=2) for _ in range(2)]
            PT = [arena.get(1024, BF16) for _ in range(2)]
            AT = [arena.get(TW, BF16) for _ in range(2)]
            ct_off = arena.off
            CT = arena.get(4 * TT, BF16).rearrange("p (c t) -> p c t", c=4)
            UT = [arena.get(TW + 8, F32) for _ in range(2)]
            HS = arena.get(512, F32)
            T1 = arena.get(512, F32)
            RC = arena.get(512, F32)
            OS = arena.get(512, F32)
            end_off = arena.off
            if isP:
                arena.off = ct_off
            FA = arena.get(512, F32)
            FBb = [arena.get(512, F32) for _ in range(2)]
            FC = arena.get(512, F32)
            PKt = arena.get(3 * 512, BF16).rearrange("p (j t) -> p j t", j=3)
            PQt = arena.get(3 * 512, BF16).rearrange("p (j t) -> p j t", j=3)
            if isP:
                arena.off = end_off
            KC = CL = LPS = None
            if not isP:
                KC = arena.get(16 * 128, BF16).rearrange("p (b c) -> p b c", b=16)
                CL = arena.get(16 * 8, F32).rearrange("p (b c) -> p b c", b=16)
                LPS = arena.get(128, F32)

            S.add('dve', lambda e, KT=KT: e.memset(KT[64:70, :, :], 1.0),
                  writes=["KT", ("KTaug", 0), ("KTaug", 1), ("KTs", 0), ("KTs", 1)])
            for qi, q in enumerate(QT):
                S.add('dve', lambda e, q=q: e.memset(q[64:70, :, :], 1.0), writes=[("QT", qi), "QTs"])
            S.add('dve', lambda e, VA=VA: e.memset(VA[:, :, :, 64:128], 1.0), writes=["VA"])

            S.add('pool', lambda e, l=l: e.dma_start(out=WF[:].rearrange("p k c -> p (k c)"), in_=wf[l]), writes=["WF"], dma=True)

            def decay_chain(src_fa_ready_res, width, carry, slot, kdst, qdst):
                w = width
                cur = FBb[slot]
                ini = 0.0 if carry is None else carry
                S.add('dve', lambda e, cur=cur, ini=ini, w=w: e.tensor_tensor_scan(
                    out=cur[0:8, 0:w], data0=ONES[0:8, 0:w], data1=FA[0:8, 0:w], initial=ini, op0=ALU.mult, op1=ALU.add),
                    reads=["FA", "ONES", ("FB", 1 - slot)], writes=[("FB", slot)])
                S.add('dve', lambda e, cur=cur, w=w: e.tensor_scalar(out=FC[0:8, 0:w], in0=cur[0:8, 0:w], scalar1=8.0, scalar2=None, op0=ALU.mult),
                      reads=[("FB", slot)], writes=["FC"])
                for j in range(3):
                    S.add('dve', lambda e, j=j, w=w: e.tensor_copy(out=PKt[0:8, j, 0:w], in_=FC[0:8, 0:w]), reads=["FC"], writes=["PKt"])
                    if j < 2:
                        S.add('dve', lambda e, j=j, w=w: e.tensor_tensor(out=FC[0:8, 0:w], in0=FC[0:8, 0:w], in1=PKt[0:8, j, 0:w], op=ALU.subtract),
                              reads=["PKt"], writes=["FC"])
                S.add('sp', lambda e, w=w, kdst=kdst: e.dma_start(out=kdst, in_=PKt[0:8, :, 0:w]), reads=["PKt"], writes=["pkd"], dma=True)
                if qdst is not None:
                    S.add('dve', lambda e, w=w: e.tensor_scalar(out=PQt[0:8, :, 0:w], in0=PKt[0:8, :, 0:w], scalar1=-1.0, scalar2=None, op0=ALU.mult),
                          reads=["PKt"], writes=["PQt"])
                    S.add('sp', lambda e, w=w, qdst=qdst: e.dma_start(out=qdst, in_=PQt[0:8, :, 0:w]), reads=["PQt"], writes=["pqd"], dma=True)
                return cur[0:8, w - 1:w]

            def lp_from_psum(pb, w, outdst):
                S.add('act', lambda e, pb=pb, w=w: e.activation(out=FA[0:8, 0:w], in_=PS[pb][0:8, 0:w], func=AF.Exp,
                                                              bias=NBF[:, l:l + 1], scale=-1.0), reads=[P(pb), "NBF"], writes=["FA"])
                S.add('act', lambda e, w=w: e.activation(out=FA[0:8, 0:w], in_=FA[0:8, 0:w], func=AF.Ln, bias=1.0, scale=1.0),
                      reads=["FA"], writes=["FA"])
                S.add('dve', lambda e, w=w: e.tensor_scalar(out=FC[0:8, 0:w], in0=FA[0:8, 0:w], scalar1=-1.0, scalar2=None, op0=ALU.mult),
                      reads=["FA"], writes=["FC"])
                S.add('sp', lambda e, w=w, o=outdst: e.dma_start(out=o, in_=FC[0:8, 0:w]), reads=["FC"], writes=[], dma=True)

            if isP:
                carry = None
                for t in range(NT):
                    pb = t % 4
                    proj(pb, None, 0, 8, t, "WF", wt=WF)
                    lp_from_psum(pb, TW, lfpd[l, n, :, t * TW:(t + 1) * TW])
                    carry = decay_chain(None, TW, carry, t % 2, pkd[:, :, t * TW:(t + 1) * TW], pqd[:, :, t * TW:(t + 1) * TW])
            else:
                proj(0, None, 0, 8, 0, "WF", wt=WF)
                lp_from_psum(0, 128, lfsd[l])
                S.add('dve', lambda e: e.tensor_copy(out=LPS[0:8, 0:128], in_=FA[0:8, 0:128]), reads=["FA"], writes=["LPS"])
                for sq in range(4):
                    S.add('sp', lambda e, sq=sq: e.dma_start(out=CL[:], in_=clf[l, sq].rearrange("(b p) h -> p b h", p=128)),
                          writes=["CL"], dma=True)
                    carry = None
                    for t4 in range(4):
                        pb = t4 % 4
                        for i in range(4):
                            b = t4 * 4 + i
                            S.add('pe', lambda e, pb=pb, i=i, b=b: e.transpose(PS[pb][0:8, i * 128:(i + 1) * 128], CL[:, b, :], IDF[:]),
                                  reads=["CL", "IDF"], writes=[P(pb)])
                        S.add('act', lambda e, pb=pb: e.activation(out=FA[0:8, 0:512], in_=PS[pb][0:8, 0:512], func=AF.Copy, scale=-1.0),
                              reads=[P(pb)], writes=["FA"])
                        carry = decay_chain(None, 512, carry, t4 % 2, pks[sq, :, :, t4 * 512:(t4 + 1) * 512], None)
                    S.add('dve', lambda e, sq=sq: e.tensor_copy(out=FA[0:8, 0:32], in_=LPS[0:8, sq * 32:(sq + 1) * 32]), reads=["LPS"], writes=["FA"])
                    decay_chain(None, 32, carry, 0, pks[sq, :, :, T:T + 32], pqs[sq])

            S.stage("F")
            if isP:
                S.barrier()
            cslot = {}

            def conv_views(buf):
                if isP:
                    return buf[:, 2:2 + TW], buf[:, 1:1 + TW], buf[:, 0:TW], None
                v = buf[:, 0:4 * 34].rearrange("p (s c) -> p s c", s=4)
                return v[:, :, 2:34], v[:, :, 1:33], v[:, :, 0:32], v[:, :, 0:2]

            def psv(pb):
                if isP:
                    return PS[pb][:, 0:TW]
                return PS[pb][:, 0:128].rearrange("p (s c) -> p s c", s=4)

            def sbv(ap):
                if isP:
                    return ap
                return ap.rearrange("p (s c) -> p s c", s=4)

            h_slot = None
            for c in range(4):
                s_bc = load_wa(l, 4 + c)
                if c % 2 == 0:
                    h_slot = load_wa(l, 8 + c // 2)
                for t in range(NT):
                    base = (3 * (c * NT + t)) % 6
                    pgb, pgc, ph = base, base + 1, base + 2
                    proj(pgb, s_bc, 0, 128, t, ("WA", s_bc))
                    proj(pgc, s_bc, 128, 128, t, ("WA", s_bc))
                    proj(ph, h_slot, (c % 2) * 128, 128, t, ("WA", h_slot))
                    ub = UT[t % 2]
                    cur, m1, m2, halo = conv_views(ub)
                    S.add('act', lambda e, ph=ph: e.activation(out=sbv(HS[:, 0:TW]), in_=psv(ph), func=AF.Copy),
                          reads=[P(ph)], writes=["HS"])
                    S.add('dve', lambda e, pgc=pgc, cur=cur: e.tensor_tensor(out=cur, in0=psv(pgc), in1=sbv(HS[:, 0:TW]), op=ALU.mult),
                          reads=[P(pgc), "HS"], writes=[("UT", t % 2)])
                    if isP:
                        if t == 0:
                            S.add('dve', lambda e, ub=ub: e.memset(ub[:, 0:2], 0.0), writes=[("UT", t % 2)])
                        else:
                            pu = UT[(t - 1) % 2]
                            S.add('dve', lambda e, ub=ub, pu=pu: e.tensor_copy(out=ub[:, 0:2], in_=pu[:, TW:TW + 2]),
                                  reads=[("UT", (t - 1) % 2)], writes=[("UT", t % 2)])
                    else:
                        o0 = ((l * 4 + c) * 4) * 2
                        S.add('dve', lambda e, halo=halo, o0=o0: e.tensor_copy(
                            out=halo, in_=SMIX[:, o0:o0 + 8].rearrange("p (s c) -> p s c", s=4)),
                            reads=["SMIX"], writes=[("UT", t % 2)])
                    wi = (l * 4 + c) * 3
                    S.add('act', lambda e, cur=cur, wi=wi, c=c: e.activation(
                        out=sbv(T1[:, 0:TW]), in_=cur, func=AF.Identity, bias=CB[:, l * 4 + c:l * 4 + c + 1], scale=CW[:, wi + 2:wi + 3]),
                        reads=[("UT", t % 2), "CW", "CB"], writes=["T1"])
                    S.add('dve', lambda e, m1=m1, wi=wi: e.scalar_tensor_tensor(
                        out=sbv(T1[:, 0:TW]), in0=m1, scalar=CW[:, wi + 1:wi + 2], in1=sbv(T1[:, 0:TW]), op0=ALU.mult, op1=ALU.add),
                        reads=[("UT", t % 2), "CW"], writes=["T1"])
                    S.add('dve', lambda e, m2=m2, wi=wi: e.scalar_tensor_tensor(
                        out=sbv(T1[:, 0:TW]), in0=m2, scalar=CW[:, wi:wi + 1], in1=sbv(T1[:, 0:TW]), op0=ALU.mult, op1=ALU.add),
                        reads=[("UT", t % 2), "CW"], writes=["T1"])
                    S.add('dve', lambda e, pgb=pgb, c=c, t=t: e.tensor_tensor(
                        out=sbv(CT[:, c, t * TW:(t + 1) * TW]), in0=psv(pgb), in1=sbv(T1[:, 0:TW]), op=ALU.mult),
                        reads=[P(pgb), "T1"], writes=[("CT", c, t)])
                    if isP and t == NT - 1:
                        S.add('sp', lambda e, ub=ub, c=c: e.dma_start(out=mcd[l, n, c], in_=ub[:, TW:TW + 2]),
                              reads=[("UT", t % 2)], writes=[], dma=True)
                    if not isP:
                        v = ub[:, 0:4 * 34].rearrange("p (s c) -> p s c", s=4)
                        for sq in range(4):
                            S.add('sp', lambda e, v=v, c=c, sq=sq: e.dma_start(out=mcd[l, 4 + sq, c], in_=v[:, sq, 32:34]),
                                  reads=[("UT", t % 2)], writes=[], dma=True)

            S.stage("C")
            conv_rows = None
            for pair in range(4):
                s_qk = load_wa(l, pair)
                s_v = load_wv(l, pair)
                wo_a = load_wr(wo[l, pair])
                if pair == 0:
                    conv_rows = [load_wr(wo[l, 4 + c]) for c in range(4)]
                S.stage("w1")
                for t in range(NT):
                    pb = (t % 2) * 1 + 6
                    proj(pb, s_qk, 128, 128, t, ("WA", s_qk))
                    S.stage("ka")
                    S.stage(f"ka{t}")
                    ks = KST[t % 2]
                    S.add('act', lambda e, pb=pb, ks=ks: e.activation(out=ks[:, 0:TW], in_=PS[pb][:, 0:TW], func=AF.Copy),
                          reads=[P(pb)], writes=[("KST", t % 2)])
                    S.stage("kb")
                    S.stage(f"kb{t}")
                    kdst = (kpd[l, n, pair * 128:(pair + 1) * 128, t * TW:(t + 1) * TW] if isP
                            else ksd[l, pair * 128:(pair + 1) * 128, :])
                    S.add('sp', lambda e, ks=ks, kdst=kdst: e.dma_start(out=kdst, in_=ks[:, 0:TW]),
                          reads=[("KST", t % 2)], writes=[], dma=True)
                    S.stage("kc")
                    S.stage(f"kc{t}")
                    if isP:
                        for hl in range(2):
                            if hl == 1:
                                S.stage("kd")
                                S.stage(f"kd{t}")
                            mkv = os.environ.get("MK_V", "")
                            if hl == 0:
                                tt_ = (3 - t) if mkv == "addr" else t
                                if True:
                                    S.add('dve', lambda e, ks=ks, tt_=tt_: e.tensor_copy(out=KT[0:64, 0, tt_ * TW:(tt_ + 1) * TW], in_=ks[0:64, 0:TW]),
                                          reads=[("KST", t % 2)], writes=[("KT", hl, t)])
                                else:
                                    S.add('dve', lambda e, pb=pb, tt_=tt_: e.tensor_copy(out=KT[0:64, 0, tt_ * TW:(tt_ + 1) * TW], in_=PS[pb][0:64, 0:TW]),
                                          reads=[P(pb)], writes=[("KT", hl, t)])
                            else:
                                S.add('act', lambda e, pb=pb, t=t: e.activation(out=KT[0:64, 1, t * TW:(t + 1) * TW], in_=PS[pb][64:128, 0:TW], func=AF.Copy),
                                      reads=[P(pb)], writes=[("KT", hl, t)])
                    S.stage("ke")
                    S.stage(f"ke{t}")
                S.stage("k1")
                if isP:
                    for b in range(NB):
                        pb = b % 6
                        for k in range(8):
                            S.add('pe', lambda e, pb=pb, k=k, b=b: e.matmul(
                                PS[pb][:, 0:128], XT[:, k, b * 128:(b + 1) * 128], WV[s_v][:, k, :], start=(k == 0), stop=(k == 7)),
                                reads=[("XT", b), ("WV", s_v)], writes=[P(pb)])
                        vs_ = VST[b % 2]
                        S.add('act', lambda e, pb=pb, vs_=vs_: e.activation(out=vs_[:], in_=PS[pb][:, 0:128], func=AF.Copy),
                              reads=[P(pb)], writes=[("VST", b % 2)])
                        S.add('sp', lambda e, vs_=vs_, b=b: e.dma_start(out=vpd[l, n, b * 128:(b + 1) * 128, pair * 128:(pair + 1) * 128], in_=vs_[:]),
                              reads=[("VST", b % 2)], writes=[], dma=True)
                        S.add('dve', lambda e, pb=pb, b=b: e.tensor_copy(
                            out=VA[:, b, :, 0:64], in_=PS[pb][:, 0:128].rearrange("p (h c) -> p h c", h=2)),
                            reads=[P(pb)], writes=[("VA", b)])
                    S.stage("v1")
                    for hl in range(2):
                        hh = pair * 2 + hl
                        S.add('sp', lambda e, hl=hl, hh=hh: e.dma_start(out=KT[67:70, hl, 0:TT], in_=pkd[hh, :, 0:TT]),
                              reads=["pkd"], writes=[("KTaug", hl)], dma=True)

                S.stage("kv")
                if isP:
                    def q_stage(t):
                        qt = QT[t % 2]
                        qres = ("QT", t % 2)
                        pb = 6 + (t % 2)
                        proj(pb, s_qk, 0, 128, t, ("WA", s_qk))
                        S.add('act', lambda e, pb=pb, qt=qt: e.activation(out=qt[0:64, 0, :], in_=PS[pb][0:64, 0:TW], func=AF.Copy),
                              reads=[P(pb)], writes=[qres])
                        S.add('act', lambda e, pb=pb, qt=qt: e.activation(out=qt[0:64, 1, :], in_=PS[pb][64:128, 0:TW], func=AF.Copy),
                              reads=[P(pb)], writes=[qres])
                        for hl in range(2):
                            hh = pair * 2 + hl
                            S.add('sp', lambda e, hl=hl, hh=hh, qt=qt, t=t: e.dma_start(out=qt[64:67, hl, :], in_=pqd[hh, :, t * TW:(t + 1) * TW]),
                                  reads=["pqd"], writes=[qres], dma=True)

                    def attn_stage(t):
                        qt = QT[t % 2]
                        qres = ("QT", t % 2)
                        at = AT[t % 2]
                        ares = ("AT", t % 2)
                        nkb = 4 * t + 4
                        ng = nkb // 2
                        glist = [(hl, g) for hl in range(2) for g in range(ng)]

                        def qk_exp(idx):
                            hl, g = glist[idx]
                            par = idx % 2
                            sb0 = par * 2
                            ptb = PT[par]
                            pres = ("PT", par)
                            kres = [("KT", hl, tt) for tt in range(t + 1)] + [("KTaug", hl), "KT"]
                            for i in range(2):
                                kb = g * 2 + i
                                j = kb - 4 * t
                                c0 = 0 if j < 0 else j * 128
                                S.add('pe', lambda e, sbk=sb0 + i, kb=kb, hl=hl, qt=qt, c0=c0, j=j: e.matmul(
                                    PS[sbk][:, c0:TW], KT[0:70, hl, kb * 128:(kb + 1) * 128], qt[0:70, hl, c0:TW],
                                    start=True, stop=(j < 0)),
                                    reads=kres + [qres], writes=[P(sb0 + i)])
                                if j >= 0:
                                    S.add('pe', lambda e, sbk=sb0 + i, c0=c0: e.matmul(
                                        PS[sbk][:, c0:c0 + 128], IDB[:], MASK[:], start=False, stop=True),
                                        reads=["IDB", "MASK"], writes=[P(sb0 + i)])
                                S.add('act', lambda e, sbk=sb0 + i, ptb=ptb, i=i, c0=c0: e.activation(
                                    out=ptb[:, i * 512 + c0:i * 512 + TW], in_=PS[sbk][:, c0:TW], func=AF.Exp, scale=0.125),
                                    reads=[P(sb0 + i)], writes=[pres])

                        def pv_norm(idx):
                            hl, g = glist[idx]
                            par = idx % 2
                            ptb = PT[par]
                            pres = ("PT", par)
                            ob = 4 + hl
                            for i in range(2):
                                kb = g * 2 + i
                                j = kb - 4 * t
                                c0 = 0 if j < 0 else j * 128
                                S.add('pe', lambda e, ob=ob, kb=kb, hl=hl, ptb=ptb, i=i, c0=c0, nkb=nkb: e.matmul(
                                    PS[ob][:, c0:TW], VA[:, kb, hl, :], ptb[:, i * 512 + c0:i * 512 + TW],
                                    start=(kb == 0), stop=(kb == nkb - 1)),
                                    reads=[pres, ("VA", kb), "VA"], writes=[P(ob)])
                            if g == ng - 1:
                                S.add('dve', lambda e, ob=ob: e.reciprocal(out=RC[64:128, 0:TW], in_=PS[ob][64:128, 0:TW]),
                                      reads=[P(ob)], writes=["RC"])
                                S.add('dve', lambda e, ob=ob, hl=hl, at=at: e.tensor_tensor(
                                    out=at[hl * 64:(hl + 1) * 64, 0:TW], in0=PS[ob][0:64, 0:TW], in1=RC[64:128, 0:TW], op=ALU.mult),
                                    reads=[P(ob), "RC"], writes=[ares])

                        qk_exp(0)
                        for idx in range(len(glist)):
                            if idx + 1 < len(glist):
                                qk_exp(idx + 1)
                            pv_norm(idx)

                    def wout_stage(t):
                        at = AT[t % 2]
                        ares = ("AT", t % 2)
                        S.stage("att")
                        for bb in range(TW // 128):
                            b = t * (TW // 128) + bb
                            terms = [(at[:, bb * 128:(bb + 1) * 128], WR[wo_a], [ares, ("WR", wo_a)])]
                            if pair == 0:
                                for c in range(4):
                                    terms.append((CT[:, c, b * 128:(b + 1) * 128], WR[conv_rows[c]], [("CT", c, t), ("WR", conv_rows[c])]))
                            bpair = [(6, 7), (4, 5)][bb % 2]
                            for hf in range(2):
                                pb2 = bpair[hf]
                                for ti, (lt, rw, rs) in enumerate(terms):
                                    S.add('pe', lambda e, pb2=pb2, lt=lt, rw=rw, hf=hf, ti=ti, nt_=len(terms): e.matmul(
                                        PS[pb2][:, 0:512], lt, rw[:, hf * 512:(hf + 1) * 512], start=(ti == 0), stop=(ti == nt_ - 1)),
                                        reads=rs, writes=[P(pb2)])
                            x_update(b, pair == 0, bpair)

                    q_stage(0)
                    for t in range(NT):
                        attn_stage(t)
                        if t + 1 < NT:
                            q_stage(t + 1)
                        wout_stage(t)
                else:
                    sample_attention(l, pair, s_qk, s_v, wo_a, conv_rows, KT, VA, QT, PT, AT, CT, RC, OS, FA, FBb, FC, PKt, PQt, KC, CL,
                                     decay_chain, proj, x_update)

            S.stage("pairs")
            load_ln(l, 0)
            layer_norm_all()
            S.stage("ln1")

            build_xt()
            S.barrier()
            arena.reset()
            A2 = arena.get(6 * TT, BF16).rearrange("p (c t) -> p c t", c=6)
            GT = [arena.get(TW + 8, F32) for _ in range(2)]
            T2 = [arena.get(512, F32) for _ in range(2)]
            T3 = [arena.get(512, F32) for _ in range(2)]
            for gi, (j0, j1) in enumerate(FFN_GROUPS):
                for j in range(j0, j1):
                    jl = j - j0
                    s_up = load_wa(l, 10 + j)
                    for t in range(NT):
                        base = 2 * ((j * NT + t) % 3)
                        pg, pv = base, base + 1
                        proj(pg, s_up, 0, 128, t, ("WA", s_up))
                        proj(pv, s_up, 128, 128, t, ("WA", s_up))
                        gb_ = GT[t % 2]
                        cur, m1, m2, halo = conv_views(gb_)
                        t2 = T2[t % 2]
                        t3 = T3[t % 2]
                        S.add('act', lambda e, pg=pg, cur=cur: e.activation(out=cur, in_=psv(pg), func=AF.Copy),
                              reads=[P(pg)], writes=[("GT", t % 2)])
                        if isP:
                            if t == 0:
                                S.add('dve', lambda e, gb_=gb_: e.memset(gb_[:, 0:2], 0.0), writes=[("GT", t % 2)])
                            else:
                                pu = GT[(t - 1) % 2]
                                S.add('dve', lambda e, gb_=gb_, pu=pu: e.tensor_copy(out=gb_[:, 0:2], in_=pu[:, TW:TW + 2]),
                                      reads=[("GT", (t - 1) % 2)], writes=[("GT", t % 2)])
                        else:
                            o0 = ((l * NCH + j) * 4) * 2
                            S.add('dve', lambda e, halo=halo, o0=o0: e.tensor_copy(
                                out=halo, in_=SFFN[:, o0:o0 + 8].rearrange("p (s c) -> p s c", s=4)),
                                reads=["SFFN"], writes=[("GT", t % 2)])
                        wi = (l * NCH + j) * 3
                        bi = l * NCH + j
                        S.add('act', lambda e, pg=pg, t2=t2, wi=wi, bi=bi: e.activation(
                            out=sbv(t2[:, 0:TW]), in_=psv(pg), func=AF.Identity, bias=FB[:, bi:bi + 1], scale=FW[:, wi + 2:wi + 3]),
                            reads=[P(pg), "FW", "FB"], writes=[("T2", t % 2)])
                        S.add('dve', lambda e, m1=m1, t2=t2, wi=wi: e.scalar_tensor_tensor(
                            out=sbv(t2[:, 0:TW]), in0=m1, scalar=FW[:, wi + 1:wi + 2], in1=sbv(t2[:, 0:TW]), op0=ALU.mult, op1=ALU.add),
                            reads=[("GT", t % 2), "FW"], writes=[("T2", t % 2)])
                        S.add('dve', lambda e, m2=m2, t2=t2, wi=wi: e.scalar_tensor_tensor(
                            out=sbv(t2[:, 0:TW]), in0=m2, scalar=FW[:, wi:wi + 1], in1=sbv(t2[:, 0:TW]), op0=ALU.mult, op1=ALU.add),
                            reads=[("GT", t % 2), "FW"], writes=[("T2", t % 2)])
                        S.add('act', lambda e, t2=t2, t3=t3: e.activation(out=t3[:, 0:TW], in_=t2[:, 0:TW], func=AF.Silu),
                              reads=[("T2", t % 2)], writes=[("T3", t % 2)])
                        S.add('dve', lambda e, pv=pv, t3=t3, jl=jl, t=t: e.tensor_tensor(
                            out=A2[:, jl, t * TW:(t + 1) * TW], in0=PS[pv][:, 0:TW], in1=t3[:, 0:TW], op=ALU.mult),
                            reads=[P(pv), ("T3", t % 2)], writes=[("A2", jl, t)])
                        if isP and t == NT - 1:
                            S.add('sp', lambda e, gb_=gb_, j=j: e.dma_start(out=fcd[l, n, j], in_=gb_[:, TW:TW + 2]),
                                  reads=[("GT", t % 2)], writes=[], dma=True)
                        if not isP:
                            v = gb_[:, 0:4 * 34].rearrange("p (s c) -> p s c", s=4)
                            for sq in range(4):
                                S.add('sp', lambda e, v=v, j=j, sq=sq: e.dma_start(out=fcd[l, 4 + sq, j], in_=v[:, sq, 32:34]),
                                      reads=[("GT", t % 2)], writes=[], dma=True)
                rows = [load_wr(wdn[l, j]) for j in range(j0, j1)]
                for b in range(NB):
                    t = b * 128 // TW
                    bpair = [(6, 7), (4, 5), (2, 3), (0, 1)][b % 4]
                    for hf in range(2):
                        pb2 = bpair[hf]
                        for ti, j in enumerate(range(j0, j1)):
                            jl = j - j0
                            S.add('pe', lambda e, pb2=pb2, jl=jl, b=b, r=rows[ti], hf=hf, ti=ti, n_=j1 - j0: e.matmul(
                                PS[pb2][:, 0:512], A2[:, jl, b * 128:(b + 1) * 128], WR[r][:, hf * 512:(hf + 1) * 512],
                                start=(ti == 0), stop=(ti == n_ - 1)),
                                reads=[("A2", jl, t), ("WR", rows[ti])], writes=[P(pb2)])
                    x_update(b, gi == 0, bpair)
            load_ln(l, 1)
            layer_norm_all()

        for b in range(NB):
            dst = yp[n, b * 128:(b + 1) * 128, :] if isP else ys
            S.add('sp', lambda e, b=b, d=dst: e.dma_start(out=d, in_=X[:, b, :]), reads=[("X", b)], writes=[], dma=True)

    def sample_attention(l, pair, s_qk, s_v, wo_a, conv_rows, KT, VA, QT, PT, AT, CT, RC, OS, FA, FBb, FC, PKt, PQt, KC, CL,
                         decay_chain, proj, x_update):
        TW = 128
        qt = QT[0]
        proj(6, s_qk, 0, 128, 0, ("WA", s_qk))
        S.add('act', lambda e: e.activation(out=qt[0:64, 0, :], in_=PS[6][0:64, 0:128], func=AF.Copy), reads=[P(6)], writes=["QTs"])
        S.add('act', lambda e: e.activation(out=qt[0:64, 1, :], in_=PS[6][64:128, 0:128], func=AF.Copy), reads=[P(6)], writes=["QTs"])
        at = AT[0]
        for sq in range(4):
            S.add('pool', lambda e, sq=sq: e.dma_start(
                out=KC[:], in_=ck[l, sq, :, pair * 128:(pair + 1) * 128].rearrange("(b p) c -> p b c", p=128)),
                writes=["KC"], dma=True, force_barrier=True)
            for hl in range(2):
                S.add('pool', lambda e, sq=sq, hl=hl: e.dma_start(
                    out=VA[:, 0:16, hl, 0:64],
                    in_=cv[l, sq, :, pair * 128 + hl * 64:pair * 128 + (hl + 1) * 64].rearrange("(b p) c -> p b c", p=128)),
                    writes=[("VAs", hl)], dma=True, force_barrier=True)
            for g in range(4):
                pb = g % 2
                for i in range(4):
                    b = g * 4 + i
                    S.add('pe', lambda e, pb=pb, i=i, b=b: e.matmul(
                        PS[pb][:, i * 128:(i + 1) * 128], KC[:, b, :], IDB[:], start=True, stop=True),
                        reads=["KC", "IDB"], writes=[P(pb)])
                S.add('act', lambda e, pb=pb, g=g: e.activation(out=KT[0:64, 0, g * 512:(g + 1) * 512], in_=PS[pb][0:64, 0:512], func=AF.Copy),
                      reads=[P(pb)], writes=[("KTs", 0)])
                S.add('act', lambda e, pb=pb, g=g: e.activation(out=KT[0:64, 1, g * 512:(g + 1) * 512], in_=PS[pb][64:128, 0:512], func=AF.Copy),
                      reads=[P(pb)], writes=[("KTs", 1)])
            proj(7, s_qk, 128, 128, 0, ("WA", s_qk))
            S.add('act', lambda e, sq=sq: e.activation(out=KT[0:64, 0, T:T + 32], in_=PS[7][0:64, sq * 32:(sq + 1) * 32], func=AF.Copy),
                  reads=[P(7)], writes=[("KTs", 0)])
            S.add('act', lambda e, sq=sq: e.activation(out=KT[0:64, 1, T:T + 32], in_=PS[7][64:128, sq * 32:(sq + 1) * 32], func=AF.Copy),
                  reads=[P(7)], writes=[("KTs", 1)])
            for k in range(8):
                S.add('pe', lambda e, k=k, sq=sq: e.matmul(
                    PS[5][0:32, 0:128], XT[:, k, sq * 32:(sq + 1) * 32], WV[s_v][:, k, :], start=(k == 0), stop=(k == 7)),
                    reads=[("XT", 0), ("WV", s_v)], writes=[P(5)])
            vs_ = VST[sq % 2]
            S.add('act', lambda e, vs_=vs_: e.activation(out=vs_[0:32, :], in_=PS[5][0:32, 0:128], func=AF.Copy),
                  reads=[P(5)], writes=[("VST", sq % 2)])
            S.add('sp', lambda e, vs_=vs_, sq=sq: e.dma_start(out=vsd[l, sq, :, pair * 128:(pair + 1) * 128], in_=vs_[0:32, :]),
                  reads=[("VST", sq % 2)], writes=[], dma=True)
            S.add('act', lambda e: e.activation(out=VA[0:32, 16, :, 0:64], in_=PS[5][0:32, 0:128].rearrange("p (h c) -> p h c", h=2), func=AF.Copy),
                  reads=[P(5)], writes=[("VAs", 2)])
            for hl in range(2):
                hh = pair * 2 + hl
                S.add('sp', lambda e, hl=hl, hh=hh, sq=sq: e.dma_start(out=KT[67:70, hl, 0:T + 32], in_=pks[sq, hh, :, :]),
                      reads=["pkd"], writes=[("KTs", hl)], dma=True)
                S.add('sp', lambda e, hl=hl, hh=hh, sq=sq: e.dma_start(out=qt[64:67, hl, sq * 32:(sq + 1) * 32], in_=pqs[sq, hh, :, :]),
                      reads=["pqd"], writes=["QTs"], dma=True)
            for hl in range(2):
                ob = 4
                q_ap = qt[0:70, hl, sq * 32:(sq + 1) * 32]
                for b in range(16):
                    S.add('pe', lambda e, b=b, hl=hl, q_ap=q_ap: e.matmul(
                        PS[2][:, b * 32:(b + 1) * 32], KT[0:70, hl, b * 128:(b + 1) * 128], q_ap, start=True, stop=True),
                        reads=[("KTs", hl), "QTs", "KT"], writes=[P(2)])
                S.add('pe', lambda e, hl=hl, q_ap=q_ap: e.matmul(
                    PS[3][0:32, 0:32], KT[0:70, hl, T:T + 32], q_ap, start=True, stop=False),
                    reads=[("KTs", hl), "QTs", "KT"], writes=[P(3)])
                S.add('pe', lambda e: e.matmul(PS[3][0:32, 0:32], IDB[0:32, 0:32], MASK[0:32, 0:32], start=False, stop=True),
                      reads=["IDB", "MASK"], writes=[P(3)])
                S.add('act', lambda e: e.activation(out=PT[0][:, 0:512], in_=PS[2][:, 0:512], func=AF.Exp, scale=0.125),
                      reads=[P(2)], writes=[("PT", 0)])
                S.add('act', lambda e: e.activation(out=PT[1][0:32, 0:32], in_=PS[3][0:32, 0:32], func=AF.Exp, scale=0.125),
                      reads=[P(3)], writes=[("PT", 1)])
                for b in range(16):
                    S.add('pe', lambda e, b=b, hl=hl: e.matmul(
                        PS[ob][:, 0:32], VA[:, b, hl, :], PT[0][:, b * 32:(b + 1) * 32], start=(b == 0), stop=False),
                        reads=[("PT", 0), ("VAs", hl), "VA"], writes=[P(ob)])
                S.add('pe', lambda e, hl=hl: e.matmul(
                    PS[ob][:, 0:32], VA[0:32, 16, hl, :], PT[1][0:32, 0:32], start=False, stop=True),
                    reads=[("PT", 1), ("VAs", 2), "VA"], writes=[P(ob)])
                S.add('dve', lambda e: e.reciprocal(out=RC[64:128, 0:32], in_=PS[ob][64:128, 0:32]), reads=[P(ob)], writes=["RC"])
                S.add('dve', lambda e, hl=hl, sq=sq: e.tensor_tensor(
                    out=at[hl * 64:(hl + 1) * 64, sq * 32:(sq + 1) * 32], in0=PS[ob][0:64, 0:32], in1=RC[64:128, 0:32], op=ALU.mult),
                    reads=[P(ob), "RC"], writes=[("AT", 0)])
        terms = [(at[:, 0:128], WR[wo_a], [("AT", 0), ("WR", wo_a)])]
        if pair == 0:
            for c in range(4):
                terms.append((CT[:, c, 0:128], WR[conv_rows[c]], [("CT", c, 0), ("WR", conv_rows[c])]))
        for hf in range(2):
            pb2 = 6 + hf
            for ti, (lt, rw, rs) in enumerate(terms):
                S.add('pe', lambda e, pb2=pb2, lt=lt, rw=rw, hf=hf, ti=ti, nt_=len(terms): e.matmul(
                    PS[pb2][:, 0:512], lt, rw[:, hf * 512:(hf + 1) * 512], start=(ti == 0), stop=(ti == nt_ - 1)),
                    reads=rs, writes=[P(pb2)])
        x_update(0, pair == 0, (6, 7))

    for n in range(nps):
        S.phase = n
        run_sequence('p', n)
    if do_sample:
        S.phase = 4
        run_sequence('s', 0)

    sem_keys = S.finalize()
    with ExitStack() as es:
        sems = {k: es.enter_context(nc.semaphore("s_" + "_".join(str(x) for x in k))) for k in sem_keys}
        block = es.enter_context(nc.Block())

        @block.tensor
        def _(e):
            S.emit(e, 'pe', sems)

        @block.scalar
        def _(e):
            S.emit(e, 'act', sems)

        @block.vector
        def _(e):
            S.emit(e, 'dve', sems)

        @block.gpsimd
        def _(e):
            S.emit(e, 'pool', sems)

        @block.sync
        def _(e):
            S.emit(e, 'sp', sems)
    return nc


def _prep_weights(w_in, b_f, conv_w, conv_b, w_out, ln1_g, ln1_b, w_up, ffn_conv_w, ffn_conv_b, w_down, ln2_g, ln2_b):
    f = np.float32
    w_in = np.asarray(w_in, f)
    w_up = np.asarray(w_up, f)

    def unit(mat):
        C = mat.shape[1]
        return mat.reshape(8, 128, C).transpose(1, 0, 2)

    wa = np.empty((L, 32, 128, 8, 256), f)
    wv = np.empty((L, 4, 128, 8, 128), f)
    wf = np.empty((L, 128, 8, 8), f)
    for l in range(L):
        W = w_in[l]
        q, k, v = W[:, 0:512], W[:, 512:1024], W[:, 1024:1536]
        fl = W[:, 1536:1544]
        gb, gc, hh = W[:, 1544:2056], W[:, 2056:2568], W[:, 2568:3080]
        for p in range(4):
            wa[l, p, :, :, 0:128] = unit(q[:, p * 128:(p + 1) * 128])
            wa[l, p, :, :, 128:256] = unit(k[:, p * 128:(p + 1) * 128])
            wv[l, p] = unit(v[:, p * 128:(p + 1) * 128])
        for c in range(4):
            wa[l, 4 + c, :, :, 0:128] = unit(gb[:, c * 128:(c + 1) * 128])
            wa[l, 4 + c, :, :, 128:256] = unit(gc[:, c * 128:(c + 1) * 128])
        for c2 in range(2):
            wa[l, 8 + c2, :, :, 0:128] = unit(hh[:, (2 * c2) * 128:(2 * c2 + 1) * 128])
            wa[l, 8 + c2, :, :, 128:256] = unit(hh[:, (2 * c2 + 1) * 128:(2 * c2 + 2) * 128])
        for j in range(NCH):
            wa[l, 10 + j, :, :, 0:128] = unit(w_up[l][:, j * 128:(j + 1) * 128])
            wa[l, 10 + j, :, :, 128:256] = unit(w_up[l][:, DFF + j * 128:DFF + (j + 1) * 128])
        wf[l] = unit(fl)
    wo = np.ascontiguousarray(np.asarray(w_out, f).reshape(L, 8, 128, D))
    wdn = np.ascontiguousarray(np.asarray(w_down, f).reshape(L, NCH, 128, D))
    lnrep = np.empty((L, 2, 128, 2 * D), f)
    lnrep[:, 0, :, 0:D] = np.asarray(ln1_g, f)[:, None, :]
    lnrep[:, 0, :, D:] = np.asarray(ln1_b, f)[:, None, :]
    lnrep[:, 1, :, 0:D] = np.asarray(ln2_g, f)[:, None, :]
    lnrep[:, 1, :, D:] = np.asarray(ln2_b, f)[:, None, :]
    cwd = np.ascontiguousarray(np.asarray(conv_w, f).reshape(L, 3, 4, 128).transpose(3, 0, 2, 1)).reshape(128, L * 4 * 3)
    cbd = np.ascontiguousarray(np.asarray(conv_b, f).reshape(L, 4, 128).transpose(2, 0, 1)).reshape(128, L * 4)
    fwd = np.ascontiguousarray(np.asarray(ffn_conv_w, f).reshape(L, 3, NCH, 128).transpose(3, 0, 2, 1)).reshape(128, L * NCH * 3)
    fbd = np.ascontiguousarray(np.asarray(ffn_conv_b, f).reshape(L, NCH, 128).transpose(2, 0, 1)).reshape(128, L * NCH)
    bfd = np.ascontiguousarray(np.asarray(b_f, f).T)
    idfd = np.eye(128, dtype=f)
    idbd = np.eye(128).astype(ml_dtypes.bfloat16)
    kk = np.arange(128)[:, None]
    qq = np.arange(128)[None, :]
    maskd = np.where(kk <= qq, 0.0, NEG).astype(ml_dtypes.bfloat16)
    return dict(wa=wa.reshape(L, 32, 128, 8 * 256), wv=wv.reshape(L, 4, 128, 8 * 128), wf=wf.reshape(L, 128, 64), wo=wo, wdn=wdn,
                lnrep=lnrep, cwd=cwd, cbd=cbd, fwd=fwd, fbd=fbd, bfd=bfd, idfd=idfd, idbd=idbd, maskd=maskd)


_NC_CACHE = {}


def kernel(x_prompt, x_sample, cache_k, cache_v, cache_logf, state_mix_conv, state_ffn_conv,
           w_in, b_f, conv_w, conv_b, w_out, ln1_g, ln1_b, w_up, ffn_conv_w, ffn_conv_b, w_down, ln2_g, ln2_b):
    f = np.float32
    nps = int(os.environ.get("MK_NPS", NPS))
    nl = int(os.environ.get("MK_NL", L))
    do_s = int(os.environ.get("MK_SAMPLE", 1)) == 1
    ncores = 8
    wd = _prep_weights(w_in, b_f, conv_w, conv_b, w_out, ln1_g, ln1_b, w_up, ffn_conv_w, ffn_conv_b, w_down, ln2_g, ln2_b)
    x_prompt = np.asarray(x_prompt, f)
    x_sample = np.asarray(x_sample, f)
    cache_k = np.asarray(cache_k, f)
    cache_v = np.asarray(cache_v, f)
    cache_logf = np.asarray(cache_logf, f)
    smc = np.asarray(state_mix_conv, f)
    sfc = np.asarray(state_ffn_conv, f)
    in_maps = []
    for c in range(ncores):
        sl = slice(c * 4, (c + 1) * 4)
        m = dict(wd)
        m["xp"] = np.ascontiguousarray(x_prompt[sl])
        m["xs"] = np.ascontiguousarray(x_sample[sl].reshape(128, D))
        m["ck"] = np.ascontiguousarray(cache_k[:, sl].reshape(L, 4, T, 512))
        m["cv"] = np.ascontiguousarray(cache_v[:, sl].reshape(L, 4, T, 512))
        m["clf"] = np.ascontiguousarray(cache_logf[:, sl])
        m["smix"] = np.ascontiguousarray(smc[:, sl].reshape(L, 4, 2, 4, 128).transpose(4, 0, 3, 1, 2)).reshape(128, -1)
        m["sffn"] = np.ascontiguousarray(sfc[:, sl].reshape(L, 4, 2, NCH, 128).transpose(4, 0, 3, 1, 2)).reshape(128, -1)
        in_maps.append(m)
    key = (nps, nl, do_s)
    if key not in _NC_CACHE:
        _NC_CACHE[key] = build_program(nps, nl, do_s)
    nc = _NC_CACHE[key]
    res = run_bass_kernel_spmd(nc, in_maps, core_ids=list(range(ncores)))
    R = res.results
    B = 32
    y_prompt = np.empty((B, T, D), f)
    y_sample = np.empty((B, 32, D), f)
    k_prompt = np.empty((L, B, T, H, 64), f)
    v_prompt = np.empty((L, B, T, H, 64), f)
    logf_prompt = np.empty((L, B, T, H), f)
    mix_p = np.empty((L, B, 2, 512), f)
    ffn_p = np.empty((L, B, 2, DFF), f)
    k_sample = np.empty((L, B, 32, H, 64), f)
    v_sample = np.empty((L, B, 32, H, 64), f)
    logf_sample = np.empty((L, B, 32, H), f)
    mix_s = np.empty((L, B, 2, 512), f)
    ffn_s = np.empty((L, B, 2, DFF), f)
    for c in range(ncores):
        r = R[c]
        sl = slice(c * 4, (c + 1) * 4)
        y_prompt[sl] = r["yp"]
        y_sample[sl] = r["ys"].reshape(4, 32, D)
        k_prompt[:, sl] = r["kpd"].reshape(L, 4, H, 64, T).transpose(0, 1, 4, 2, 3)
        v_prompt[:, sl] = r["vpd"].reshape(L, 4, T, H, 64)
        logf_prompt[:, sl] = r["lfpd"].transpose(0, 1, 3, 2)
        mc = r["mcd"]
        fc = r["fcd"]
        mix_p[:, sl] = mc[:, 0:4].transpose(0, 1, 4, 2, 3).reshape(L, 4, 2, 512)
        mix_s[:, sl] = mc[:, 4:8].transpose(0, 1, 4, 2, 3).reshape(L, 4, 2, 512)
        ffn_p[:, sl] = fc[:, 0:4].transpose(0, 1, 4, 2, 3).reshape(L, 4, 2, DFF)
        ffn_s[:, sl] = fc[:, 4:8].transpose(0, 1, 4, 2, 3).reshape(L, 4, 2, DFF)
        k_sample[:, sl] = r["ksd"].reshape(L, H, 64, 4, 32).transpose(0, 3, 4, 1, 2)
        v_sample[:, sl] = r["vsd"].reshape(L, 4, 32, H, 64)
        logf_sample[:, sl] = r["lfsd"].reshape(L, H, 4, 32).transpose(0, 2, 3, 1)
    return (y_prompt, y_sample, k_prompt, v_prompt, logf_prompt, mix_p, ffn_p,
            k_sample, v_sample, logf_sample, mix_s, ffn_s)
```
